# Optimizing a Trainium2 kernel written in Bass

```python
import math
import jax, jax.numpy as jnp
from jax import lax
import numpy as np

D_MODEL = 1024
BATCH = 8
SEQ = 2048
DEPTH = 1
DEC_BATCH = 128
DEC_SEQ = 1
PAST_LEN = 16384
PAGE_SIZE = 128

EPS = 1e-6
S5_WIDTH = D_MODEL
S5_GROUP = 16
S5_GROUPS = S5_WIDTH // S5_GROUP
S5_STATE = 64
S5_DT_MIN = 1e-3
S5_DT_MAX = 1e-1
M_EXPAND = 2
M_INNER = M_EXPAND * D_MODEL
M_HEADDIM = 64
M_HEADS = M_INNER // M_HEADDIM
M_GROUPS = 4
M_HPG = M_HEADS // M_GROUPS
M_STATE = 128
M_CONV = 4
M_CONV_DIM = M_INNER + 2 * M_GROUPS * M_STATE
M_CHUNK = 128
M_DT_MIN = 1e-3
M_DT_MAX = 1e-1
D_FF = -(-8 * D_MODEL // (3 * 256)) * 256
OFF_Z = S5_WIDTH
OFF_XBC = OFF_Z + M_INNER
OFF_DT = OFF_XBC + M_CONV_DIM
OFF_GA = OFF_DT + M_HEADS
OFF_GB = OFF_GA + D_MODEL
IN_COLS = OFF_GB + D_MODEL

kernel_name = "hybrid_s5_ssd_gated_decoder_step"


def _rmsnorm(x, g):
    x32 = x.astype(jnp.float32)
    r = x32 * lax.rsqrt(jnp.mean(x32 * x32, axis=-1, keepdims=True) + EPS)
    return (r * g.astype(jnp.float32)).astype(x.dtype)


def _modulate(h, shift, scale):
    return h * (1.0 + scale[:, None, :]) + shift[:, None, :]


def _cplx_affine(e1, e2):
    a1r, a1i, b1r, b1i = e1
    a2r, a2i, b2r, b2i = e2
    return (a2r * a1r - a2i * a1i,
            a2r * a1i + a2i * a1r,
            a2r * b1r - a2i * b1i + b2r,
            a2r * b1i + a2i * b1r + b2i)


def _s5_mixer(u, h0_re, h0_im, lam_re, lam_im, log_dt, b_re, b_im, c_re, c_im, d_skip, w_glu, b_glu):
    bsz, L, _ = u.shape
    u32 = u.astype(jnp.float32)
    lr = lam_re.astype(jnp.float32)
    li = lam_im.astype(jnp.float32)
    dt = jnp.exp(log_dt.astype(jnp.float32))[:, None]
    mag = jnp.exp(lr * dt)
    abar_re = mag * jnp.cos(li * dt)
    abar_im = mag * jnp.sin(li * dt)
    den = lr * lr + li * li
    nr = abar_re - 1.0
    ni = abar_im
    f_re = (nr * lr + ni * li) / den
    f_im = (ni * lr - nr * li) / den
    br = b_re.astype(jnp.float32)
    bi = b_im.astype(jnp.float32)
    bbar_re = f_re[..., None] * br - f_im[..., None] * bi
    bbar_im = f_re[..., None] * bi + f_im[..., None] * br
    ug = u32.reshape(bsz, L, S5_GROUPS, S5_GROUP)
    bu_re = jnp.einsum("blgh,gnh->blgn", ug, bbar_re)
    bu_im = jnp.einsum("blgh,gnh->blgn", ug, bbar_im)
    shp = (bsz, L, S5_GROUPS, S5_STATE)
    a_re = jnp.broadcast_to(abar_re, shp)
    a_im = jnp.broadcast_to(abar_im, shp)
    acr, aci, bcr, bci = lax.associative_scan(_cplx_affine, (a_re, a_im, bu_re, bu_im), axis=1)
    h0r = h0_re.astype(jnp.float32)[:, None]
    h0i = h0_im.astype(jnp.float32)[:, None]
    h_re = bcr + acr * h0r - aci * h0i
    h_im = bci + acr * h0i + aci * h0r
    y = (jnp.einsum("ghn,blgn->blgh", c_re.astype(jnp.float32), h_re)
         - jnp.einsum("ghn,blgn->blgh", c_im.astype(jnp.float32), h_im)).reshape(bsz, L, S5_WIDTH)
    y = (y + d_skip.astype(jnp.float32) * u32).astype(u.dtype)
    ya = jax.nn.gelu(y)
    out = ya * jax.nn.sigmoid(ya @ w_glu + b_glu)
    return out, h_re[:, -1], h_im[:, -1]


def _ssd(xs, dt, a_head, bm, cm, h0):
    bsz, L = xs.shape[0], xs.shape[1]
    q = M_CHUNK if L % M_CHUNK == 0 else L
    nc = L // q
    xdt = (xs * dt[..., None]).reshape(bsz, nc, q, M_GROUPS, M_HPG, M_HEADDIM)
    a = jnp.moveaxis((dt * a_head).reshape(bsz, nc, q, M_GROUPS, M_HPG), 2, -1)
    bc = bm.reshape(bsz, nc, q, M_GROUPS, M_STATE)
    cc = cm.reshape(bsz, nc, q, M_GROUPS, M_STATE)
    a_cs = jnp.cumsum(a, axis=-1)
    causal = jnp.tril(jnp.ones((q, q), dtype=bool))
    decay_in = jnp.exp(jnp.where(causal, a_cs[..., :, None] - a_cs[..., None, :], -jnp.inf))
    cb = jnp.einsum("bclgn,bcsgn->bcgls", cc, bc)
    y_diag = jnp.einsum("bcgls,bcgrls,bcsgrp->bclgrp", cb, decay_in, xdt)
    decay_to_end = jnp.exp(a_cs[..., -1:] - a_cs)
    chunk_states = jnp.einsum("bclgn,bcgrl,bclgrp->bcgrpn", bc, decay_to_end, xdt)
    chunk_decay = jnp.exp(a_cs[..., -1])

    def step(h, inp):
        dec, st = inp
        return dec[..., None, None] * h + st, h

    h_final, h_enter = lax.scan(step, h0, (jnp.moveaxis(chunk_decay, 1, 0), jnp.moveaxis(chunk_states, 1, 0)))
    h_enter = jnp.moveaxis(h_enter, 0, 1)
    y_off = jnp.einsum("bclgn,bcgrpn,bcgrl->bclgrp", cc, h_enter, jnp.exp(a_cs))
    y = (y_diag + y_off).reshape(bsz, L, M_GROUPS, M_HPG, M_HEADDIM)
    return y, h_final


def _ssd_mixer(z, xbc, dt_raw, conv_buf, h0, conv_w, conv_b, dt_bias, a_log, d_skip, norm_w):
    bsz, L, _ = xbc.shape
    xbc_full = jnp.concatenate([conv_buf.astype(xbc.dtype), xbc], axis=1)
    conv = conv_b + sum(xbc_full[:, k:k + L] * conv_w[k] for k in range(M_CONV))
    new_buf = xbc_full[:, -(M_CONV - 1):]
    act = jax.nn.silu(conv).astype(jnp.float32)
    xs = act[..., :M_INNER].reshape(bsz, L, M_GROUPS, M_HPG, M_HEADDIM)
    bm = act[..., M_INNER:M_INNER + M_GROUPS * M_STATE].reshape(bsz, L, M_GROUPS, M_STATE)
    cm = act[..., M_INNER + M_GROUPS * M_STATE:].reshape(bsz, L, M_GROUPS, M_STATE)
    dt = jax.nn.softplus(dt_raw.astype(jnp.float32) + dt_bias.astype(jnp.float32)).reshape(bsz, L, M_GROUPS, M_HPG)
    a_head = -jnp.exp(a_log.astype(jnp.float32)).reshape(M_GROUPS, M_HPG)
    h0g = h0.astype(jnp.float32).reshape(bsz, M_GROUPS, M_HPG, M_HEADDIM, M_STATE)
    y, h_final = _ssd(xs, dt, a_head, bm, cm, h0g)
    y = y + d_skip.astype(jnp.float32).reshape(M_GROUPS, M_HPG)[:, :, None] * xs
    y = y.reshape(bsz, L, M_INNER).astype(z.dtype)
    y = _rmsnorm(y * jax.nn.silu(z), norm_w)
    return y, h_final.reshape(bsz, M_HEADS, M_HEADDIM, M_STATE), new_buf


def _layer(x, c, h5_re, h5_im, h_ssm, conv_buf, p):
    mod = jax.nn.silu(c) @ p["w_ada"] + p["b_ada"]
    sh_m, sc_m, gt_m, sh_f, sc_f, gt_f = jnp.split(mod, 6, axis=-1)
    h = _modulate(_rmsnorm(x, p["norm_pre_mix"]), sh_m, sc_m)
    proj = h @ p["w_in"]
    u = proj[..., :OFF_Z]
    z = proj[..., OFF_Z:OFF_XBC]
    xbc = proj[..., OFF_XBC:OFF_DT]
    dt_raw = proj[..., OFF_DT:OFF_GA]
    g_a = jax.nn.sigmoid(proj[..., OFF_GA:OFF_GB])
    g_b = jax.nn.sigmoid(proj[..., OFF_GB:])
    ya, h5_re_new, h5_im_new = _s5_mixer(u, h5_re, h5_im, p["s5_lam_re"], p["s5_lam_im"], p["s5_log_dt"],
                                         p["s5_b_re"], p["s5_b_im"], p["s5_c_re"], p["s5_c_im"],
                                         p["s5_d"], p["s5_w_glu"], p["s5_b_glu"])
    yb, h_ssm_new, conv_new = _ssd_mixer(z, xbc, dt_raw, conv_buf, h_ssm, p["m_conv_w"], p["m_conv_b"],
                                         p["m_dt_bias"], p["m_a_log"], p["m_d"], p["m_norm"])
    merged = g_a * (ya @ p["w_branch_s5"]) + g_b * (yb @ p["w_branch_ssd"])
    mix = merged @ p["w_out"]
    x = x + gt_m[:, None, :] * _rmsnorm(mix, p["norm_post_mix"])
    h = _modulate(_rmsnorm(x, p["norm_pre_ffn"]), sh_f, sc_f)
    gu = h @ p["w_ffn_in"]
    f = (jax.nn.silu(gu[..., :D_FF]) * gu[..., D_FF:]) @ p["w_ffn_out"]
    x = x + gt_f[:, None, :] * _rmsnorm(f, p["norm_post_ffn"])
    return x, h5_re_new.astype(x.dtype), h5_im_new.astype(x.dtype), h_ssm_new.astype(x.dtype), conv_new.astype(x.dtype)


def setup_inputs(seed: int = 0) -> dict:
    key = jax.random.key(seed)
    ks = jax.random.split(key, 40)
    f32 = jnp.float32
    nrm = lambda k, shape, s: jax.random.normal(k, shape, f32) * s
    D = DEPTH
    n_idx = jnp.arange(S5_STATE, dtype=f32)
    dt_m = jnp.exp(jax.random.uniform(ks[30], (D, M_HEADS), f32, math.log(M_DT_MIN), math.log(M_DT_MAX)))
    return {
        "x_prompt": nrm(ks[0], (BATCH, SEQ, D_MODEL), 1.0),
        "x_sample": nrm(ks[1], (DEC_BATCH, DEC_SEQ, D_MODEL), 1.0),
        "state_s5_re": nrm(ks[2], (D, DEC_BATCH, S5_GROUPS, S5_STATE), 0.3),
        "state_s5_im": nrm(ks[3], (D, DEC_BATCH, S5_GROUPS, S5_STATE), 0.3),
        "state_ssm": nrm(ks[4], (D, DEC_BATCH, M_HEADS, M_HEADDIM, M_STATE), 0.1),
        "state_conv": nrm(ks[5], (D, DEC_BATCH, M_CONV - 1, M_CONV_DIM), 1.0),
        "c_prompt": nrm(ks[6], (BATCH, D_MODEL), 1.0),
        "c_sample": nrm(ks[7], (DEC_BATCH, D_MODEL), 1.0),
        "w_ada": nrm(ks[8], (D, D_MODEL, 6 * D_MODEL), 0.5 * D_MODEL ** -0.5),
        "b_ada": nrm(ks[9], (D, 6 * D_MODEL), 0.01),
        "norm_pre_mix": 1.0 + nrm(ks[10], (D, D_MODEL), 0.01),
        "norm_post_mix": 1.0 + nrm(ks[11], (D, D_MODEL), 0.01),
        "norm_pre_ffn": 1.0 + nrm(ks[12], (D, D_MODEL), 0.01),
        "norm_post_ffn": 1.0 + nrm(ks[13], (D, D_MODEL), 0.01),
        "w_in": nrm(ks[14], (D, D_MODEL, IN_COLS), D_MODEL ** -0.5),
        "s5_lam_re": -0.5 + nrm(ks[15], (D, S5_GROUPS, S5_STATE), 0.01),
        "s5_lam_im": math.pi * n_idx + nrm(ks[16], (D, S5_GROUPS, S5_STATE), 0.01),
        "s5_log_dt": jax.random.uniform(ks[17], (D, S5_GROUPS), f32, math.log(S5_DT_MIN), math.log(S5_DT_MAX)),
        "s5_b_re": nrm(ks[18], (D, S5_GROUPS, S5_STATE, S5_GROUP), (2 * S5_GROUP) ** -0.5),
        "s5_b_im": nrm(ks[19], (D, S5_GROUPS, S5_STATE, S5_GROUP), (2 * S5_GROUP) ** -0.5),
        "s5_c_re": nrm(ks[20], (D, S5_GROUPS, S5_GROUP, S5_STATE), (2 * S5_STATE) ** -0.5),
        "s5_c_im": nrm(ks[21], (D, S5_GROUPS, S5_GROUP, S5_STATE), (2 * S5_STATE) ** -0.5),
        "s5_d": nrm(ks[22], (D, S5_WIDTH), 1.0),
        "s5_w_glu": nrm(ks[23], (D, S5_WIDTH, S5_WIDTH), S5_WIDTH ** -0.5),
        "s5_b_glu": nrm(ks[24], (D, S5_WIDTH), 0.01),
        "m_conv_w": nrm(ks[25], (D, M_CONV, M_CONV_DIM), M_CONV ** -0.5),
        "m_conv_b": nrm(ks[26], (D, M_CONV_DIM), 0.01),
        "m_dt_bias": dt_m + jnp.log(-jnp.expm1(-dt_m)),
        "m_a_log": jnp.log(jax.random.uniform(ks[27], (D, M_HEADS), f32, 1.0, 16.0)),
        "m_d": 1.0 + nrm(ks[28], (D, M_HEADS), 0.1),
        "m_norm": 1.0 + nrm(ks[29], (D, M_INNER), 0.01),
        "w_branch_s5": nrm(ks[31], (D, S5_WIDTH, D_MODEL), S5_WIDTH ** -0.5),
        "w_branch_ssd": nrm(ks[32], (D, M_INNER, D_MODEL), M_INNER ** -0.5),
        "w_out": nrm(ks[33], (D, D_MODEL, D_MODEL), D_MODEL ** -0.5),
        "w_ffn_in": nrm(ks[34], (D, D_MODEL, 2 * D_FF), D_MODEL ** -0.5),
        "w_ffn_out": nrm(ks[35], (D, D_FF, D_MODEL), D_FF ** -0.5),
    }


def reference(x_prompt, x_sample, state_s5_re, state_s5_im, state_ssm, state_conv, c_prompt, c_sample,
              w_ada, b_ada, norm_pre_mix, norm_post_mix, norm_pre_ffn, norm_post_ffn, w_in,
              s5_lam_re, s5_lam_im, s5_log_dt, s5_b_re, s5_b_im, s5_c_re, s5_c_im, s5_d, s5_w_glu, s5_b_glu,
              m_conv_w, m_conv_b, m_dt_bias, m_a_log, m_d, m_norm,
              w_branch_s5, w_branch_ssd, w_out, w_ffn_in, w_ffn_out):
    bp = x_prompt.shape[0]
    dt_ = x_prompt.dtype
    xp, xs = x_prompt, x_sample
    p_re, p_im, p_ssm, p_conv = [], [], [], []
    s_re, s_im, s_ssm, s_conv = [], [], [], []
    for l in range(DEPTH):
        p = {
            "w_ada": w_ada[l], "b_ada": b_ada[l],
            "norm_pre_mix": norm_pre_mix[l], "norm_post_mix": norm_post_mix[l],
            "norm_pre_ffn": norm_pre_ffn[l], "norm_post_ffn": norm_post_ffn[l],
            "w_in": w_in[l],
            "s5_lam_re": s5_lam_re[l], "s5_lam_im": s5_lam_im[l], "s5_log_dt": s5_log_dt[l],
            "s5_b_re": s5_b_re[l], "s5_b_im": s5_b_im[l], "s5_c_re": s5_c_re[l], "s5_c_im": s5_c_im[l],
            "s5_d": s5_d[l], "s5_w_glu": s5_w_glu[l], "s5_b_glu": s5_b_glu[l],
            "m_conv_w": m_conv_w[l], "m_conv_b": m_conv_b[l], "m_dt_bias": m_dt_bias[l],
            "m_a_log": m_a_log[l], "m_d": m_d[l], "m_norm": m_norm[l],
            "w_branch_s5": w_branch_s5[l], "w_branch_ssd": w_branch_ssd[l], "w_out": w_out[l],
            "w_ffn_in": w_ffn_in[l], "w_ffn_out": w_ffn_out[l],
        }
        xp, a1, a2, a3, a4 = _layer(
            xp, c_prompt,
            jnp.zeros((bp, S5_GROUPS, S5_STATE), dt_), jnp.zeros((bp, S5_GROUPS, S5_STATE), dt_),
            jnp.zeros((bp, M_HEADS, M_HEADDIM, M_STATE), dt_), jnp.zeros((bp, M_CONV - 1, M_CONV_DIM), dt_), p)
        p_re.append(a1); p_im.append(a2); p_ssm.append(a3); p_conv.append(a4)
        xs, b1, b2, b3, b4 = _layer(xs, c_sample, state_s5_re[l], state_s5_im[l], state_ssm[l], state_conv[l], p)
        s_re.append(b1); s_im.append(b2); s_ssm.append(b3); s_conv.append(b4)
    y_prompt = xp
    y_sample = xs
    s5_re_prompt = jnp.stack(p_re, 0)
    s5_im_prompt = jnp.stack(p_im, 0)
    ssm_prompt = jnp.stack(p_ssm, 0)
    conv_prompt = jnp.stack(p_conv, 0)
    s5_re_sample = jnp.stack(s_re, 0)
    s5_im_sample = jnp.stack(s_im, 0)
    ssm_sample = jnp.stack(s_ssm, 0)
    conv_sample = jnp.stack(s_conv, 0)
    return (y_prompt, y_sample, s5_re_prompt, s5_im_prompt, ssm_prompt, conv_prompt,
            s5_re_sample, s5_im_sample, ssm_sample, conv_sample)
```

```python
import math
import numpy as np
from contextlib import ExitStack
import concourse.bass as bass
import concourse.mybir as mybir
from concourse.bass_utils import run_bass_kernel_spmd

F32 = mybir.dt.float32
BF16 = mybir.dt.bfloat16
AF = mybir.ActivationFunctionType
ALU = mybir.AluOpType
AX = mybir.AxisListType

ENGS = ("pe", "act", "dve", "pool", "sp")
NCORES = 8
D = 1024
SEQ = 2048
NS = 16
T = 128
NST = SEQ // T
D_FF = 2816
IN_COLS = 8224
OFF_Z, OFF_XBC, OFF_DT, OFF_GA, OFF_GB = 1024, 3072, 6144, 6176, 7200
EPS = 1e-6
TWO_PI = 2.0 * math.pi
DEBUG = {}
SAMPLE = {}
STOP = None
NST_RUN = NST
ILW = [1, 2, 1]
PROBE_WB = False


class Prog:
    def __init__(self, nc, stack):
        self.nc = nc
        self.stack = stack
        self.streams = {e: [] for e in ENGS}
        self.cnt = {e: 0 for e in ENGS}
        self.sems = {e: stack.enter_context(nc.semaphore("s_" + e)) for e in ENGS}
        self.waited = {e: {} for e in ENGS}
        self.lastw = {}
        self.readers = {}
        self.chan_sem = {}
        self.chan_cnt = {}

    def _deps(self, reads, writes):
        ev = []
        for r in reads:
            if r in self.lastw:
                ev.append(self.lastw[r])
        for w in writes:
            if w in self.lastw:
                ev.append(self.lastw[w])
            ev.extend(self.readers.get(w, ()))
        return ev

    def _commit(self, reads, writes, event):
        for w in writes:
            self.lastw[w] = event
            self.readers[w] = []
        for r in reads:
            if r in writes:
                continue
            self.readers.setdefault(r, []).append(event)

    def _emit_waits(self, eng, deps):
        need = {}
        for (src, val) in deps:
            if val > need.get(src, 0):
                need[src] = val
        out = []
        for src, val in need.items():
            if self.waited[eng].get(src, 0) >= val:
                continue
            self.waited[eng][src] = val
            sem = self.sems[src] if src in self.sems else self.chan_sem[src]
            out.append((sem, val))
        return out

    def op(self, eng, fn, reads=(), writes=()):
        reads = list(reads)
        writes = list(writes)
        deps = self._deps(reads, writes)
        waits = self._emit_waits(eng, deps)
        self.cnt[eng] += 1
        sem = self.sems[eng]
        calls = []

        class _Rec:
            def __getattr__(self_, name):
                def f(*a, **k):
                    calls.append((name, a, k))
                    return None
                return f

        fn(_Rec())
        assert calls

        def run(e, calls=calls, waits=waits, sem=sem):
            for (s, v) in waits:
                e.wait_ge(s, v)
            last = None
            for (name, a, k) in calls:
                last = getattr(e, name)(*a, **k)
            last.then_inc(sem, 1)

        self.streams[eng].append(run)
        self._commit(reads, writes, (eng, self.cnt[eng]))

    def dma(self, queue, out, in_, chan, reads=(), writes=(), **kw):
        reads = list(reads)
        writes = list(writes)
        if chan not in self.chan_sem:
            self.chan_sem[chan] = self.stack.enter_context(self.nc.semaphore("c_" + str(chan)))
            self.chan_cnt[chan] = 0
        deps = self._deps(reads, writes)
        waits = self._emit_waits(queue, deps)
        self.chan_cnt[chan] += 16
        val = self.chan_cnt[chan]
        csem = self.chan_sem[chan]

        def run(e, waits=waits, csem=csem, out=out, in_=in_, kw=kw):
            for (s, v) in waits:
                e.wait_ge(s, v)
            e.dma_start(out=out, in_=in_, **kw).then_inc(csem, 16)

        self.streams[queue].append(run)
        self._commit(reads, writes, (chan, val))

    def barrier(self):
        evs = [(e, self.cnt[e]) for e in ENGS if self.cnt[e] > 0]
        evs += [(c, v) for c, v in self.chan_cnt.items() if v > 0]
        for eng in ENGS:
            waits = self._emit_waits(eng, evs)

            def run(e, waits=waits):
                for (s, v) in waits:
                    e.wait_ge(s, v)

            self.streams[eng].append(run)

    def emit_phase(self):
        self.barrier()
        self.emit()
        self.streams = {e: [] for e in ENGS}

    def final_wait(self, eng, keys):
        deps = [self.lastw[k] for k in keys if k in self.lastw]
        waits = self._emit_waits(eng, deps)

        def run(e, waits=waits):
            for (s, v) in waits:
                e.wait_ge(s, v)

        self.streams[eng].append(run)

    def emit(self):
        nc = self.nc
        with nc.Block() as block:
            @block.tensor
            def _(e):
                for f in self.streams["pe"]:
                    f(e)

            @block.scalar
            def _(e):
                for f in self.streams["act"]:
                    f(e)

            @block.vector
            def _(e):
                for f in self.streams["dve"]:
                    f(e)

            @block.gpsimd
            def _(e):
                for f in self.streams["pool"]:
                    f(e)

            @block.sync
            def _(e):
                for f in self.streams["sp"]:
                    f(e)


WEIGHT_SHAPES = {
    "w_ada": [D, 6 * D], "b_ada": [1, 6 * D],
    "norm_pre_mix": [1, D], "norm_post_mix": [1, D], "norm_pre_ffn": [1, D], "norm_post_ffn": [1, D],
    "w_in": [D, IN_COLS],
    "s5_lam_re": [64, 64], "s5_lam_im": [64, 64], "s5_log_dt": [1, 64],
    "s5_b_re": [64, 64, 16], "s5_b_im": [64, 64, 16], "s5_c_re": [64, 16, 64], "s5_c_im": [64, 16, 64],
    "s5_d": [1, D], "s5_w_glu": [D, D], "s5_b_glu": [1, D],
    "m_conv_w": [4, 3072], "m_conv_b": [1, 3072], "m_dt_bias": [1, 32], "m_a_log": [1, 32], "m_d": [1, 32],
    "m_norm": [1, 2048],
    "w_branch_s5": [D, D], "w_branch_ssd": [2048, D], "w_out": [D, D],
    "w_ffn_in": [D, 2 * D_FF], "w_ffn_out": [D_FF, D],
}
IN_SHAPES = {
    "xp": [SEQ, D], "xs": [NS, D], "c17": [NS + 1, D],
    "s5re_in": [NS, 4096], "s5im_in": [NS, 4096], "ssm_in": [NS, 32, 64, 128], "conv_in": [NS * 3, 3072],
}
OUT_SHAPES = {
    "y_p": [SEQ, D], "y_s": [NS, D], "s5re_p": [32, 128], "s5im_p": [32, 128], "ssm_p": [2048, 128],
    "conv_p": [3, 3072], "s5re_s": [NS, 4096], "s5im_s": [NS, 4096], "ssm_s": [NS, 32, 64, 128],
    "conv_s": [NS, 3, 3072],
}


def build_nc(debug_names=(), stop=None, nst=None):
    global STOP, NST_RUN
    STOP = stop
    NST_RUN = nst or NST
    SAMPLE.clear()
    nc = bass.Bass("TRN2", target_bir_lowering=False)
    dr = {}
    for n, s in list(IN_SHAPES.items()) + list(WEIGHT_SHAPES.items()):
        dr[n] = nc.dram_tensor(n, s, F32, kind="ExternalInput").ap()
    for n, s in OUT_SHAPES.items():
        dr[n] = nc.dram_tensor(n, s, F32, kind="ExternalOutput").ap()
    dbg_out = {}

    with ExitStack() as st:
        st.enter_context(nc.allow_non_contiguous_dma(reason="small strided parameter loads"))
        P = Prog(nc, st)

        ARW = 52800
        arena = st.enter_context(nc.sbuf_tensor("arena", [128, ARW], F32))
        bump = {"lo": 0, "hi": ARW, "peak": 0}

        def _carve(off, shape, dt):
            esz = 2 if dt == BF16 else 4
            n = 1
            for d_ in shape[1:]:
                n *= d_
            words = (n * esz + 3) // 4
            v = arena[0:shape[0], off:off + words]
            if dt != F32:
                v = v.bitcast(dt)
            v = v[:, 0:n]
            if len(shape) == 3:
                v = v.rearrange("p (a b) -> p a b", a=shape[1])
            elif len(shape) == 4:
                v = v.rearrange("p (a b c) -> p a b c", a=shape[1], b=shape[2])
            elif len(shape) == 5:
                v = v.rearrange("p (a b c d) -> p a b c d", a=shape[1], b=shape[2], c=shape[3])
            return v, words

        def sb(name, shape, dt=F32):
            v, words = _carve(bump["lo"], shape, dt)
            bump["lo"] += words
            assert bump["lo"] <= bump["hi"], ("SBUF arena overflow at", name, bump)
            bump["peak"] = max(bump["peak"], bump["lo"])
            return v

        def ssb(name, shape, dt=F32):
            esz = 2 if dt == BF16 else 4
            n = 1
            for d_ in shape[1:]:
                n *= d_
            words = (n * esz + 3) // 4
            bump["hi"] -= words
            assert bump["lo"] <= bump["hi"], ("SBUF arena overflow (scratch) at", name, bump)
            v, _ = _carve(bump["hi"], shape, dt)
            return v

        def psum(name, shape, dt=F32):
            return st.enter_context(nc.psum_tensor(name, shape, dt))

        def dump(name, ap, key, shape):
            if name not in debug_names:
                return
            t = nc.dram_tensor("dbg_" + name, shape, F32, kind="ExternalOutput").ap()
            dbg_out[name] = t
            P.dma("sp" if ap.dtype == F32 else "pool", t, ap, "dbg_" + name, reads=[key], writes=["dbg_" + name])

        def finish():
            P.final_wait("sp", [k for k in P.lastw if str(k).startswith(("o_", "yout", "dbg_"))])
            P.emit()
            return nc, dbg_out

        NPS = 6
        ps_f = [psum("psf%d" % i, [128, 512], F32) for i in range(NPS)]
        ps_b = [psum("psb%d" % i, [128, 1024], BF16) for i in range(2)]
        ps_ctr = [0, 0]

        def next_ps():
            i = ps_ctr[0] % NPS
            ps_ctr[0] += 1
            return ps_f[i], "psf%d" % i

        def next_psb():
            i = ps_ctr[1] % 2
            ps_ctr[1] += 1
            return ps_b[i], "psb%d" % i

        ident = sb("ident", [128, 128], F32)
        identb = sb("identb", [128, 128], BF16)
        ones = sb("ones", [128, 128], F32)
        tri = sb("tri", [128, 128], F32)
        su = sb("su", [128, 128], F32)
        P.op("pool", lambda e: e.memset(ones[:], 1.0), writes=["ones"])
        P.op("pool", lambda e: e.affine_select(out=ident[:], in_=ones[:], pattern=[[-1, 128]], compare_op=ALU.is_equal,
                                               fill=0.0, base=0, channel_multiplier=1), reads=["ones"], writes=["ident"])
        P.op("pool", lambda e: e.affine_select(out=tri[:], in_=ones[:], pattern=[[1, 128]], compare_op=ALU.is_ge,
                                               fill=0.0, base=0, channel_multiplier=-1), reads=["ones"], writes=["tri"])
        P.op("pool", lambda e: e.affine_select(out=su[:], in_=ones[:], pattern=[[-1, 128]], compare_op=ALU.is_gt,
                                               fill=0.0, base=0, channel_multiplier=1), reads=["ones"], writes=["su"])
        P.op("dve", lambda e: e.tensor_copy(out=identb[:], in_=ident[:]), reads=["ident"], writes=["identb"])

        def bc_row(name, src, n, alloc=None):
            t = (alloc or sb)(name, [128, n], F32)
            P.dma("sp", t[:], src.partition_broadcast(128), name, writes=[name])
            return t

        mnorm = sb("mnorm", [128, 2048], BF16)
        P.dma("pool", mnorm[:], dr["m_norm"][0, :].partition_broadcast(128), "mnorm", writes=["mnorm"])
        dtb = bc_row("dtb", dr["m_dt_bias"][0, :], 32)
        alog = bc_row("alog", dr["m_a_log"][0, :], 32, ssb)
        mdr = bc_row("mdr", dr["m_d"][0, :], 32)
        arow = sb("arow", [128, 32], F32)
        P.op("act", lambda e: e.activation(out=arow[:], in_=alog[:], func=AF.Exp), reads=["alog"], writes=["arow"])
        P.op("dve", lambda e: e.tensor_scalar(out=arow[:], in0=arow[:], scalar1=-1.0, scalar2=None, op0=ALU.mult),
             reads=["arow"], writes=["arow"])
        s5d = sb("s5d", [128, 8], F32)
        bglu = sb("bglu", [128, 8], F32)
        cw = sb("cw", [128, 4, 24], F32)
        cb = sb("cb", [128, 24], F32)
        P.dma("sp", s5d[:], dr["s5_d"][0, :].rearrange("(a p) -> p a", p=128), "s5d", writes=["s5d"])
        P.dma("sp", bglu[:], dr["s5_b_glu"][0, :].rearrange("(a p) -> p a", p=128), "bglu", writes=["bglu"])
        P.dma("sp", cb[:], dr["m_conv_b"][0, :].rearrange("(a p) -> p a", p=128), "cb", writes=["cb"])
        for k in range(4):
            P.dma("sp", cw[:, k, :], dr["m_conv_w"][k, :].rearrange("(a p) -> p a", p=128), "cw", writes=["cw"])

        if STOP == "s0":
            dump("cw", cw[:].rearrange("p a b -> p (a b)"), "cw", [128, 96])
            dump("mnorm", mnorm[:], "mnorm", [128, 2048])
            dump("tri", tri[:], "tri", [128, 128])
            return finish()
        NWB = 3
        WBE = 4096
        scratch = {}
        cast_prev = {}
        wbufs = [sb("wbuf%d" % i, [128, WBE], BF16) for i in range(NWB)]
        WB_EXTRA = []

        def scratch_chunk(wd, K, c0, cwid):
            nm = wd.name
            if nm == "w_ada":
                return None, None
            key = (nm, c0, cwid)
            if key not in scratch:
                KT = K // 128
                t = nc.dram_tensor("scr_%s_%d_%d" % (nm, c0, cwid), [128, KT * cwid], BF16).ap()
                sk = "scr_%s_%d" % (nm, c0)
                src = wd.rearrange("(k p) c -> p k c", p=128)[:, :, c0:c0 + cwid]
                ch = len(scratch) % 8
                prev = cast_prev.get(ch)
                P.dma("pool", t.rearrange("p (k c) -> p k c", c=cwid), src, "cast%d" % ch, reads=([prev] if prev else []), writes=[sk])
                cast_prev[ch] = sk
                scratch[key] = (t, sk)
            return scratch[key]

        _cw_dummy = None

        def _cw(KT, ncols):
            c = min(512, ncols)
            while c * KT > WBE:
                c //= 2
            return c
        w_ctr = [0]

        def load_w(wd, K, c0, cwid):
            KT = (K + 127) // 128
            i = w_ctr[0] % (NWB + len(WB_EXTRA))
            w_ctr[0] += 1
            key = "wbuf%d" % i
            flat = (wbufs + WB_EXTRA)[i][:, 0:KT * cwid]
            view = flat.rearrange("p (k c) -> p k c", c=cwid)
            sc, sk = scratch_chunk(wd, K, c0, cwid)
            if sc is None:
                src = wd.rearrange("(k p) c -> p k c", p=128)[:, :, c0:c0 + cwid]
                P.dma("pool", view, src, key, writes=[key])
            else:
                P.dma("sp", flat, sc, key, reads=[sk], writes=[key])
            return view, key

        def drain(g):
            for _ in g:
                pass

        def interleave(gens, weights=None):
            gens = list(gens)
            weights = list(weights or [1] * len(gens))
            live = [True] * len(gens)
            while any(live):
                for i, g in enumerate(gens):
                    if not live[i]:
                        continue
                    for _ in range(weights[i]):
                        try:
                            next(g)
                        except StopIteration:
                            live[i] = False
                            break

        def linear_fm(*a, **k):
            drain(linear_fm_g(*a, **k))

        def linear_tm(*a, **k):
            drain(linear_tm_g(*a, **k))

        def transpose_to(*a, **k):
            drain(transpose_to_g(*a, **k))

        def linear_fm_g(inT, in_key, K, wd, c0, ncols, tt, evac, cwid=512):
            KT = K // 128
            cwid = _cw(KT, ncols)
            ct = 0
            for cc in range(0, ncols, cwid):
                wv, wk = load_w(wd, K, c0 + cc, cwid)
                for j in range(0, cwid, 128):
                    pt, pk = next_ps()

                    def mm(e, pt=pt, wv=wv, j=j):
                        last = None
                        for kt in range(KT):
                            last = e.matmul(pt[:, 0:tt], lhsT=wv[:, kt, j:j + 128], rhs=inT[:, kt, 0:tt],
                                            start=(kt == 0), stop=(kt == KT - 1))
                        return last

                    P.op("pe", mm, reads=[wk, in_key], writes=[pk])
                    evac(pt, pk, ct)
                    ct += 1
                yield

        def linear_tm_g(inT, in_key, K, wd, c0, ncols, tiles, evac, cwid=512):
            KT = K // 128
            cwid = _cw(KT, ncols)
            for cc in range(0, ncols, cwid):
                wv, wk = load_w(wd, K, c0 + cc, cwid)
                for ti, (t0, rows) in enumerate(tiles):
                    pt, pk = next_ps()

                    def mm(e, pt=pt, wv=wv, t0=t0, rows=rows):
                        last = None
                        for kt in range(KT):
                            last = e.matmul(pt[0:rows, 0:cwid], lhsT=inT[:, kt, t0:t0 + rows], rhs=wv[:, kt, :],
                                            start=(kt == 0), stop=(kt == KT - 1))
                        return last

                    P.op("pe", mm, reads=[wk, in_key], writes=[pk])
                    evac(pt, pk, ti, cc, cwid)
                    yield

        ev_ctr = [0]

        def copy_evac(dst, dkey, src, skey):
            ev_ctr[0] += 1
            if True:
                P.op("act", lambda e: e.copy(out=dst, in_=src), reads=[skey], writes=[dkey])
            else:
                P.op("dve", lambda e: e.tensor_copy(out=dst, in_=src), reads=[skey], writes=[dkey])

        def transpose_to_g(dst_fn, dkey, src_fn, skey, n, rows, cols, dt):
            rp = (rows + 3) // 4 * 4
            for i0 in range(0, n, 4):
                cnt = min(4, n - i0)
                if dt == BF16:
                    pt, pk = next_psb()
                    idn = identb
                else:
                    pt, pk = next_ps()
                    idn = ident

                def tr(e, pt=pt, i0=i0, cnt=cnt, idn=idn):
                    last = None
                    for a in range(cnt):
                        last = e.transpose(pt[0:cols, a * rp:a * rp + rows], src_fn(i0 + a), idn[0:rows, 0:rows])
                    return last

                P.op("pe", tr, reads=[skey, "ident", "identb"], writes=[pk])
                src = pt[0:cols, 0:cnt * rp].rearrange("p (a r) -> p a r", r=rp)[:, :, 0:rows]
                copy_evac(dst_fn(i0, cnt), dkey, src, pk)
                yield

        modp = sb("modp", [128, 2, D], BF16)
        modcol = sb("modcol", [128, 6, 8], F32)

        def ada_compute(alloc, want_prompt, mods_t=None):
            c17 = alloc("c17", [NS + 1, D], F32)
            c17s = alloc("c17s", [NS + 1, D], BF16)
            c17T = alloc("c17T", [128, 8, NS + 1], BF16)
            badac = alloc("badac", [NS + 1, 512], F32)
            modc = alloc("modc", [NS + 1, 512], F32)
            nrows = {}
            for nm_, key_ in (("npm", "norm_pre_mix"), ("npo", "norm_post_mix"), ("npf", "norm_pre_ffn"), ("nqf", "norm_post_ffn")):
                t_ = alloc(nm_, [128, D], F32)
                P.dma("sp", t_[:], dr[key_][0, :].partition_broadcast(128), nm_, writes=[nm_])
                nrows[nm_] = t_
            P.dma("sp", c17[:], dr["c17"], "c17", writes=["c17"])
            P.op("act", lambda e: e.activation(out=c17s[:], in_=c17[:], func=AF.Silu), reads=["c17"], writes=["c17s"])
            transpose_to(lambda i0, cnt: c17T[:, i0:i0 + cnt, :], "c17T", lambda i: c17s[:, i * 128:(i + 1) * 128], "c17s",
                         8, NS + 1, 128, BF16)
            if want_prompt:
                sel16 = alloc("sel16", [NS + 1, 128], F32)
                P.op("pool", lambda e: e.affine_select(out=sel16[:], in_=ones[0:NS + 1, :], pattern=[[0, 128]], compare_op=ALU.is_equal,
                                                       fill=0.0, base=-NS, channel_multiplier=1), reads=["ones"], writes=["sel16"])

            def ada_evac(pt, pk, ti, cc, cwid):
                j, off = cc // D, cc % D
                P.dma("sp", badac[:, 0:cwid], dr["b_ada"][0, cc:cc + cwid].partition_broadcast(NS + 1), "badac", writes=["badac"])
                P.op("dve", lambda e: e.tensor_tensor(out=modc[:, 0:cwid], in0=pt[0:NS + 1, 0:cwid], in1=badac[:, 0:cwid], op=ALU.add),
                     reads=[pk, "badac"], writes=["modc"])
                if want_prompt and j in (2, 5):
                    p2, pk2 = next_ps()
                    P.op("pe", lambda e: e.matmul(p2[:, 0:cwid], lhsT=sel16[:, :], rhs=modc[:, 0:cwid], start=True, stop=True),
                         reads=["sel16", "modc"], writes=[pk2])
                    P.op("act", lambda e: e.copy(out=modp[:, j // 3, off:off + cwid], in_=p2[:, 0:cwid]), reads=[pk2], writes=["modp"])
                elif want_prompt:
                    p2, pk2 = next_ps()

                    def trm(e):
                        last = None
                        for b_ in range(cwid // 128):
                            last = e.transpose(p2[:, b_ * 32:b_ * 32 + NS + 1], modc[0:NS + 1, b_ * 128:(b_ + 1) * 128], ident[0:NS + 1, 0:NS + 1])
                        return last

                    P.op("pe", trm, reads=["modc", "ident"], writes=[pk2])
                    kt0 = off // 128
                    P.op("act", lambda e: e.copy(out=modcol[:, j, kt0:kt0 + cwid // 128],
                                                 in_=p2[:, 0:(cwid // 128) * 32].rearrange("p (b c) -> p b c", c=32)[:, :, NS]),
                         reads=[pk2], writes=["modcol"])
                else:
                    P.op("dve", lambda e: e.tensor_copy(out=mods_t[:, j, off:off + cwid], in_=modc[0:NS, 0:cwid]), reads=["modc"], writes=["mods"])

            linear_tm(c17T, "c17T", D, dr["w_ada"], 0, 6 * D, [(0, NS + 1)], ada_evac)
            if want_prompt:
                for (jsc, wname) in ((1, "norm_pre_mix"), (4, "norm_pre_ffn")):
                    ncol = alloc("ncol%d" % jsc, [128, 8], F32)
                    P.dma("sp", ncol[:], dr[wname][0, :].rearrange("(a p) -> p a", p=128), "ncol%d" % jsc, writes=["ncol%d" % jsc])
                    P.op("dve", lambda e, jsc=jsc, ncol=ncol: e.scalar_tensor_tensor(
                        out=modcol[:, jsc, :], in0=modcol[:, jsc, :], scalar=1.0, op0=ALU.add, in1=ncol[:], op1=ALU.mult),
                        reads=["modcol", "ncol%d" % jsc], writes=["modcol"])
                for (gi, nwk) in ((0, "npo"), (1, "nqf")):
                    nw = nrows[nwk]
                    P.op("dve", lambda e, gi=gi, nw=nw: e.tensor_tensor(out=modp[:, gi, :], in0=modp[:, gi, :], in1=nw[:, :], op=ALU.mult),
                         reads=["modp", nwk], writes=["modp"])
            else:
                mt, rows, key = mods_t, NS, "mods"
                for (jsc, nwk) in ((1, "npm"), (4, "npf")):
                    nw = nrows[nwk]
                    P.op("dve", lambda e, jsc=jsc, nw=nw: e.scalar_tensor_tensor(
                        out=mt[:, jsc, :], in0=mt[:, jsc, :], scalar=1.0, op0=ALU.add, in1=nw[0:rows, :], op1=ALU.mult),
                        reads=[key, nwk], writes=[key])
                for (jg, nwk) in ((2, "npo"), (5, "nqf")):
                    nw = nrows[nwk]
                    P.op("dve", lambda e, jg=jg, nw=nw: e.tensor_tensor(
                        out=mt[:, jg, :], in0=mt[:, jg, :], in1=nw[0:rows, :], op=ALU.mult), reads=[key, nwk], writes=[key])

        ada_compute(ssb, True)

        if STOP == "s1":
            dump("modp", modp[:].rearrange("p a b -> p (a b)"), "modp", [128, 2 * D])
            dump("modcol", modcol[:].rearrange("p a b -> p (a b)"), "modcol", [128, 48])
            return finish()
        lre = ssb("lre", [128, 32], F32)
        lim = ssb("lim", [128, 32], F32)
        ldt = ssb("ldt", [128, 32], F32)
        P.dma("sp", lre[:], dr["s5_lam_re"].rearrange("(q gl) n -> (gl n) q", gl=2), "lre", writes=["lre"])
        P.dma("sp", lim[:], dr["s5_lam_im"].rearrange("(q gl) n -> (gl n) q", gl=2), "lim", writes=["lim"])
        for gl in range(2):
            P.dma("sp", ldt[gl * 64:(gl + 1) * 64, :],
                  dr["s5_log_dt"][0, :].rearrange("(q gl) -> gl q", gl=2)[gl, :].partition_broadcast(64), "ldt", writes=["ldt"])
        s5t = sb("s5t", [128, 12, 32], F32)
        DT_, TH, RHO, AR, AI, FR, FI, DEN, T0, T1, T2, T3 = range(12)

        def s5op(fn, wr):
            P.op("dve", fn, reads=["lre", "lim", "ldt", "s5t"], writes=wr)

        def frac_sin(out_ap, ang_ap, tmp_f, tmp_i, shape_key, quarter):
            P.op("dve", lambda e: e.tensor_scalar(out=tmp_f, in0=ang_ap, scalar1=1.0 / TWO_PI, scalar2=0.25 * quarter,
                                                  op0=ALU.mult, op1=ALU.add), reads=[shape_key], writes=[shape_key])
            P.op("dve", lambda e: e.tensor_copy(out=tmp_i, in_=tmp_f), reads=[shape_key], writes=[shape_key])
            P.op("dve", lambda e: e.tensor_copy(out=out_ap, in_=tmp_i), reads=[shape_key], writes=[shape_key])
            P.op("dve", lambda e: e.tensor_tensor(out=tmp_f, in0=tmp_f, in1=out_ap, op=ALU.subtract), reads=[shape_key], writes=[shape_key])
            P.op("dve", lambda e: e.tensor_scalar(out=out_ap, in0=tmp_f, scalar1=0.5, scalar2=None, op0=ALU.is_gt),
                 reads=[shape_key], writes=[shape_key])
            P.op("dve", lambda e: e.tensor_tensor(out=tmp_f, in0=tmp_f, in1=out_ap, op=ALU.subtract), reads=[shape_key], writes=[shape_key])
            P.op("dve", lambda e: e.tensor_scalar(out=out_ap, in0=tmp_f, scalar1=-0.5, scalar2=None, op0=ALU.is_lt),
                 reads=[shape_key], writes=[shape_key])
            P.op("dve", lambda e: e.tensor_tensor(out=tmp_f, in0=tmp_f, in1=out_ap, op=ALU.add), reads=[shape_key], writes=[shape_key])
            P.op("act", lambda e: e.activation(out=out_ap, in_=tmp_f, func=AF.Sin, scale=TWO_PI), reads=[shape_key], writes=[shape_key])

        tmpi = ssb("tmpi", [128, 32 * 64], mybir.dt.int32)
        tmpf = ssb("tmpf", [128, 32 * 64], F32)
        P.op("act", lambda e: e.activation(out=s5t[:, DT_, :], in_=ldt[:], func=AF.Exp), reads=["ldt"], writes=["s5t"])
        s5op(lambda e: e.tensor_tensor(out=s5t[:, TH, :], in0=lim[:], in1=s5t[:, DT_, :], op=ALU.mult), ["s5t"])
        s5op(lambda e: e.tensor_tensor(out=s5t[:, T0, :], in0=lre[:], in1=s5t[:, DT_, :], op=ALU.mult), ["s5t"])
        P.op("act", lambda e: e.activation(out=s5t[:, RHO, :], in_=s5t[:, T0, :], func=AF.Exp), reads=["s5t"], writes=["s5t"])
        frac_sin(s5t[:, T1, :], s5t[:, TH, :], tmpf[:, 0:32], tmpi[:, 0:32], "s5t", 1)
        frac_sin(s5t[:, T2, :], s5t[:, TH, :], tmpf[:, 0:32], tmpi[:, 0:32], "s5t", 0)
        s5op(lambda e: e.tensor_tensor(out=s5t[:, AR, :], in0=s5t[:, RHO, :], in1=s5t[:, T1, :], op=ALU.mult), ["s5t"])
        s5op(lambda e: e.tensor_tensor(out=s5t[:, AI, :], in0=s5t[:, RHO, :], in1=s5t[:, T2, :], op=ALU.mult), ["s5t"])
        s5op(lambda e: e.tensor_tensor(out=s5t[:, T0, :], in0=lre[:], in1=lre[:], op=ALU.mult), ["s5t"])
        s5op(lambda e: e.tensor_tensor(out=s5t[:, T1, :], in0=lim[:], in1=lim[:], op=ALU.mult), ["s5t"])
        s5op(lambda e: e.tensor_tensor(out=s5t[:, DEN, :], in0=s5t[:, T0, :], in1=s5t[:, T1, :], op=ALU.add), ["s5t"])
        s5op(lambda e: e.reciprocal(out=s5t[:, DEN, :], in_=s5t[:, DEN, :]), ["s5t"])
        s5op(lambda e: e.tensor_scalar(out=s5t[:, T0, :], in0=s5t[:, AR, :], scalar1=-1.0, scalar2=None, op0=ALU.add), ["s5t"])
        s5op(lambda e: e.tensor_tensor(out=s5t[:, T1, :], in0=s5t[:, T0, :], in1=lre[:], op=ALU.mult), ["s5t"])
        s5op(lambda e: e.tensor_tensor(out=s5t[:, T2, :], in0=s5t[:, AI, :], in1=lim[:], op=ALU.mult), ["s5t"])
        s5op(lambda e: e.tensor_tensor(out=s5t[:, T1, :], in0=s5t[:, T1, :], in1=s5t[:, T2, :], op=ALU.add), ["s5t"])
        s5op(lambda e: e.tensor_tensor(out=s5t[:, FR, :], in0=s5t[:, T1, :], in1=s5t[:, DEN, :], op=ALU.mult), ["s5t"])
        s5op(lambda e: e.tensor_tensor(out=s5t[:, T1, :], in0=s5t[:, AI, :], in1=lre[:], op=ALU.mult), ["s5t"])
        s5op(lambda e: e.tensor_tensor(out=s5t[:, T2, :], in0=s5t[:, T0, :], in1=lim[:], op=ALU.mult), ["s5t"])
        s5op(lambda e: e.tensor_tensor(out=s5t[:, T1, :], in0=s5t[:, T1, :], in1=s5t[:, T2, :], op=ALU.subtract), ["s5t"])
        s5op(lambda e: e.tensor_tensor(out=s5t[:, FI, :], in0=s5t[:, T1, :], in1=s5t[:, DEN, :], op=ALU.mult), ["s5t"])

        if STOP == "s2":
            dump("s5t", s5t[:].rearrange("p a b -> p (a b)"), "s5t", [128, 12 * 32])
            return finish()
        cosT = sb("cosT", [128, 32, 64], F32)
        sinT = sb("sinT", [128, 32, 64], F32)
        ang = ssb("ang", [128, 32, 64], F32)
        iot = ssb("iot", [128, 64], F32)
        P.op("pool", lambda e: e.iota(iot[:], [[1, 64]], base=1, channel_multiplier=0, allow_small_or_imprecise_dtypes=True),
             writes=["iot"])
        P.op("dve", lambda e: e.tensor_tensor(out=ang[:], in0=s5t[:, TH, :].unsqueeze(2).to_broadcast([128, 32, 64]),
                                              in1=iot[:].unsqueeze(1).to_broadcast([128, 32, 64]), op=ALU.mult),
             reads=["s5t", "iot"], writes=["ang"])
        angf = ang[:].rearrange("p a b -> p (a b)")
        P.op("dve", lambda e: e.tensor_copy(out=tmpf[:, 0:1], in_=tmpf[:, 0:1]), reads=["ang", "s5t"], writes=["tab"])
        frac_sin(cosT[:].rearrange("p a b -> p (a b)"), angf, tmpf[:], tmpi[:], "tab", 1)
        frac_sin(sinT[:].rearrange("p a b -> p (a b)"), angf, tmpf[:], tmpi[:], "tab", 0)

        if STOP == "s3":
            dump("cosT", cosT[:].rearrange("p a b -> p (a b)"), "tab", [128, 2048])
            return finish()
        bre = ssb("bre", [128, 32, 16], F32)
        bim = ssb("bim", [128, 32, 16], F32)
        P.dma("sp", bre[:], dr["s5_b_re"].rearrange("(q gl) n i -> (gl n) q i", gl=2), "bre", writes=["bre"])
        P.dma("sp", bim[:], dr["s5_b_im"].rearrange("(q gl) n i -> (gl n) q i", gl=2), "bim", writes=["bim"])
        bbr = ssb("bbr", [128, 32, 16], F32)
        bbi = ssb("bbi", [128, 32, 16], F32)
        bt = ssb("bt", [128, 32, 16], F32)
        frb = s5t[:, FR, :].unsqueeze(2).to_broadcast([128, 32, 16])
        fib = s5t[:, FI, :].unsqueeze(2).to_broadcast([128, 32, 16])
        RB = ["bre", "bim", "s5t", "bt", "bbr", "bbi"]
        P.op("dve", lambda e: e.tensor_tensor(out=bbr[:], in0=bre[:], in1=frb, op=ALU.mult), reads=RB, writes=["bbr"])
        P.op("dve", lambda e: e.tensor_tensor(out=bt[:], in0=bim[:], in1=fib, op=ALU.mult), reads=RB, writes=["bt"])
        P.op("dve", lambda e: e.tensor_tensor(out=bbr[:], in0=bbr[:], in1=bt[:], op=ALU.subtract), reads=RB, writes=["bbr"])
        P.op("dve", lambda e: e.tensor_tensor(out=bbi[:], in0=bim[:], in1=frb, op=ALU.mult), reads=RB, writes=["bbi"])
        P.op("dve", lambda e: e.tensor_tensor(out=bt[:], in0=bre[:], in1=fib, op=ALU.mult), reads=RB, writes=["bt"])
        P.op("dve", lambda e: e.tensor_tensor(out=bbi[:], in0=bbi[:], in1=bt[:], op=ALU.add), reads=RB, writes=["bbi"])
        wb = sb("wb", [128, 8, 2, 128], BF16)
        x4 = ssb("x4", [128, 4, 2, 16], F32)
        P.op("pool", lambda e: e.memset(x4[:].rearrange("p a b c -> p (a b c)"), 0.0), writes=["x4"])
        for G in range(8):
            for ri, bb in enumerate((bbr, bbi)):
                bk = "bbr" if ri == 0 else "bbi"
                for gl in range(2):
                    P.op("dve", lambda e, G=G, bb=bb, gl=gl: e.tensor_copy(out=x4[gl * 64:(gl + 1) * 64, :, gl, :],
                                                                           in_=bb[gl * 64:(gl + 1) * 64, 4 * G:4 * G + 4, :]),
                         reads=[bk], writes=["x4"])
                pt, pk = next_ps()
                P.op("pe", lambda e, pt=pt: e.transpose(pt[:, 0:128], x4[:].rearrange("p a b c -> p (a b c)"), ident[:]),
                     reads=["x4", "ident"], writes=[pk])
                P.op("act", lambda e, pt=pt, G=G, ri=ri: e.copy(out=wb[:, G, ri, :], in_=pt[:, 0:128]), reads=[pk], writes=["wb"])
        ctr_ = ssb("ctr_", [128, 32, 16], F32)
        cti_ = ssb("cti_", [128, 32, 16], F32)
        zc = ssb("zc", [128, 2, 64], F32)
        for (dst, dk, src) in ((ctr_, "ctr_", dr["s5_c_re"]), (cti_, "cti_", dr["s5_c_im"])):
            for qb in range(4):
                v = src.rearrange("(qq gl) j n -> qq j gl n", gl=2)[8 * qb:8 * qb + 8]
                for qq in range(8):
                    P.dma("sp", zc[16 * qq:16 * qq + 16, :, :], v[qq], "zc", writes=["zc"])
                pt, pk = next_ps()
                P.op("pe", lambda e, pt=pt: e.transpose(pt[:, 0:128], zc[:].rearrange("p a b -> p (a b)"), ident[:]),
                     reads=["zc", "ident"], writes=[pk])
                P.op("act", lambda e, pt=pt, dst=dst, qb=qb: e.copy(out=dst[:, 8 * qb:8 * qb + 8, :],
                                                                   in_=pt[:, 0:128].rearrange("p (a b) -> p a b", b=16)),
                     reads=[pk], writes=[dk])
        wd_ = sb("wd_", [128, 32, 2, 32], BF16)
        P.op("pool", lambda e: e.memset(wd_[:].rearrange("p a b c -> p (a b c)"), 0.0), writes=["wd_"])
        for q in range(32):
            for gl in range(2):
                c0 = 16 * gl
                P.op("dve", lambda e, q=q, gl=gl, c0=c0: e.tensor_copy(out=wd_[gl * 64:(gl + 1) * 64, q, 0, c0:c0 + 16],
                                                                       in_=ctr_[gl * 64:(gl + 1) * 64, q, :]),
                     reads=["ctr_"], writes=["wd_"])
                P.op("dve", lambda e, q=q, gl=gl, c0=c0: e.tensor_scalar(out=wd_[gl * 64:(gl + 1) * 64, q, 1, c0:c0 + 16],
                                                                         in0=cti_[gl * 64:(gl + 1) * 64, q, :], scalar1=-1.0,
                                                                         scalar2=None, op0=ALU.mult),
                     reads=["cti_"], writes=["wd_"])
        P.emit_phase()
        if STOP == "setup":
            dump("cosT", cosT[:].rearrange("p a b -> p (a b)"), "tab", [128, 2048])
            dump("sinT", sinT[:].rearrange("p a b -> p (a b)"), "tab", [128, 2048])
            dump("s5t", s5t[:].rearrange("p a b -> p (a b)"), "s5t", [128, 12 * 32])
            dump("mods", mods[:].rearrange("p a b -> p (a b)"), "mods", [NS, 6 * D])
            return finish()
        bump['hi'] = ARW
        print('arena after setup: lo=%d words' % bump['lo'])

        s5c = sb("s5c", [128, 32, 2], F32)
        P.op("pool", lambda e: e.memset(s5c[:].rearrange("p a b -> p (a b)"), 0.0), writes=["s5c"])
        uz = sb("uz", [128, 4, T], BF16)
        P.op("pool", lambda e: e.memset(uz[:].rearrange("p a b -> p (a b)"), 0.0), writes=["uz"])
        hT = sb("hT", [128, 4, 512], F32)
        hTb = sb("hTb", [128, 4, 512], BF16)
        P.op("pool", lambda e: e.memset(hT[:].rearrange("p a b -> p (a b)"), 0.0), writes=["hT"])
        P.op("pool", lambda e: e.memset(hTb[:].rearrange("p a b -> p (a b)"), 0.0), writes=["hTb"])
        xbcT = sb("xbcT", [128, 24, T + 3], BF16)
        P.op("pool", lambda e: e.memset(xbcT[:].rearrange("p a b -> p (a b)"), 0.0), writes=["xbcT"])

        x_tm = sb("x_tm", [128, 1, D], F32)
        hT_ = sb("hT_", [128, 8, T], BF16)
        uT = sb("uT", [128, 8, T], BF16)
        zs = sb("zs", [128, 1, 2048], BF16)
        gaT = sb("gaT", [128, 8, T], BF16)
        gbT = sb("gbT", [128, 8, T], BF16)
        dtr = sb("dtr", [128, 1, 32], F32)
        yaT = sb("yaT", [128, 8, T], BF16)
        yaoT = sb("yaoT", [128, 8, T], BF16)
        yBT = sb("yBT", [128, 16, T], BF16)
        st1 = sb("st1", [128, 8], F32)
        ys5 = sb("ys5", [128, T], F32)
        cv = [sb("cv%d" % i, [128, T], F32) for i in range(2)]
        sm = sb("sm", [128, 8, 32], F32)
        ytm = sb("ytm", [128, 2048], F32)
        yBtm = sb("yBtm", [128, 2048], BF16)
        hn = sb("hn", [128, D], BF16)
        uT_1 = sb("uT_1", [128, 8, T], BF16)
        zs_1 = sb("zs_1", [128, 1, 2048], BF16)
        gaT_1 = sb("gaT_1", [128, 8, T], BF16)
        gbT_1 = sb("gbT_1", [128, 8, T], BF16)
        dtr_1 = sb("dtr_1", [128, 1, 32], F32)
        yaoT_1 = sb("yaoT_1", [128, 8, T], BF16)
        yBT_1 = sb("yBT_1", [128, 16, T], BF16)
        xbcT_1 = sb("xbcT_1", [128, 24, T + 3], BF16)
        P.op("pool", lambda e: e.memset(xbcT_1[:].rearrange("p a b -> p (a b)"), 0.0), writes=["xbcT1"])
        PB = [dict(uT=uT, zs=zs, xbcT=xbcT, dtr=dtr, gaT=gaT, gbT=gbT, yaoT=yaoT, yBT=yBT),
              dict(uT=uT_1, zs=zs_1, xbcT=xbcT_1, dtr=dtr_1, gaT=gaT_1, gbT=gbT_1, yaoT=yaoT_1, yBT=yBT_1)]
        M1 = bump["lo"]
        actT = sb("actT", [128, 24, T], BF16)
        mrg = sb("mrg", [128, 8, T], BF16)
        mrgT = sb("mrgT", [128, 8, T], BF16)
        factT = sb("factT", [128, 22, T], BF16)
        wk2 = sb("wk2", [128, D], F32)
        M2 = bump["lo"]
        xtail = sb("xtail", [128, 24, 4], F32)
        s5S = [sb("s5S%d" % i, [128, T // 64, 2, 4, 64], F32) for i in range(2)]
        s5t2 = [sb("s5t2%d" % i, [128, 4, T], F32) for i in range(2)]
        rzs = [sb("rz%d" % i, [128, 2, 4, 64], F32) for i in range(2)]
        t8s = [sb("t8%d" % i, [128, 2, 4], F32) for i in range(2)]
        hch = [sb("hch%d" % i, [128, 2, 4, 64], F32) for i in range(2)]
        hbf = [sb("hbf%d" % i, [128, 4, 2, T], BF16) for i in range(2)]
        xtm = sb("xtm", [128, 2048], BF16)
        btm = sb("btm", [128, 4, 128], BF16)
        Rb = sb("Rb", [128, 8, 128], F32)
        LT = sb("LT", [128, 8, 128], BF16)
        MT = sb("MT", [128, 8, 128], BF16)
        CBm = sb("CBm", [128, 128], BF16)
        xdt = sb("xdt", [128, 512], BF16)
        X2 = sb("X2", [128, 512], BF16)
        yt1 = sb("yt1", [128, 512], F32)
        cps = yt1
        if PROBE_WB:
            WB_EXTRA.append(ytm[:, :].bitcast(BF16))
        print("arena: M1=%d M2=%d end=%d of %d" % (M1, M2, bump["lo"], ARW))

        def rms_mod(ti, rows, mt, jA, jB, mkey, tok0):
            xv = x_tm[0:rows, ti, :]
            P.op("act", lambda e: e.activation(out=hn[0:rows, 0:D], in_=xv, func=AF.Square, accum_out=st1[0:rows, 0:1]),
                 reads=["x_tm"], writes=["hn", "st1"])
            P.op("act", lambda e: e.activation(out=st1[0:rows, 1:2], in_=st1[0:rows, 0:1], func=AF.Sqrt, scale=1.0 / D, bias=EPS),
                 reads=["st1"], writes=["st1"])
            P.op("dve", lambda e: e.reciprocal(out=st1[0:rows, 2:3], in_=st1[0:rows, 1:2]), reads=["st1"], writes=["st1"])
            P.op("dve", lambda e: e.scalar_tensor_tensor(out=wk2[0:rows, 0:D], in0=xv, scalar=st1[0:rows, 2:3], op0=ALU.mult,
                                                         in1=mt[0:rows, jA, :], op1=ALU.mult),
                 reads=["x_tm", "st1", mkey], writes=["wk2"])
            P.op("dve", lambda e: e.tensor_tensor(out=hn[0:rows, :], in0=wk2[0:rows, 0:D], in1=mt[0:rows, jB, :], op=ALU.add),
                 reads=["wk2", mkey], writes=["hn"])
            transpose_to(lambda i0, cnt: hT_[:, i0:i0 + cnt, tok0:tok0 + rows], "hT_",
                         lambda i: hn[0:rows, i * 128:(i + 1) * 128], "hn", 8, rows, 128, BF16)

        def rms_mod_p(jA, jB):
            xv = x_tm[:, 0, :]
            P.op("act", lambda e: e.activation(out=hn[:, 0:D], in_=xv, func=AF.Square, accum_out=st1[:, 0:1]),
                 reads=["x_tm"], writes=["hn", "st1"])
            P.op("act", lambda e: e.activation(out=st1[:, 1:2], in_=st1[:, 0:1], func=AF.Sqrt, scale=1.0 / D, bias=EPS),
                 reads=["st1"], writes=["st1"])
            P.op("dve", lambda e: e.reciprocal(out=st1[:, 2:3], in_=st1[:, 1:2]), reads=["st1"], writes=["st1"])
            P.op("dve", lambda e: e.tensor_scalar(out=hn[:, :], in0=xv, scalar1=st1[:, 2:3], scalar2=None, op0=ALU.mult),
                 reads=["x_tm", "st1"], writes=["hn"])
            for i0 in range(0, 8, 4):
                pt, pk = next_psb()

                def tr(e, pt=pt, i0=i0):
                    last = None
                    for a in range(4):
                        last = e.transpose(pt[:, a * 128:(a + 1) * 128], hn[:, (i0 + a) * 128:(i0 + a + 1) * 128], identb[:])
                    return last

                P.op("pe", tr, reads=["hn", "identb"], writes=[pk])
                for a in range(4):
                    kt = i0 + a
                    P.op("act", lambda e, pt=pt, a=a, kt=kt: e.activation(out=hT_[:, kt, 0:128], in_=pt[:, a * 128:(a + 1) * 128], func=AF.Identity,
                                                                         scale=modcol[:, jA, kt:kt + 1], bias=modcol[:, jB, kt:kt + 1]),
                         reads=[pk, "modcol"], writes=["hT_"])

        def resid_gate(src_tm, skey, ti, rows, mt, jG, mkey):
            P.op("act", lambda e: e.activation(out=hn[0:rows, 0:D], in_=src_tm, func=AF.Square, accum_out=st1[0:rows, 4:5]),
                 reads=[skey], writes=["hn", "st1"])
            P.op("act", lambda e: e.activation(out=st1[0:rows, 5:6], in_=st1[0:rows, 4:5], func=AF.Sqrt, scale=1.0 / D, bias=EPS),
                 reads=["st1"], writes=["st1"])
            P.op("dve", lambda e: e.reciprocal(out=st1[0:rows, 6:7], in_=st1[0:rows, 5:6]), reads=["st1"], writes=["st1"])
            P.op("dve", lambda e: e.scalar_tensor_tensor(out=src_tm, in0=src_tm, scalar=st1[0:rows, 6:7], op0=ALU.mult,
                                                         in1=mt[0:rows, jG, :], op1=ALU.mult),
                 reads=[skey, "st1", mkey], writes=[skey])
            P.op("dve", lambda e: e.tensor_tensor(out=x_tm[0:rows, ti, :], in0=x_tm[0:rows, ti, :], in1=src_tm, op=ALU.add),
                 reads=[skey, "x_tm"], writes=["x_tm"])


        def s5_load_u(G, tt, par=0):
            (uT,) = [PB[par][n_] for n_ in ("uT",)]
            kk = lambda n_: n_ if par == 0 else n_ + "1"
            for r in range(4):
                P.op("pool", lambda e, r=r: e.tensor_copy(out=uz[32 * r:32 * r + 32, r, 0:tt], in_=uT[32 * r:32 * r + 32, G, 0:tt]),
                     reads=[kk("uT")], writes=["uz"])

        def s5_rot_in(G, tt, par=0):
            gp = G % 2
            S, Tm = s5S[gp], s5t2[gp]
            sk, tk = "s5S%d" % gp, "s5t2%d" % gp
            nch = tt // 64
            s5_load_u(G, tt, par)
            pr, pkr = next_ps()
            pi_, pki = next_ps()

            def mm(e):
                last = None
                for r in range(4):
                    e.matmul(pr[:, r * tt:(r + 1) * tt], lhsT=wb[:, G, 0, :], rhs=uz[:, r, 0:tt], start=True, stop=True)
                    last = e.matmul(pi_[:, r * tt:(r + 1) * tt], lhsT=wb[:, G, 1, :], rhs=uz[:, r, 0:tt], start=True, stop=True)
                return last

            P.op("pe", mm, reads=["wb", "uz"], writes=[pkr, pki])
            pv = lambda p_: p_[:, 0:4 * tt].rearrange("p (q c t) -> p q c t", q=4, t=64)
            So = lambda ri: S[:, 0:nch, ri].rearrange("p c r t -> p r c t")
            Tv = Tm[:, :, 0:tt].rearrange("p q (c t) -> p q c t", t=64)
            cb_ = cosT[:, 4 * G:4 * G + 4, :].unsqueeze(2).to_broadcast([128, 4, nch, 64])
            sb_ = sinT[:, 4 * G:4 * G + 4, :].unsqueeze(2).to_broadcast([128, 4, nch, 64])
            RK = [pkr, pki, "cosT", "sinT", sk, tk]
            P.op("dve", lambda e: e.tensor_tensor(out=So(0), in0=pv(pr), in1=cb_, op=ALU.mult), reads=RK, writes=[sk])
            P.op("dve", lambda e: e.tensor_tensor(out=Tv, in0=pv(pi_), in1=sb_, op=ALU.mult), reads=RK, writes=[tk])
            P.op("dve", lambda e: e.tensor_tensor(out=So(0), in0=So(0), in1=Tv, op=ALU.add), reads=RK, writes=[sk])
            P.op("dve", lambda e: e.tensor_tensor(out=So(1), in0=pv(pi_), in1=cb_, op=ALU.mult), reads=RK, writes=[sk])
            P.op("dve", lambda e: e.tensor_tensor(out=Tv, in0=pv(pr), in1=sb_, op=ALU.mult), reads=RK, writes=[tk])
            P.op("dve", lambda e: e.tensor_tensor(out=So(1), in0=So(1), in1=Tv, op=ALU.subtract), reads=RK, writes=[sk])

        def s5_scan_chunk(G, c):
            gp = G % 2
            S = s5S[gp]
            sk = "s5S%d" % gp
            hprev = hch[gp]
            hpk = "hch%d" % gp
            rz, t8 = rzs[gp], t8s[gp]
            rzk, t8k = "rz%d" % gp, "t8%d" % gp
            rho4 = s5t[:, RHO, 4 * G:4 * G + 4]
            if c == 0:
                P.op("pool", lambda e: e.tensor_copy(out=rz[:], in_=rho4.unsqueeze(1).unsqueeze(3).to_broadcast([128, 2, 4, 64])),
                     reads=["s5t"], writes=[rzk])
                P.op("pool", lambda e: e.memset(rz[:, :, :, 0:1], 0.0), reads=[rzk], writes=[rzk])
                carry = s5c[:, 4 * G:4 * G + 4, :].rearrange("p r i -> p i r")
            else:
                carry = hprev[:, :, :, 63]
            P.op("dve", lambda e: e.tensor_tensor(out=t8[:], in0=carry, in1=rho4.unsqueeze(1).to_broadcast([128, 2, 4]), op=ALU.mult),
                 reads=["s5c", "s5t", hpk], writes=[t8k])
            P.op("dve", lambda e: e.tensor_tensor(out=S[:, c, :, :, 0], in0=S[:, c, :, :, 0], in1=t8[:], op=ALU.add), reads=[t8k, sk], writes=[sk])
            flat = S[:, c].rearrange("p i r t -> p (i r t)")
            P.op("dve", lambda e: e.tensor_tensor_scan(out=flat, data0=rz[:].rearrange("p i r t -> p (i r t)"), data1=flat,
                                                       initial=0.0, op0=ALU.mult, op1=ALU.add), reads=[sk, rzk], writes=[sk])

        def s5_rot_out(G, c, tt, last):
            gp = G % 2
            S, Tm = s5S[gp], s5t2[gp]
            sk, tk = "s5S%d" % gp, "s5t2%d" % gp
            hc = hch[gp]
            hk = "hch%d" % gp
            sl = slice(c * 64, (c + 1) * 64)
            co, si = cosT[:, 4 * G:4 * G + 4, :], sinT[:, 4 * G:4 * G + 4, :]
            RK = [sk, tk, hk, "cosT", "sinT"]
            E = "pool"
            Tc = Tm[:, :, sl]
            P.op(E, lambda e: e.tensor_tensor(out=hc[:, 0], in0=S[:, c, 0], in1=co, op=ALU.mult), reads=RK, writes=[hk])
            P.op(E, lambda e: e.tensor_tensor(out=Tc, in0=S[:, c, 1], in1=si, op=ALU.mult), reads=RK, writes=[tk])
            P.op(E, lambda e: e.tensor_tensor(out=hc[:, 0], in0=hc[:, 0], in1=Tc, op=ALU.subtract), reads=RK, writes=[hk])
            P.op(E, lambda e: e.tensor_tensor(out=hc[:, 1], in0=S[:, c, 1], in1=co, op=ALU.mult), reads=RK, writes=[hk])
            P.op(E, lambda e: e.tensor_tensor(out=Tc, in0=S[:, c, 0], in1=si, op=ALU.mult), reads=RK, writes=[tk])
            P.op(E, lambda e: e.tensor_tensor(out=hc[:, 1], in0=hc[:, 1], in1=Tc, op=ALU.add), reads=RK, writes=[hk])
            hb = hbf[gp]
            hbk = "hbf%d" % gp
            for ri in range(2):
                P.op("act", lambda e, ri=ri: e.copy(out=hb[:, :, ri, sl], in_=hc[:, ri]), reads=[hk], writes=[hbk])
            if last:
                for ri in range(2):
                    P.op(E, lambda e, ri=ri: e.tensor_copy(out=s5c[:, 4 * G:4 * G + 4, ri], in_=hc[:, ri, :, 63]), reads=[hk], writes=["s5c"])

        def s5_half_g(tt, par, gp):
            nch = tt // 64
            for G in range(gp, 8, 2):
                s5_rot_in(G, tt, par)
                yield
                for c in range(nch):
                    s5_scan_chunk(G, c)
                    yield
                    s5_rot_out(G, c, tt, c == nch - 1)
                    yield
                s5_readout(G, tt, hbf[gp], "hbf%d" % gp, None, par)
                yield

        def s5_prompt_g(tt, par=0):
            live = [s5_half_g(tt, par, 0), s5_half_g(tt, par, 1)]
            while live:
                for g in list(live):
                    try:
                        next(g)
                        yield
                    except StopIteration:
                        live.remove(g)
            yield from s5_glu_g(tt, par)

        def s5_readout(G, tt, hb, hbk, hsel=None, par=0):
            (uT,) = [PB[par][n_] for n_ in ("uT",)]
            kk = lambda n_: n_ if par == 0 else n_ + "1"
            py, pky = next_ps()

            def mm(e):
                last = None
                for r in range(4):
                    for ri in range(2):
                        rhs = hb[:, r, ri, 0:tt] if hsel is None else hsel(r, ri)
                        last = e.matmul(py[32 * r:32 * r + 32, 0:tt], lhsT=wd_[:, 4 * G + r, ri, :], rhs=rhs,
                                        start=(ri == 0), stop=(ri == 1), tile_position=(0, 32 * r))
                return last

            P.op("pe", mm, reads=["wd_", hbk], writes=[pky])
            P.op("dve", lambda e: e.scalar_tensor_tensor(out=ys5[:, 0:tt], in0=uT[:, G, 0:tt], scalar=s5d[:, G:G + 1], op0=ALU.mult,
                                                         in1=py[:, 0:tt], op1=ALU.add), reads=[pky, kk("uT"), "s5d"], writes=["ys5"])
            P.op("act", lambda e: e.activation(out=yaT[:, G, 0:tt], in_=ys5[:, 0:tt], func=AF.Gelu_apprx_tanh), reads=["ys5"], writes=["yaT"])

        def s5_glu(tt):
            drain(s5_glu_g(tt))

        def s5_glu_g(tt, par=0):
            (yaoT,) = [PB[par][n_] for n_ in ("yaoT",)]
            kk = lambda n_: n_ if par == 0 else n_ + "1"
            def ev(pt, pk, ct):
                P.op("act", lambda e: e.activation(out=ys5[:, 0:tt], in_=pt[:, 0:tt], func=AF.Sigmoid, bias=bglu[:, ct:ct + 1], scale=1.0),
                     reads=[pk, "bglu"], writes=["ys5"])
                P.op("dve", lambda e: e.tensor_tensor(out=yaoT[:, ct, 0:tt], in0=yaT[:, ct, 0:tt], in1=ys5[:, 0:tt], op=ALU.mult),
                     reads=["ys5", "yaT"], writes=[kk("yaoT")])

            yield from linear_fm_g(yaT, "yaT", D, dr["s5_w_glu"], 0, D, tt, ev)


        def ssd_dt(rows, ti, par=0):
            (dtr,) = [PB[par][n_] for n_ in ("dtr",)]
            kk = lambda n_: n_ if par == 0 else n_ + "1"
            P.op("dve", lambda e: e.tensor_tensor(out=sm[0:rows, 7, :], in0=dtr[0:rows, ti, :], in1=dtb[0:rows, :], op=ALU.add),
                 reads=[kk("dtr"), "dtb"], writes=["sm"])
            P.op("act", lambda e: e.activation(out=sm[0:rows, 7, :], in_=sm[0:rows, 7, :], func=AF.Exp), reads=["sm"], writes=["sm"])
            P.op("act", lambda e: e.activation(out=sm[0:rows, 0, :], in_=sm[0:rows, 7, :], func=AF.Ln, bias=1.0, scale=1.0),
                 reads=["sm"], writes=["sm"])
            P.op("dve", lambda e: e.tensor_tensor(out=sm[0:rows, 1, :], in0=sm[0:rows, 0, :], in1=arow[0:rows, :], op=ALU.mult),
                 reads=["sm", "arow"], writes=["sm"])

        def conv_prompt_g(tt, par=0):
            (xbcT,) = [PB[par][n_] for n_ in ("xbcT",)]
            kk = lambda n_: n_ if par == 0 else n_ + "1"
            for ct in range(24):
                t_ = cv[ct % 2]
                tk = "cv%d" % (ct % 2)
                P.op("dve", lambda e, ct=ct, t_=t_: e.tensor_scalar(out=t_[:, 0:tt], in0=xbcT[:, ct, 0:tt], scalar1=cw[:, 0, ct:ct + 1],
                                                                    scalar2=None, op0=ALU.mult), reads=[kk("xbcT"), "cw"], writes=[tk])
                for k in range(1, 4):
                    P.op("dve", lambda e, ct=ct, t_=t_, k=k: e.scalar_tensor_tensor(out=t_[:, 0:tt], in0=xbcT[:, ct, k:k + tt],
                                                                                    scalar=cw[:, k, ct:ct + 1], op0=ALU.mult,
                                                                                    in1=t_[:, 0:tt], op1=ALU.add),
                         reads=[kk("xbcT"), "cw", tk], writes=[tk])
                P.op("act", lambda e, ct=ct, t_=t_: e.activation(out=actT[:, ct, 0:tt], in_=t_[:, 0:tt], func=AF.Silu, bias=cb[:, ct:ct + 1], scale=1.0),
                     reads=[tk, "cb"], writes=["actT"])
                yield

        def gate_norm_out(rows, ti, tok0):
            drain(gate_norm_out_g(rows, ti, tok0))

        def gate_norm_out_g(rows, ti, tok0, par=0):
            zs, yBT = [PB[par][n_] for n_ in ("zs", "yBT",)]
            kk = lambda n_: n_ if par == 0 else n_ + "1"
            P.op("dve", lambda e: e.tensor_tensor(out=ytm[0:rows, :], in0=ytm[0:rows, :], in1=zs[0:rows, ti, :], op=ALU.mult),
                 reads=["ytm", kk("zs")], writes=["ytm"])
            P.op("act", lambda e: e.activation(out=yBtm[0:rows, :], in_=ytm[0:rows, :], func=AF.Square, accum_out=st1[0:rows, 3:4]),
                 reads=["ytm"], writes=["yBtm", "st1b"])
            P.op("act", lambda e: e.activation(out=st1[0:rows, 7:8], in_=st1[0:rows, 3:4], func=AF.Sqrt, scale=1.0 / 2048, bias=EPS),
                 reads=["st1b"], writes=["st1b"])
            P.op("dve", lambda e: e.reciprocal(out=st1[0:rows, 3:4], in_=st1[0:rows, 7:8]), reads=["st1b"], writes=["st1b"])
            P.op("dve", lambda e: e.scalar_tensor_tensor(out=yBtm[0:rows, :], in0=ytm[0:rows, :], scalar=st1[0:rows, 3:4], op0=ALU.mult,
                                                         in1=mnorm[0:rows, :], op1=ALU.mult), reads=["ytm", "st1b", "mnorm"], writes=["yBtm"])
            yield
            yield from transpose_to_g(lambda i0, cnt: yBT[:, i0:i0 + cnt, tok0:tok0 + rows], kk("yBT"),
                                      lambda i: yBtm[0:rows, i * 128:(i + 1) * 128], "yBtm", 16, rows, 128, BF16)

        def ssd_prompt_g(tt, par=0):
            kk = lambda n_: n_ if par == 0 else n_ + "1"
            yield from conv_prompt_g(tt, par)
            for c in range(tt // 128):
                cs_ = slice(c * 128, (c + 1) * 128)
                yield from transpose_to_g(lambda i0, cnt: xtm[:, i0 * 128:(i0 + cnt) * 128].rearrange("p (a r) -> p a r", r=128), "xtm",
                                          lambda i: actT[:, i, cs_], "actT", 16, 128, 128, BF16)
                yield from transpose_to_g(lambda i0, cnt: btm[:, i0:i0 + cnt, :], "btm", lambda i: actT[:, 16 + i, cs_], "actT", 4, 128, 128, BF16)
                ssd_dt(128, c, par)
                yield
                pt, pk = next_ps()
                P.op("pe", lambda e, pt=pt: e.matmul(pt[:, 0:32], lhsT=tri[:], rhs=sm[:, 1, :], start=True, stop=True),
                     reads=["tri", "sm"], writes=[pk])
                P.op("pe", lambda e, pt=pt: e.matmul(pt[:, 32:64], lhsT=ones[:], rhs=sm[:, 1, :], start=True, stop=True),
                     reads=["ones", "sm", pk], writes=[pk])
                P.op("act", lambda e, pt=pt: e.copy(out=sm[:, 2:4, :], in_=pt[:, 0:64].rearrange("p (a b) -> p a b", b=32)),
                     reads=[pk], writes=["sm"])
                P.op("act", lambda e: e.activation(out=sm[:, 4, :], in_=sm[:, 2, :], func=AF.Exp), reads=["sm"], writes=["sm"])
                P.op("act", lambda e: e.activation(out=sm[:, 6, :], in_=sm[:, 3, :], func=AF.Exp), reads=["sm"], writes=["sm"])
                P.op("dve", lambda e: e.tensor_tensor(out=sm[:, 7, :], in0=sm[:, 3, :], in1=sm[:, 2, :], op=ALU.subtract), reads=["sm"], writes=["sm"])
                P.op("act", lambda e: e.activation(out=sm[:, 7, :], in_=sm[:, 7, :], func=AF.Exp), reads=["sm"], writes=["sm"])
                P.op("dve", lambda e: e.tensor_tensor(out=sm[:, 5, :], in0=sm[:, 7, :], in1=sm[:, 0, :], op=ALU.mult), reads=["sm"], writes=["sm"])
                yield
                for g in range(4):
                    hs = slice(8 * g, 8 * g + 8)
                    P.op("dve", lambda e, hs=hs: e.tensor_tensor(out=Rb[:], in0=tri[:].unsqueeze(1).to_broadcast([128, 8, 128]),
                                                                 in1=sm[:, 1, hs].unsqueeze(2).to_broadcast([128, 8, 128]), op=ALU.mult),
                         reads=["tri", "sm"], writes=["Rb"])
                    for hh in range(2):
                        pa, pka = next_ps()
                        P.op("pe", lambda e, pa=pa, hh=hh: e.matmul(pa[:, :], lhsT=su[:], rhs=Rb[:, 4 * hh:4 * hh + 4, :].rearrange("p a b -> p (a b)"),
                                                                    start=True, stop=True), reads=["su", "Rb"], writes=[pka])
                        P.op("act", lambda e, pa=pa, hh=hh: e.activation(out=LT[:, 4 * hh:4 * hh + 4, :].rearrange("p a b -> p (a b)"),
                                                                         in_=pa[:, :], func=AF.Exp), reads=[pka], writes=["LT"])
                    pc, pkc = next_ps()
                    P.op("pe", lambda e, pc=pc, g=g: e.matmul(pc[:, 0:128], lhsT=actT[:, 16 + g, cs_], rhs=actT[:, 20 + g, cs_], start=True, stop=True),
                         reads=["actT"], writes=[pkc])
                    P.op("dve", lambda e, pc=pc: e.tensor_tensor(out=CBm[:], in0=pc[:, 0:128], in1=tri[:], op=ALU.mult), reads=[pkc, "tri"], writes=["CBm"])
                    P.op("dve", lambda e: e.tensor_tensor(out=MT[:], in0=LT[:], in1=CBm[:].unsqueeze(1).to_broadcast([128, 8, 128]), op=ALU.mult),
                         reads=["LT", "CBm"], writes=["MT"])
                    xg = xtm[:, 512 * g:512 * g + 512].rearrange("p (j d) -> p j d", d=64)
                    P.op("dve", lambda e, hs=hs, xg=xg: e.tensor_tensor(out=xdt[:].rearrange("p (j d) -> p j d", d=64), in0=xg,
                                                                        in1=sm[:, 0, hs].unsqueeze(2).to_broadcast([128, 8, 64]), op=ALU.mult),
                         reads=["xtm", "sm"], writes=["xdt"])
                    P.op("dve", lambda e, hs=hs, xg=xg: e.tensor_tensor(out=X2[:].rearrange("p (j d) -> p j d", d=64), in0=xg,
                                                                        in1=sm[:, 5, hs].unsqueeze(2).to_broadcast([128, 8, 64]), op=ALU.mult),
                         reads=["xtm", "sm"], writes=["X2"])
                    pyd, pkyd = next_ps()

                    def ydm(e, pyd=pyd, g=g):
                        last = None
                        for j in range(8):
                            last = e.matmul(pyd[:, 64 * j:64 * j + 64], lhsT=MT[:, j, :], rhs=xdt[:, 64 * j:64 * j + 64], start=True, stop=True)
                        return last

                    P.op("pe", ydm, reads=["MT", "xdt"], writes=[pkyd])
                    pyo, pkyo = next_ps()
                    P.op("pe", lambda e, pyo=pyo, g=g: e.matmul(pyo[:, :], lhsT=actT[:, 20 + g, cs_], rhs=hTb[:, g, :], start=True, stop=True),
                         reads=["actT", "hTb"], writes=[pkyo])
                    P.op("dve", lambda e, pyo=pyo, hs=hs: e.tensor_tensor(out=yt1[:].rearrange("p (j d) -> p j d", d=64),
                                                                          in0=pyo[:, :].rearrange("p (j d) -> p j d", d=64),
                                                                          in1=sm[:, 4, hs].unsqueeze(2).to_broadcast([128, 8, 64]), op=ALU.mult),
                         reads=[pkyo, "sm"], writes=["yt1"])
                    P.op("dve", lambda e, pyd=pyd, g=g: e.tensor_tensor(out=ytm[:, 512 * g:512 * g + 512], in0=yt1[:], in1=pyd[:, :], op=ALU.add),
                         reads=[pkyd, "yt1"], writes=["ytm"])
                    P.op("dve", lambda e, hs=hs, xg=xg: e.tensor_tensor(out=yt1[:].rearrange("p (j d) -> p j d", d=64), in0=xg,
                                                                        in1=mdr[:, hs].unsqueeze(2).to_broadcast([128, 8, 64]), op=ALU.mult),
                         reads=["xtm", "mdr", "ytm"], writes=["yt1"])
                    P.op("dve", lambda e, g=g: e.tensor_tensor(out=ytm[:, 512 * g:512 * g + 512], in0=ytm[:, 512 * g:512 * g + 512], in1=yt1[:], op=ALU.add),
                         reads=["yt1", "ytm"], writes=["ytm"])
                    pst, pkst = next_ps()
                    P.op("pe", lambda e, pst=pst, g=g: e.matmul(pst[:, :], lhsT=btm[:, g, :], rhs=X2[:], start=True, stop=True),
                         reads=["btm", "X2"], writes=[pkst])
                    hv = hT[:, g, :].rearrange("p (j d) -> p j d", d=64)
                    P.op("dve", lambda e, hv=hv, hs=hs: e.tensor_tensor(out=hv, in0=hv, in1=sm[:, 6, hs].unsqueeze(2).to_broadcast([128, 8, 64]), op=ALU.mult),
                         reads=["hT", "sm"], writes=["hT"])
                    P.op("dve", lambda e, pst=pst, g=g: e.tensor_tensor(out=hT[:, g, :], in0=hT[:, g, :], in1=pst[:, :], op=ALU.add),
                         reads=["hT", pkst], writes=["hT"])
                    P.op("act", lambda e, g=g: e.copy(out=hTb[:, g, :], in_=hT[:, g, :]), reads=["hT"], writes=["hTb"])
                    yield
                yield from gate_norm_out_g(128, c, c * 128, par)

        def in_proj(*a, **k):
            drain(in_proj_g(*a, **k))

        def in_proj_g(tt, tiles, want_xbc_tm, par=0, prompt=False):
            uT, zs, xbcT, dtr, gaT, gbT = [PB[par][n_] for n_ in ("uT", "zs", "xbcT", "dtr", "gaT", "gbT")]
            kk = lambda n_: n_ if par == 0 else n_ + "1"
            def ev_u(pt, pk, ct):
                copy_evac(uT[:, ct, 0:tt], kk("uT"), pt[:, 0:tt], pk)

            yield from linear_fm_g(hT_, "hT_", D, dr["w_in"], 0, 1024, tt, ev_u)

            def ev_z(pt, pk, ti, cc, cwid):
                rows = tiles[ti][1]
                P.op("act", lambda e: e.activation(out=zs[0:rows, ti, cc:cc + cwid], in_=pt[0:rows, 0:cwid], func=AF.Silu), reads=[pk], writes=[kk("zs")])

            yield from linear_tm_g(hT_, "hT_", D, dr["w_in"], OFF_Z, 2048, tiles, ev_z)

            def ev_x(pt, pk, ct):
                copy_evac(xbcT[:, ct, 3:3 + tt], kk("xbcT"), pt[:, 0:tt], pk)
                if want_xbc_tm:
                    P.op("dve", lambda e: e.tensor_copy(out=xtail[:, ct, 0:3], in_=pt[:, tt - 3:tt]), reads=[pk, kk("xbcT")], writes=["xtail"])
                if "xs32" in SAMPLE:
                    P.op("dve", lambda e: e.tensor_copy(out=SAMPLE["xs32"][:, ct, :], in_=pt[:, 0:tt]), reads=[pk, kk("xbcT")], writes=["xs32"])

            yield from linear_fm_g(hT_, "hT_", D, dr["w_in"], OFF_XBC, 3072, tt, ev_x)
            if prompt:
                ox = PB[1 - par]["xbcT"]
                okey = "xbcT" if par == 1 else "xbcT1"
                P.op("act", lambda e: e.copy(out=xbcT[:, :, 0:3], in_=ox[:, :, tt:tt + 3]), reads=[okey], writes=[kk("xbcT")])
            def ev_dt(pt, pk, ti, cc, cwid):
                rows = tiles[ti][1]
                copy_evac(dtr[0:rows, ti, :], kk("dtr"), pt[0:rows, 0:32], pk)

            yield from linear_tm_g(hT_, "hT_", D, dr["w_in"], OFF_DT, 32, tiles, ev_dt)

            def ev_g(dst, dk):
                flat = dst[:].rearrange("p a b -> p (a b)")

                def f(pt, pk, ti, cc, cwid):
                    rows = tiles[ti][1]
                    P.op("act", lambda e: e.activation(out=flat[0:rows, cc:cc + cwid], in_=pt[0:rows, 0:cwid], func=AF.Sigmoid), reads=[pk], writes=[dk])
                return f

            yield from linear_tm_g(hT_, "hT_", D, dr["w_in"], OFF_GA, 1024, tiles, ev_g(gaT, kk("gaT")))
            yield from linear_tm_g(hT_, "hT_", D, dr["w_in"], OFF_GB, 1024, tiles, ev_g(gbT, kk("gbT")))

        def merge_ffn(*a, **k):
            drain(merge_ffn_g(*a, **k))

        def merge_ffn_g(tt, tiles, mt, mkey, y_dram, par=0, x_src=None, prompt=False):
            jG1, jG2 = (0, 1) if prompt else (2, 5)
            yaoT, yBT, gaT, gbT = [PB[par][n_] for n_ in ("yaoT", "yBT", "gaT", "gbT")]
            kk = lambda n_: n_ if par == 0 else n_ + "1"
            if x_src is not None:
                for ti, (t0, rows) in enumerate(tiles):
                    P.dma("pool", x_tm[0:rows, ti, :], x_src[t0:t0 + rows, :], "xin", reads=["yout"], writes=["x_tm"])
            ga_f = gaT[:].rearrange("p a b -> p (a b)")
            gb_f = gbT[:].rearrange("p a b -> p (a b)")
            mrg_f = mrg[:].rearrange("p a b -> p (a b)")
            mrgT_f = mrgT[:].rearrange("p a b -> p (a b)")

            def ev_a(pt, pk, ti, cc, cwid):
                rows = tiles[ti][1]
                P.op("dve", lambda e: e.tensor_tensor(out=mrg_f[0:rows, cc:cc + cwid], in0=pt[0:rows, 0:cwid], in1=ga_f[0:rows, cc:cc + cwid], op=ALU.mult),
                     reads=[pk, kk("gaT")], writes=["mrg"])

            yield from linear_tm_g(yaoT, kk("yaoT"), D, dr["w_branch_s5"], 0, D, tiles, ev_a)

            def ev_b(pt, pk, ti, cc, cwid):
                rows = tiles[ti][1]
                P.op("dve", lambda e: e.tensor_tensor(out=wk2[0:rows, 0:cwid], in0=pt[0:rows, 0:cwid], in1=gb_f[0:rows, cc:cc + cwid], op=ALU.mult),
                     reads=[pk, kk("gbT")], writes=["wk2"])
                P.op("dve", lambda e: e.tensor_tensor(out=mrgT_f[0:rows, cc:cc + cwid], in0=wk2[0:rows, 0:cwid], in1=mrg_f[0:rows, cc:cc + cwid], op=ALU.add),
                     reads=["wk2", "mrg"], writes=["mrgT"])

            yield from linear_tm_g(yBT, kk("yBT"), 2048, dr["w_branch_ssd"], 0, D, tiles, ev_b)
            for ti, (t0, rows) in enumerate(tiles):
                yield from transpose_to_g(lambda i0, cnt, t0=t0, rows=rows: mrg[:, i0:i0 + cnt, t0:t0 + rows], "mrg",
                                          lambda i, rows=rows: mrgT_f[0:rows, i * 128:(i + 1) * 128], "mrgT", 8, rows, 128, BF16)
            yield from linear_tm_seq_g(mrg, "mrg", D, dr["w_out"], D, tiles, jG1, mt, mkey)
            for ti, (t0, rows) in enumerate(tiles):
                if prompt:
                    rms_mod_p(4, 3)
                else:
                    rms_mod(ti, rows, mt, 4, 3, mkey, t0)

            def ev_gate(pt, pk, ct):
                if ct < 22:
                    P.op("act", lambda e: e.activation(out=factT[:, ct, 0:tt], in_=pt[:, 0:tt], func=AF.Silu), reads=[pk], writes=["factT"])
                else:
                    P.op("dve", lambda e: e.tensor_tensor(out=factT[:, ct - 22, 0:tt], in0=pt[:, 0:tt], in1=factT[:, ct - 22, 0:tt], op=ALU.mult),
                         reads=[pk, "factT"], writes=["factT"])

            yield from linear_fm_g(hT_, "hT_", D, dr["w_ffn_in"], 0, 2 * D_FF, tt, ev_gate)
            yield from linear_tm_seq_g(factT, "factT", D_FF, dr["w_ffn_out"], D, tiles, jG2, mt, mkey)
            for ti, (t0, rows) in enumerate(tiles):
                P.dma("pool", y_dram[t0:t0 + rows, :], x_tm[0:rows, ti, :], "yout", reads=["x_tm"], writes=["yout"])

        wk2b = [wk2]

        def linear_tm_seq_g(inT, in_key, K, wd, ncols, tiles, jG, mt, mkey):
            def ev(pt, pk, ti, cc, cwid):
                rows = tiles[ti][1]
                copy_evac(wk2b[ti][0:rows, cc:cc + cwid], "wk2", pt[0:rows, 0:cwid], pk)

            yield from linear_tm_g(inT, in_key, K, wd, 0, ncols, tiles, ev)
            for ti, (t0, rows) in enumerate(tiles):
                resid_gate(wk2b[ti][0:rows, :], "wk2", ti, rows, mt, jG, mkey)

        tiles_p = [(0, 128)]

        def dense_in_g(si):
            P.dma("pool", x_tm[:, 0, :], dr["xp"][si * T:(si + 1) * T, :], "xin", reads=["yout"], writes=["x_tm"])
            rms_mod_p(1, 0)
            yield
            yield from in_proj_g(T, tiles_p, want_xbc_tm=(si == NST_RUN - 1), par=si % 2, prompt=True)

        def dense_out_g(si):
            yield from merge_ffn_g(T, tiles_p, modp, "modp", dr["y_p"][si * T:(si + 1) * T, :], par=si % 2,
                                   x_src=dr["xp"][si * T:(si + 1) * T, :], prompt=True)

        def chain(*gs):
            for g in gs:
                yield from g

        drain(dense_in_g(0))
        for si in range(NST_RUN):
            dg_ = []
            if si >= 1:
                dg_.append(dense_out_g(si - 1))
            if si + 1 < NST_RUN:
                dg_.append(dense_in_g(si + 1))
            interleave([chain(*dg_), s5_prompt_g(T, si % 2), ssd_prompt_g(T, si % 2)], ILW)
        drain(dense_out_g(NST_RUN - 1))
        if STOP == "p_1":
            return finish()
        if STOP == "pb1":
            for n_ in ("yaoT", "gaT", "gbT", "uT"):
                dump(n_, PB[1][n_][:].rearrange("p a b -> p (a b)"), n_ + "1", [128, 8 * T])
            dump("yBT", PB[1]["yBT"][:].rearrange("p a b -> p (a b)"), "yBT1", [128, 16 * T])
            dump("zs", PB[1]["zs"][:, 0, :], "zs1", [128, 2048])
            return finish()

        for ri, nm in enumerate(("s5re_p", "s5im_p")):
            pt, pk = next_ps()
            P.op("pe", lambda e, pt=pt, ri=ri: e.transpose(pt[0:32, 0:128], s5c[:, :, ri], ident[:]), reads=["s5c", "ident"], writes=[pk])
            P.op("act", lambda e, pt=pt, ri=ri: e.copy(out=wk2[0:32, ri * 128:(ri + 1) * 128], in_=pt[0:32, 0:128]), reads=[pk], writes=["wk2"])
            P.dma("sp", dr[nm], wk2[0:32, ri * 128:(ri + 1) * 128], "o_" + nm, reads=["wk2"], writes=["o_" + nm])
        if STOP == "e1":
            return finish()
        hout = ytm[:, :].rearrange("p (a b) -> p a b", b=128)
        transpose_to(lambda i0, cnt: hout[:, i0:i0 + cnt, :], "ytm",
                     lambda i: hT[:, i // 4, (i % 4) * 128:(i % 4 + 1) * 128], "hT", 16, 128, 128, F32)
        P.dma("sp", dr["ssm_p"].rearrange("(a p) n -> p a n", p=128), hout, "o_ssm_p", reads=["ytm"], writes=["o_ssm_p"])
        if STOP == "e2":
            return finish()
        for i0 in range(0, 24, 4):
            pt, pk = next_ps()

            def trx(e, pt=pt, i0=i0):
                last = None
                for a in range(4):
                    last = e.transpose(pt[0:3, a * 128:(a + 1) * 128], xtail[:, i0 + a, 0:3], ident[:])
                return last

            P.op("pe", trx, reads=["xtail", "ident"], writes=[pk])
            P.op("act", lambda e: e.copy(out=cps[0:3, :], in_=pt[0:3, :]), reads=[pk], writes=["yt1"])
            P.dma("sp", dr["conv_p"][:, i0 * 128:(i0 + 4) * 128], cps[0:3, :], "o_conv_p", reads=["yt1"], writes=["o_conv_p"])
        if STOP == "prompt_only":
            return finish()
        P.emit_phase()
        tiles_s = [(0, NS)]
        bump["lo"] = M1
        bump["hi"] = ARW
        mods = ssb("mods", [NS, 6, D], BF16)
        xs32 = ssb("xs32", [128, 24, NS], F32)
        ada_compute(sb, False, mods)
        P.emit_phase()
        bump["lo"] = M1
        P.dma("pool", x_tm[0:NS, 0, :], dr["xs"], "xin", writes=["x_tm"])
        rms_mod(0, NS, mods, 1, 0, "mods", 0)
        SAMPLE["xs32"] = xs32
        in_proj(NS, tiles_s, want_xbc_tm=False)
        P.emit_phase()
        bump["lo"] = M1
        hst = cosT[:].rearrange("p a b -> p (a b)").rearrange("p (a b) -> p a b", b=128)
        tmp3 = sinT[:].rearrange("p a b -> p (a b)").rearrange("p (a b) -> p a b", b=128)
        stg = sb("stg", [48, 4096], F32)
        h0T = sb("h0T", [128, 2, 32, NS], F32)
        for ri, nm in enumerate(("s5re_in", "s5im_in")):
            P.dma("sp", stg[0:NS, :], dr[nm], "stg", writes=["stg"])
            transpose_to(lambda i0, cnt, ri=ri: h0T[:, ri, i0:i0 + cnt, :], "h0T",
                         lambda i: stg[0:NS, i * 128:(i + 1) * 128], "stg", 32, NS, 128, F32)
        pbr, pkbr = next_ps()
        pbi, pkbi = next_ps()

        for G in range(8):
            s5_load_u(G, NS)

            def bus(e, G=G):
                last = None
                for r in range(4):
                    q = 4 * G + r
                    e.matmul(pbr[:, q * NS:(q + 1) * NS], lhsT=wb[:, G, 0, :], rhs=uz[:, r, 0:NS], start=True, stop=True)
                    last = e.matmul(pbi[:, q * NS:(q + 1) * NS], lhsT=wb[:, G, 1, :], rhs=uz[:, r, 0:NS], start=True, stop=True)
                return last

            P.op("pe", bus, reads=["wb", "uz", pkbr, pkbi], writes=[pkbr, pkbi])
        hn5 = sb("hn5", [128, 2, 32, NS], F32)
        t5 = sb("t5", [128, 2, 32, NS], F32)
        arb = s5t[:, AR, :].unsqueeze(2).to_broadcast([128, 32, NS])
        aib = s5t[:, AI, :].unsqueeze(2).to_broadcast([128, 32, NS])
        K5 = ["h0T", "s5t", "t5", "hn5", pkbr, pkbi]
        pv5 = lambda p_: p_[:, 0:32 * NS].rearrange("p (q b) -> p q b", b=NS)
        P.op("dve", lambda e: e.tensor_tensor(out=t5[:, 0], in0=h0T[:, 0], in1=arb, op=ALU.mult), reads=K5, writes=["t5"])
        P.op("dve", lambda e: e.tensor_tensor(out=t5[:, 1], in0=h0T[:, 1], in1=aib, op=ALU.mult), reads=K5, writes=["t5"])
        P.op("dve", lambda e: e.tensor_tensor(out=t5[:, 0], in0=t5[:, 0], in1=t5[:, 1], op=ALU.subtract), reads=K5, writes=["t5"])
        P.op("dve", lambda e: e.tensor_tensor(out=hn5[:, 0], in0=t5[:, 0], in1=pv5(pbr), op=ALU.add), reads=K5, writes=["hn5"])
        P.op("dve", lambda e: e.tensor_tensor(out=t5[:, 0], in0=h0T[:, 1], in1=arb, op=ALU.mult), reads=K5, writes=["t5"])
        P.op("dve", lambda e: e.tensor_tensor(out=t5[:, 1], in0=h0T[:, 0], in1=aib, op=ALU.mult), reads=K5, writes=["t5"])
        P.op("dve", lambda e: e.tensor_tensor(out=t5[:, 0], in0=t5[:, 0], in1=t5[:, 1], op=ALU.add), reads=K5, writes=["t5"])
        P.op("dve", lambda e: e.tensor_tensor(out=hn5[:, 1], in0=t5[:, 0], in1=pv5(pbi), op=ALU.add), reads=K5, writes=["hn5"])
        hb5 = sb("hb5", [128, 2, 32, NS], BF16)
        P.op("act", lambda e: e.copy(out=hb5[:].rearrange("p a b c -> p (a b c)"), in_=hn5[:].rearrange("p a b c -> p (a b c)")),
             reads=["hn5"], writes=["hb5"])
        for G in range(8):
            s5_readout(G, NS, None, "hb5", hsel=lambda r, ri, G=G: hb5[:, ri, 4 * G + r, :])
        s5_glu(NS)
        for ri, nm in enumerate(("s5re_s", "s5im_s")):
            transpose_to(lambda i0, cnt: stg[0:NS, i0 * 128:(i0 + cnt) * 128].rearrange("p (a r) -> p a r", r=128), "stg",
                         lambda i, ri=ri: hn5[:, ri, i, :], "hn5", 32, 128, NS, F32)
            P.dma("sp", dr[nm], stg[0:NS, :], "o_" + nm, reads=["stg"], writes=["o_" + nm, "stg"])

        histT = sb("histT", [128, 24, NS * 3], F32)
        P.dma("sp", stg[0:NS * 3, 0:3072], dr["conv_in"], "stg", writes=["stg"])
        transpose_to(lambda i0, cnt: histT[:, i0:i0 + cnt, :], "histT", lambda i: stg[0:NS * 3, i * 128:(i + 1) * 128], "stg",
                     24, NS * 3, 128, F32)
        actS = sb("actS", [128, 24, NS], F32)
        for ct in range(24):
            t_ = cv[ct % 2]
            tk = "cv%d" % (ct % 2)
            hv_ = histT[:, ct, :].rearrange("p (b k) -> p b k", k=3)
            P.op("dve", lambda e, ct=ct, t_=t_: e.tensor_scalar(out=t_[:, 0:NS], in0=xs32[:, ct, :], scalar1=cw[:, 3, ct:ct + 1], scalar2=None,
                                                                op0=ALU.mult), reads=["xs32", "cw"], writes=[tk])
            for k in range(3):
                P.op("dve", lambda e, ct=ct, t_=t_, k=k, hv_=hv_: e.scalar_tensor_tensor(out=t_[:, 0:NS], in0=hv_[:, :, k], scalar=cw[:, k, ct:ct + 1],
                                                                                         op0=ALU.mult, in1=t_[:, 0:NS], op1=ALU.add),
                     reads=["histT", "cw", tk], writes=[tk])
            P.op("act", lambda e, ct=ct, t_=t_: e.activation(out=actS[:, ct, :], in_=t_[:, 0:NS], func=AF.Silu, bias=cb[:, ct:ct + 1], scale=1.0),
                 reads=[tk, "cb"], writes=["actS"])
        P.dma("sp", dr["conv_s"][:, 0:2, :], dr["conv_in"].rearrange("(b k) c -> b k c", k=3)[:, 1:3, :], "o_conv_s", writes=["o_conv_s"])
        transpose_to(lambda i0, cnt: stg[0:NS, i0 * 128:(i0 + cnt) * 128].rearrange("p (a r) -> p a r", r=128), "stg",
                     lambda i: xs32[:, i, :], "xs32", 24, 128, NS, F32)
        P.dma("sp", dr["conv_s"][:, 2, :], stg[0:NS, 0:3072], "o_conv_s2", reads=["stg"], writes=["o_conv_s2", "stg"])

        ssd_dt(NS, 0)
        P.op("act", lambda e: e.activation(out=sm[0:NS, 2, :], in_=sm[0:NS, 1, :], func=AF.Exp), reads=["sm"], writes=["sm"])
        dfm = sb("dfm", [32, 2, NS], F32)
        for k, col in enumerate((0, 2)):
            pt, pk = next_ps()
            P.op("pe", lambda e, pt=pt, col=col: e.transpose(pt[0:32, 0:NS], sm[0:NS, col, :], ident[0:NS, 0:NS]), reads=["sm", "ident"], writes=[pk])
            P.op("act", lambda e, pt=pt, k=k: e.copy(out=dfm[:, k, :], in_=pt[0:32, 0:NS]), reads=[pk], writes=["dfm"])
        esel = [sb("esel%d" % i, [32, 128], F32) for i in range(2)]
        dex = sb("dex", [128, 16, 2, NS], F32)
        dexp = sb("dexp", [128, 16], F32)
        for hl in range(2):
            P.dma("sp", dexp[64 * hl:64 * hl + 64, :], dr["m_d"][0, :].rearrange("(hp hl) -> hl hp", hl=2)[hl, :].partition_broadcast(64),
                  "dexp", writes=["dexp"])
        for hp in range(16):
            es = esel[hp % 2]
            ek = "esel%d" % (hp % 2)
            P.op("dve", lambda e, es=es, hp=hp: e.tensor_copy(out=es[:].rearrange("h (b c) -> h b c", c=64),
                                                              in_=ident[0:32, 2 * hp:2 * hp + 2].unsqueeze(2).to_broadcast([32, 2, 64])),
                 reads=["ident"], writes=[ek])
            pt, pk = next_ps()
            P.op("pe", lambda e, pt=pt, es=es: e.matmul(pt[:, 0:2 * NS], lhsT=es[:], rhs=dfm[:].rearrange("h a b -> h (a b)"), start=True, stop=True),
                 reads=[ek, "dfm"], writes=[pk])
            copy_evac(dex[:, hp, :, :], "dex", pt[:, 0:2 * NS].rearrange("p (a b) -> p a b", b=NS), pk)
        dtx = sb("dtx", [128, 16, NS], F32)
        P.op("dve", lambda e: e.tensor_tensor(out=dtx[:], in0=dex[:, :, 0, :], in1=actS[:, 0:16, :], op=ALU.mult), reads=["dex", "actS"], writes=["dtx"])
        ysT = sb("ysT", [128, 16, NS], F32)
        dg = sb("dg", [128, 8, 128], F32)
        bcb = sb("bcb", [128, 8, 128], F32)
        red = sb("red", [128, 16], F32)
        print("sample arena end=%d of %d" % (bump["lo"], ARW))
        for b in range(NS):
            hs_ = hst
            hk = "hst"
            P.dma("sp", hs_, dr["ssm_in"][b].rearrange("(hp hl) p n -> (hl p) hp n", hl=2), hk, reads=["o_hst"], writes=[hk])
            for k in range(8):
                P.op("pool", lambda e, k=k, b=b: e.tensor_scalar(out=dg[:, k, :], in0=ident[:], scalar1=actS[:, 16 + k, b:b + 1], scalar2=None, op0=ALU.mult),
                     reads=["ident", "actS"], writes=["dg"])
            for hh in range(2):
                pt, pk = next_ps()
                P.op("pe", lambda e, pt=pt, hh=hh: e.matmul(pt[:, :], lhsT=ones[:], rhs=dg[:, 4 * hh:4 * hh + 4, :].rearrange("p a b -> p (a b)"),
                                                            start=True, stop=True), reads=["ones", "dg"], writes=[pk])
                copy_evac(bcb[:, 4 * hh:4 * hh + 4, :].rearrange("p a b -> p (a b)"), "bcb", pt[:, :], pk)
            P.op("dve", lambda e, b=b: e.tensor_tensor(out=hs_, in0=hs_, in1=dex[:, :, 1, b:b + 1].to_broadcast([128, 16, 128]), op=ALU.mult),
                 reads=[hk, "dex"], writes=[hk])
            P.op("pool", lambda e, b=b: e.tensor_tensor(out=tmp3.rearrange("p (g r) n -> p g r n", r=4),
                                                       in0=bcb[:, 0:4, :].unsqueeze(2).to_broadcast([128, 4, 4, 128]),
                                                       in1=dtx[:, :, b:b + 1].rearrange("p (g r) o -> p g r o", r=4).to_broadcast([128, 4, 4, 128]), op=ALU.mult),
                 reads=["bcb", "dtx"], writes=["tmp3"])
            P.op("dve", lambda e: e.tensor_tensor(out=hs_, in0=hs_, in1=tmp3, op=ALU.add), reads=[hk, "tmp3"], writes=[hk])
            P.dma("sp", dr["ssm_s"][b].rearrange("(hp hl) p n -> (hl p) hp n", hl=2), hs_, "o_hst", reads=[hk], writes=["o_hst"])
            P.op("pool", lambda e: e.tensor_tensor(out=tmp3.rearrange("p (g r) n -> p g r n", r=4),
                                                  in0=hs_.rearrange("p (g r) n -> p g r n", r=4),
                                                  in1=bcb[:, 4:8, :].unsqueeze(2).to_broadcast([128, 4, 4, 128]), op=ALU.mult),
                 reads=[hk, "bcb"], writes=["tmp3"])
            P.op("dve", lambda e: e.tensor_reduce(out=red[:], in_=tmp3, axis=AX.X, op=ALU.add), reads=["tmp3"], writes=["red"])
            P.op("dve", lambda e, b=b: e.tensor_tensor(out=ysT[:, :, b], in0=dexp[:], in1=actS[:, 0:16, b], op=ALU.mult), reads=["dexp", "actS"], writes=["ysT"])
            P.op("dve", lambda e, b=b: e.tensor_tensor(out=ysT[:, :, b], in0=ysT[:, :, b], in1=red[:], op=ALU.add), reads=["red", "ysT"], writes=["ysT"])
        transpose_to(lambda i0, cnt: ytm[0:NS, i0 * 128:(i0 + cnt) * 128].rearrange("p (a r) -> p a r", r=128), "ytm",
                     lambda i: ysT[:, i, :], "ysT", 16, 128, NS, F32)
        gate_norm_out(NS, 0, 0)
        if STOP == "s_mix":
            dump("yaoT", yaoT[:].rearrange("p a b -> p (a b)"), "yaoT", [128, 8 * T])
            dump("yBT", yBT[:].rearrange("p a b -> p (a b)"), "yBT", [128, 16 * T])
            return finish()
        P.emit_phase()
        merge_ffn(NS, tiles_s, mods, "mods", dr["y_s"])

        P.final_wait("sp", [k for k in P.lastw if str(k).startswith(("o_", "yout", "dbg_"))])
        P.emit()
    return nc, dbg_out


_NC_CACHE = {}


def _prep_inputs(inputs):
    f = lambda a: np.ascontiguousarray(np.asarray(a, dtype=np.float32))
    w = {}
    for n, s in WEIGHT_SHAPES.items():
        w[n] = f(inputs[n]).reshape(s)
    maps = []
    for i in range(NCORES):
        m = dict(w)
        sl = slice(NS * i, NS * (i + 1))
        m["xp"] = f(inputs["x_prompt"][i])
        m["xs"] = f(inputs["x_sample"][sl, 0, :])
        m["c17"] = f(np.concatenate([np.asarray(inputs["c_sample"])[sl], np.asarray(inputs["c_prompt"])[i:i + 1]], axis=0))
        m["s5re_in"] = f(np.asarray(inputs["state_s5_re"])[0, sl].reshape(NS, 4096))
        m["s5im_in"] = f(np.asarray(inputs["state_s5_im"])[0, sl].reshape(NS, 4096))
        m["ssm_in"] = f(np.asarray(inputs["state_ssm"])[0, sl])
        m["conv_in"] = f(np.asarray(inputs["state_conv"])[0, sl].reshape(NS * 3, 3072))
        maps.append(m)
    return maps


def kernel(**inputs):
    if "nc" not in _NC_CACHE:
        _NC_CACHE["nc"] = build_nc(tuple(DEBUG.get("names", ())))
    nc, dbg = _NC_CACHE["nc"]
    maps = _prep_inputs(inputs)
    res = run_bass_kernel_spmd(nc, maps, core_ids=list(range(NCORES)))
    R = res.results
    if DEBUG.get("names"):
        DEBUG["out"] = [{k: r["dbg_" + k] for k in dbg} for r in R]
    cat = lambda n: np.stack([np.asarray(R[i][n]) for i in range(NCORES)], 0)
    y_p = cat("y_p").reshape(8, SEQ, D)
    y_s = cat("y_s").reshape(128, 1, D)
    s5re_p = cat("s5re_p").reshape(1, 8, 64, 64)
    s5im_p = cat("s5im_p").reshape(1, 8, 64, 64)
    ssm_p = cat("ssm_p").reshape(1, 8, 32, 64, 128)
    conv_p = cat("conv_p").reshape(1, 8, 3, 3072)
    s5re_s = cat("s5re_s").reshape(1, 128, 64, 64)
    s5im_s = cat("s5im_s").reshape(1, 128, 64, 64)
    ssm_s = cat("ssm_s").reshape(1, 128, 32, 64, 128)
    conv_s = cat("conv_s").reshape(1, 128, 3, 3072)
    return tuple(np.ascontiguousarray(a, dtype=np.float32) for a in
                 (y_p, y_s, s5re_p, s5im_p, ssm_p, conv_p, s5re_s, s5im_s, ssm_s, conv_s))
```

```python
import math
import numpy as np
from contextlib import ExitStack
import concourse.bass as bass
import concourse.mybir as mybir
from concourse.bass_utils import run_bass_kernel_spmd

F32 = mybir.dt.float32
BF16 = mybir.dt.bfloat16
AF = mybir.ActivationFunctionType
ALU = mybir.AluOpType
AX = mybir.AxisListType

ENGS = ("pe", "act", "dve", "pool", "sp")
NCORES = 8
D = 1024
SEQ = 2048
NS = 16
T = 128
NST = SEQ // T
D_FF = 2816
IN_COLS = 8224
OFF_Z, OFF_XBC, OFF_DT, OFF_GA, OFF_GB = 1024, 3072, 6144, 6176, 7200
EPS = 1e-6
TWO_PI = 2.0 * math.pi
DEBUG = {}
SAMPLE = {}
STOP = None
NST_RUN = NST
ILW = [1, 2, 1]
PROBE_WB = False


class Prog:
    def __init__(self, nc, stack):
        self.nc = nc
        self.stack = stack
        self.streams = {e: [] for e in ENGS}
        self.cnt = {e: 0 for e in ENGS}
        self.sems = {e: stack.enter_context(nc.semaphore("s_" + e)) for e in ENGS}
        self.waited = {e: {} for e in ENGS}
        self.lastw = {}
        self.readers = {}
        self.chan_sem = {}
        self.chan_cnt = {}

    def _deps(self, reads, writes):
        ev = []
        for r in reads:
            if r in self.lastw:
                ev.append(self.lastw[r])
        for w in writes:
            if w in self.lastw:
                ev.append(self.lastw[w])
            ev.extend(self.readers.get(w, ()))
        return ev

    def _commit(self, reads, writes, event):
        for w in writes:
            self.lastw[w] = event
            self.readers[w] = []
        for r in reads:
            if r in writes:
                continue
            self.readers.setdefault(r, []).append(event)

    def _emit_waits(self, eng, deps):
        need = {}
        for (src, val) in deps:
            if val > need.get(src, 0):
                need[src] = val
        out = []
        for src, val in need.items():
            if self.waited[eng].get(src, 0) >= val:
                continue
            self.waited[eng][src] = val
            sem = self.sems[src] if src in self.sems else self.chan_sem[src]
            out.append((sem, val))
        return out

    def op(self, eng, fn, reads=(), writes=()):
        reads = list(reads)
        writes = list(writes)
        deps = self._deps(reads, writes)
        waits = self._emit_waits(eng, deps)
        self.cnt[eng] += 1
        sem = self.sems[eng]
        calls = []

        class _Rec:
            def __getattr__(self_, name):
                def f(*a, **k):
                    calls.append((name, a, k))
                    return None
                return f

        fn(_Rec())
        assert calls

        def run(e, calls=calls, waits=waits, sem=sem):
            for (s, v) in waits:
                e.wait_ge(s, v)
            last = None
            for (name, a, k) in calls:
                last = getattr(e, name)(*a, **k)
            last.then_inc(sem, 1)

        self.streams[eng].append(run)
        self._commit(reads, writes, (eng, self.cnt[eng]))

    def dma(self, queue, out, in_, chan, reads=(), writes=(), **kw):
        reads = list(reads)
        writes = list(writes)
        if chan not in self.chan_sem:
            self.chan_sem[chan] = self.stack.enter_context(self.nc.semaphore("c_" + str(chan)))
            self.chan_cnt[chan] = 0
        deps = self._deps(reads, writes)
        waits = self._emit_waits(queue, deps)
        self.chan_cnt[chan] += 16
        val = self.chan_cnt[chan]
        csem = self.chan_sem[chan]

        def run(e, waits=waits, csem=csem, out=out, in_=in_, kw=kw):
            for (s, v) in waits:
                e.wait_ge(s, v)
            e.dma_start(out=out, in_=in_, **kw).then_inc(csem, 16)

        self.streams[queue].append(run)
        self._commit(reads, writes, (chan, val))

    def barrier(self):
        evs = [(e, self.cnt[e]) for e in ENGS if self.cnt[e] > 0]
        evs += [(c, v) for c, v in self.chan_cnt.items() if v > 0]
        for eng in ENGS:
            waits = self._emit_waits(eng, evs)

            def run(e, waits=waits):
                for (s, v) in waits:
                    e.wait_ge(s, v)

            self.streams[eng].append(run)

    def emit_phase(self):
        self.barrier()
        self.emit()
        self.streams = {e: [] for e in ENGS}

    def final_wait(self, eng, keys):
        deps = [self.lastw[k] for k in keys if k in self.lastw]
        waits = self._emit_waits(eng, deps)

        def run(e, waits=waits):
            for (s, v) in waits:
                e.wait_ge(s, v)

        self.streams[eng].append(run)

    def emit(self):
        nc = self.nc
        with nc.Block() as block:
            @block.tensor
            def _(e):
                for f in self.streams["pe"]:
                    f(e)

            @block.scalar
            def _(e):
                for f in self.streams["act"]:
                    f(e)

            @block.vector
            def _(e):
                for f in self.streams["dve"]:
                    f(e)

            @block.gpsimd
            def _(e):
                for f in self.streams["pool"]:
                    f(e)

            @block.sync
            def _(e):
                for f in self.streams["sp"]:
                    f(e)


WEIGHT_SHAPES = {
    "w_ada": [D, 6 * D], "b_ada": [1, 6 * D],
    "norm_pre_mix": [1, D], "norm_post_mix": [1, D], "norm_pre_ffn": [1, D], "norm_post_ffn": [1, D],
    "w_in": [D, IN_COLS],
    "s5_lam_re": [64, 64], "s5_lam_im": [64, 64], "s5_log_dt": [1, 64],
    "s5_b_re": [64, 64, 16], "s5_b_im": [64, 64, 16], "s5_c_re": [64, 16, 64], "s5_c_im": [64, 16, 64],
    "s5_d": [1, D], "s5_w_glu": [D, D], "s5_b_glu": [1, D],
    "m_conv_w": [4, 3072], "m_conv_b": [1, 3072], "m_dt_bias": [1, 32], "m_a_log": [1, 32], "m_d": [1, 32],
    "m_norm": [1, 2048],
    "w_branch_s5": [D, D], "w_branch_ssd": [2048, D], "w_out": [D, D],
    "w_ffn_in": [D, 2 * D_FF], "w_ffn_out": [D_FF, D],
}
IN_SHAPES = {
    "xp": [SEQ, D], "xs": [NS, D], "c17": [NS + 1, D],
    "s5re_in": [NS, 4096], "s5im_in": [NS, 4096], "ssm_in": [NS, 32, 64, 128], "conv_in": [NS * 3, 3072],
}
OUT_SHAPES = {
    "y_p": [SEQ, D], "y_s": [NS, D], "s5re_p": [32, 128], "s5im_p": [32, 128], "ssm_p": [2048, 128],
    "conv_p": [3, 3072], "s5re_s": [NS, 4096], "s5im_s": [NS, 4096], "ssm_s": [NS, 32, 64, 128],
    "conv_s": [NS, 3, 3072],
}


def build_nc(debug_names=(), stop=None, nst=None):
    global STOP, NST_RUN
    STOP = stop
    NST_RUN = nst or NST
    SAMPLE.clear()
    nc = bass.Bass("TRN2", target_bir_lowering=False)
    dr = {}
    for n, s in list(IN_SHAPES.items()) + list(WEIGHT_SHAPES.items()):
        dr[n] = nc.dram_tensor(n, s, F32, kind="ExternalInput").ap()
    for n, s in OUT_SHAPES.items():
        dr[n] = nc.dram_tensor(n, s, F32, kind="ExternalOutput").ap()
    dbg_out = {}

    with ExitStack() as st:
        st.enter_context(nc.allow_non_contiguous_dma(reason="small strided parameter loads"))
        P = Prog(nc, st)

        ARW = 52800
        arena = st.enter_context(nc.sbuf_tensor("arena", [128, ARW], F32))
        bump = {"lo": 0, "hi": ARW, "peak": 0}

        def _carve(off, shape, dt):
            esz = 2 if dt == BF16 else 4
            n = 1
            for d_ in shape[1:]:
                n *= d_
            words = (n * esz + 3) // 4
            v = arena[0:shape[0], off:off + words]
            if dt != F32:
                v = v.bitcast(dt)
            v = v[:, 0:n]
            if len(shape) == 3:
                v = v.rearrange("p (a b) -> p a b", a=shape[1])
            elif len(shape) == 4:
                v = v.rearrange("p (a b c) -> p a b c", a=shape[1], b=shape[2])
            elif len(shape) == 5:
                v = v.rearrange("p (a b c d) -> p a b c d", a=shape[1], b=shape[2], c=shape[3])
            return v, words

        def sb(name, shape, dt=F32):
            v, words = _carve(bump["lo"], shape, dt)
            bump["lo"] += words
            assert bump["lo"] <= bump["hi"], ("SBUF arena overflow at", name, bump)
            bump["peak"] = max(bump["peak"], bump["lo"])
            return v

        def ssb(name, shape, dt=F32):
            esz = 2 if dt == BF16 else 4
            n = 1
            for d_ in shape[1:]:
                n *= d_
            words = (n * esz + 3) // 4
            bump["hi"] -= words
            assert bump["lo"] <= bump["hi"], ("SBUF arena overflow (scratch) at", name, bump)
            v, _ = _carve(bump["hi"], shape, dt)
            return v

        def psum(name, shape, dt=F32):
            return st.enter_context(nc.psum_tensor(name, shape, dt))

        def dump(name, ap, key, shape):
            if name not in debug_names:
                return
            t = nc.dram_tensor("dbg_" + name, shape, F32, kind="ExternalOutput").ap()
            dbg_out[name] = t
            P.dma("sp" if ap.dtype == F32 else "pool", t, ap, "dbg_" + name, reads=[key], writes=["dbg_" + name])

        def finish():
            P.final_wait("sp", [k for k in P.lastw if str(k).startswith(("o_", "yout", "dbg_"))])
            P.emit()
            return nc, dbg_out

        NPS = 6
        ps_f = [psum("psf%d" % i, [128, 512], F32) for i in range(NPS)]
        ps_b = [psum("psb%d" % i, [128, 1024], BF16) for i in range(2)]
        ps_ctr = [0, 0]

        def next_ps():
            i = ps_ctr[0] % NPS
            ps_ctr[0] += 1
            return ps_f[i], "psf%d" % i

        def next_psb():
            i = ps_ctr[1] % 2
            ps_ctr[1] += 1
            return ps_b[i], "psb%d" % i

        ident = sb("ident", [128, 128], F32)
        identb = sb("identb", [128, 128], BF16)
        ones = sb("ones", [128, 128], F32)
        tri = sb("tri", [128, 128], F32)
        su = sb("su", [128, 128], F32)
        P.op("pool", lambda e: e.memset(ones[:], 1.0), writes=["ones"])
        P.op("pool", lambda e: e.affine_select(out=ident[:], in_=ones[:], pattern=[[-1, 128]], compare_op=ALU.is_equal,
                                               fill=0.0, base=0, channel_multiplier=1), reads=["ones"], writes=["ident"])
        P.op("pool", lambda e: e.affine_select(out=tri[:], in_=ones[:], pattern=[[1, 128]], compare_op=ALU.is_ge,
                                               fill=0.0, base=0, channel_multiplier=-1), reads=["ones"], writes=["tri"])
        P.op("pool", lambda e: e.affine_select(out=su[:], in_=ones[:], pattern=[[-1, 128]], compare_op=ALU.is_gt,
                                               fill=0.0, base=0, channel_multiplier=1), reads=["ones"], writes=["su"])
        P.op("dve", lambda e: e.tensor_copy(out=identb[:], in_=ident[:]), reads=["ident"], writes=["identb"])

        def bc_row(name, src, n, alloc=None):
            t = (alloc or sb)(name, [128, n], F32)
            P.dma("sp", t[:], src.partition_broadcast(128), name, writes=[name])
            return t

        mnorm = sb("mnorm", [128, 2048], BF16)
        P.dma("pool", mnorm[:], dr["m_norm"][0, :].partition_broadcast(128), "mnorm", writes=["mnorm"])
        dtb = bc_row("dtb", dr["m_dt_bias"][0, :], 32)
        alog = bc_row("alog", dr["m_a_log"][0, :], 32, ssb)
        mdr = bc_row("mdr", dr["m_d"][0, :], 32)
        arow = sb("arow", [128, 32], F32)
        P.op("act", lambda e: e.activation(out=arow[:], in_=alog[:], func=AF.Exp), reads=["alog"], writes=["arow"])
        P.op("dve", lambda e: e.tensor_scalar(out=arow[:], in0=arow[:], scalar1=-1.0, scalar2=None, op0=ALU.mult),
             reads=["arow"], writes=["arow"])
        s5d = sb("s5d", [128, 8], F32)
        bglu = sb("bglu", [128, 8], F32)
        cw = sb("cw", [128, 4, 24], F32)
        cb = sb("cb", [128, 24], F32)
        P.dma("sp", s5d[:], dr["s5_d"][0, :].rearrange("(a p) -> p a", p=128), "s5d", writes=["s5d"])
        P.dma("sp", bglu[:], dr["s5_b_glu"][0, :].rearrange("(a p) -> p a", p=128), "bglu", writes=["bglu"])
        P.dma("sp", cb[:], dr["m_conv_b"][0, :].rearrange("(a p) -> p a", p=128), "cb", writes=["cb"])
        for k in range(4):
            P.dma("sp", cw[:, k, :], dr["m_conv_w"][k, :].rearrange("(a p) -> p a", p=128), "cw", writes=["cw"])

        if STOP == "s0":
            dump("cw", cw[:].rearrange("p a b -> p (a b)"), "cw", [128, 96])
            dump("mnorm", mnorm[:], "mnorm", [128, 2048])
            dump("tri", tri[:], "tri", [128, 128])
            return finish()
        NWB = 3
        WBE = 4096
        scratch = {}
        cast_prev = {}
        wbufs = [sb("wbuf%d" % i, [128, WBE], BF16) for i in range(NWB)]
        WB_EXTRA = []

        def scratch_chunk(wd, K, c0, cwid):
            nm = wd.name
            if nm == "w_ada":
                return None, None
            key = (nm, c0, cwid)
            if key not in scratch:
                KT = K // 128
                t = nc.dram_tensor("scr_%s_%d_%d" % (nm, c0, cwid), [128, KT * cwid], BF16).ap()
                sk = "scr_%s_%d" % (nm, c0)
                src = wd.rearrange("(k p) c -> p k c", p=128)[:, :, c0:c0 + cwid]
                ch = len(scratch) % 8
                prev = cast_prev.get(ch)
                P.dma("pool", t.rearrange("p (k c) -> p k c", c=cwid), src, "cast%d" % ch, reads=([prev] if prev else []), writes=[sk])
                cast_prev[ch] = sk
                scratch[key] = (t, sk)
            return scratch[key]

        _cw_dummy = None

        def _cw(KT, ncols):
            c = min(512, ncols)
            while c * KT > WBE:
                c //= 2
            return c
        w_ctr = [0]

        def load_w(wd, K, c0, cwid):
            KT = (K + 127) // 128
            i = w_ctr[0] % (NWB + len(WB_EXTRA))
            w_ctr[0] += 1
            key = "wbuf%d" % i
            flat = (wbufs + WB_EXTRA)[i][:, 0:KT * cwid]
            view = flat.rearrange("p (k c) -> p k c", c=cwid)
            sc, sk = scratch_chunk(wd, K, c0, cwid)
            if sc is None:
                src = wd.rearrange("(k p) c -> p k c", p=128)[:, :, c0:c0 + cwid]
                P.dma("pool", view, src, key, writes=[key])
            else:
                P.dma("sp", flat, sc, key, reads=[sk], writes=[key])
            return view, key

        def drain(g):
            for _ in g:
                pass

        def interleave(gens, weights=None):
            gens = list(gens)
            weights = list(weights or [1] * len(gens))
            live = [True] * len(gens)
            while any(live):
                for i, g in enumerate(gens):
                    if not live[i]:
                        continue
                    for _ in range(weights[i]):
                        try:
                            next(g)
                        except StopIteration:
                            live[i] = False
                            break

        def linear_fm(*a, **k):
            drain(linear_fm_g(*a, **k))

        def linear_tm(*a, **k):
            drain(linear_tm_g(*a, **k))

        def transpose_to(*a, **k):
            drain(transpose_to_g(*a, **k))

        def linear_fm_g(inT, in_key, K, wd, c0, ncols, tt, evac, cwid=512):
            KT = K // 128
            cwid = _cw(KT, ncols)
            ct = 0
            for cc in range(0, ncols, cwid):
                wv, wk = load_w(wd, K, c0 + cc, cwid)
                for j in range(0, cwid, 128):
                    pt, pk = next_ps()

                    def mm(e, pt=pt, wv=wv, j=j):
                        last = None
                        for kt in range(KT):
                            last = e.matmul(pt[:, 0:tt], lhsT=wv[:, kt, j:j + 128], rhs=inT[:, kt, 0:tt],
                                            start=(kt == 0), stop=(kt == KT - 1))
                        return last

                    P.op("pe", mm, reads=[wk, in_key], writes=[pk])
                    evac(pt, pk, ct)
                    ct += 1
                yield

        def linear_tm_g(inT, in_key, K, wd, c0, ncols, tiles, evac, cwid=512):
            KT = K // 128
            cwid = _cw(KT, ncols)
            for cc in range(0, ncols, cwid):
                wv, wk = load_w(wd, K, c0 + cc, cwid)
                for ti, (t0, rows) in enumerate(tiles):
                    pt, pk = next_ps()

                    def mm(e, pt=pt, wv=wv, t0=t0, rows=rows):
                        last = None
                        for kt in range(KT):
                            last = e.matmul(pt[0:rows, 0:cwid], lhsT=inT[:, kt, t0:t0 + rows], rhs=wv[:, kt, :],
                                            start=(kt == 0), stop=(kt == KT - 1))
                        return last

                    P.op("pe", mm, reads=[wk, in_key], writes=[pk])
                    evac(pt, pk, ti, cc, cwid)
                    yield

        ev_ctr = [0]

        def copy_evac(dst, dkey, src, skey):
            ev_ctr[0] += 1
            if True:
                P.op("act", lambda e: e.copy(out=dst, in_=src), reads=[skey], writes=[dkey])
            else:
                P.op("dve", lambda e: e.tensor_copy(out=dst, in_=src), reads=[skey], writes=[dkey])

        def transpose_to_g(dst_fn, dkey, src_fn, skey, n, rows, cols, dt):
            rp = (rows + 3) // 4 * 4
            for i0 in range(0, n, 4):
                cnt = min(4, n - i0)
                if dt == BF16:
                    pt, pk = next_psb()
                    idn = identb
                else:
                    pt, pk = next_ps()
                    idn = ident

                def tr(e, pt=pt, i0=i0, cnt=cnt, idn=idn):
                    last = None
                    for a in range(cnt):
                        last = e.transpose(pt[0:cols, a * rp:a * rp + rows], src_fn(i0 + a), idn[0:rows, 0:rows])
                    return last

                P.op("pe", tr, reads=[skey, "ident", "identb"], writes=[pk])
                src = pt[0:cols, 0:cnt * rp].rearrange("p (a r) -> p a r", r=rp)[:, :, 0:rows]
                copy_evac(dst_fn(i0, cnt), dkey, src, pk)
                yield

        modp = sb("modp", [128, 2, D], BF16)
        modcol = sb("modcol", [128, 6, 8], F32)

        def ada_compute(alloc, want_prompt, mods_t=None):
            c17 = alloc("c17", [NS + 1, D], F32)
            c17s = alloc("c17s", [NS + 1, D], BF16)
            c17T = alloc("c17T", [128, 8, NS + 1], BF16)
            badac = alloc("badac", [NS + 1, 512], F32)
            modc = alloc("modc", [NS + 1, 512], F32)
            nrows = {}
            for nm_, key_ in (("npm", "norm_pre_mix"), ("npo", "norm_post_mix"), ("npf", "norm_pre_ffn"), ("nqf", "norm_post_ffn")):
                t_ = alloc(nm_, [128, D], F32)
                P.dma("sp", t_[:], dr[key_][0, :].partition_broadcast(128), nm_, writes=[nm_])
                nrows[nm_] = t_
            P.dma("sp", c17[:], dr["c17"], "c17", writes=["c17"])
            P.op("act", lambda e: e.activation(out=c17s[:], in_=c17[:], func=AF.Silu), reads=["c17"], writes=["c17s"])
            transpose_to(lambda i0, cnt: c17T[:, i0:i0 + cnt, :], "c17T", lambda i: c17s[:, i * 128:(i + 1) * 128], "c17s",
                         8, NS + 1, 128, BF16)
            if want_prompt:
                sel16 = alloc("sel16", [NS + 1, 128], F32)
                P.op("pool", lambda e: e.affine_select(out=sel16[:], in_=ones[0:NS + 1, :], pattern=[[0, 128]], compare_op=ALU.is_equal,
                                                       fill=0.0, base=-NS, channel_multiplier=1), reads=["ones"], writes=["sel16"])

            def ada_evac(pt, pk, ti, cc, cwid):
                j, off = cc // D, cc % D
                P.dma("sp", badac[:, 0:cwid], dr["b_ada"][0, cc:cc + cwid].partition_broadcast(NS + 1), "badac", writes=["badac"])
                P.op("dve", lambda e: e.tensor_tensor(out=modc[:, 0:cwid], in0=pt[0:NS + 1, 0:cwid], in1=badac[:, 0:cwid], op=ALU.add),
                     reads=[pk, "badac"], writes=["modc"])
                if want_prompt and j in (2, 5):
                    p2, pk2 = next_ps()
                    P.op("pe", lambda e: e.matmul(p2[:, 0:cwid], lhsT=sel16[:, :], rhs=modc[:, 0:cwid], start=True, stop=True),
                         reads=["sel16", "modc"], writes=[pk2])
                    P.op("act", lambda e: e.copy(out=modp[:, j // 3, off:off + cwid], in_=p2[:, 0:cwid]), reads=[pk2], writes=["modp"])
                elif want_prompt:
                    p2, pk2 = next_ps()

                    def trm(e):
                        last = None
                        for b_ in range(cwid // 128):
                            last = e.transpose(p2[:, b_ * 32:b_ * 32 + NS + 1], modc[0:NS + 1, b_ * 128:(b_ + 1) * 128], ident[0:NS + 1, 0:NS + 1])
                        return last

                    P.op("pe", trm, reads=["modc", "ident"], writes=[pk2])
                    kt0 = off // 128
                    P.op("act", lambda e: e.copy(out=modcol[:, j, kt0:kt0 + cwid // 128],
                                                 in_=p2[:, 0:(cwid // 128) * 32].rearrange("p (b c) -> p b c", c=32)[:, :, NS]),
                         reads=[pk2], writes=["modcol"])
                else:
                    P.op("dve", lambda e: e.tensor_copy(out=mods_t[:, j, off:off + cwid], in_=modc[0:NS, 0:cwid]), reads=["modc"], writes=["mods"])

            linear_tm(c17T, "c17T", D, dr["w_ada"], 0, 6 * D, [(0, NS + 1)], ada_evac)
            if want_prompt:
                for (jsc, wname) in ((1, "norm_pre_mix"), (4, "norm_pre_ffn")):
                    ncol = alloc("ncol%d" % jsc, [128, 8], F32)
                    P.dma("sp", ncol[:], dr[wname][0, :].rearrange("(a p) -> p a", p=128), "ncol%d" % jsc, writes=["ncol%d" % jsc])
                    P.op("dve", lambda e, jsc=jsc, ncol=ncol: e.scalar_tensor_tensor(
                        out=modcol[:, jsc, :], in0=modcol[:, jsc, :], scalar=1.0, op0=ALU.add, in1=ncol[:], op1=ALU.mult),
                        reads=["modcol", "ncol%d" % jsc], writes=["modcol"])
                for (gi, nwk) in ((0, "npo"), (1, "nqf")):
                    nw = nrows[nwk]
                    P.op("dve", lambda e, gi=gi, nw=nw: e.tensor_tensor(out=modp[:, gi, :], in0=modp[:, gi, :], in1=nw[:, :], op=ALU.mult),
                         reads=["modp", nwk], writes=["modp"])
            else:
                mt, rows, key = mods_t, NS, "mods"
                for (jsc, nwk) in ((1, "npm"), (4, "npf")):
                    nw = nrows[nwk]
                    P.op("dve", lambda e, jsc=jsc, nw=nw: e.scalar_tensor_tensor(
                        out=mt[:, jsc, :], in0=mt[:, jsc, :], scalar=1.0, op0=ALU.add, in1=nw[0:rows, :], op1=ALU.mult),
                        reads=[key, nwk], writes=[key])
                for (jg, nwk) in ((2, "npo"), (5, "nqf")):
                    nw = nrows[nwk]
                    P.op("dve", lambda e, jg=jg, nw=nw: e.tensor_tensor(
                        out=mt[:, jg, :], in0=mt[:, jg, :], in1=nw[0:rows, :], op=ALU.mult), reads=[key, nwk], writes=[key])

        ada_compute(ssb, True)

        if STOP == "s1":
            dump("modp", modp[:].rearrange("p a b -> p (a b)"), "modp", [128, 2 * D])
            dump("modcol", modcol[:].rearrange("p a b -> p (a b)"), "modcol", [128, 48])
            return finish()
        lre = ssb("lre", [128, 32], F32)
        lim = ssb("lim", [128, 32], F32)
        ldt = ssb("ldt", [128, 32], F32)
        P.dma("sp", lre[:], dr["s5_lam_re"].rearrange("(q gl) n -> (gl n) q", gl=2), "lre", writes=["lre"])
        P.dma("sp", lim[:], dr["s5_lam_im"].rearrange("(q gl) n -> (gl n) q", gl=2), "lim", writes=["lim"])
        for gl in range(2):
            P.dma("sp", ldt[gl * 64:(gl + 1) * 64, :],
                  dr["s5_log_dt"][0, :].rearrange("(q gl) -> gl q", gl=2)[gl, :].partition_broadcast(64), "ldt", writes=["ldt"])
        s5t = sb("s5t", [128, 12, 32], F32)
        DT_, TH, RHO, AR, AI, FR, FI, DEN, T0, T1, T2, T3 = range(12)

        def s5op(fn, wr):
            P.op("dve", fn, reads=["lre", "lim", "ldt", "s5t"], writes=wr)

        def frac_sin(out_ap, ang_ap, tmp_f, tmp_i, shape_key, quarter):
            P.op("dve", lambda e: e.tensor_scalar(out=tmp_f, in0=ang_ap, scalar1=1.0 / TWO_PI, scalar2=0.25 * quarter,
                                                  op0=ALU.mult, op1=ALU.add), reads=[shape_key], writes=[shape_key])
            P.op("dve", lambda e: e.tensor_copy(out=tmp_i, in_=tmp_f), reads=[shape_key], writes=[shape_key])
            P.op("dve", lambda e: e.tensor_copy(out=out_ap, in_=tmp_i), reads=[shape_key], writes=[shape_key])
            P.op("dve", lambda e: e.tensor_tensor(out=tmp_f, in0=tmp_f, in1=out_ap, op=ALU.subtract), reads=[shape_key], writes=[shape_key])
            P.op("dve", lambda e: e.tensor_scalar(out=out_ap, in0=tmp_f, scalar1=0.5, scalar2=None, op0=ALU.is_gt),
                 reads=[shape_key], writes=[shape_key])
            P.op("dve", lambda e: e.tensor_tensor(out=tmp_f, in0=tmp_f, in1=out_ap, op=ALU.subtract), reads=[shape_key], writes=[shape_key])
            P.op("dve", lambda e: e.tensor_scalar(out=out_ap, in0=tmp_f, scalar1=-0.5, scalar2=None, op0=ALU.is_lt),
                 reads=[shape_key], writes=[shape_key])
            P.op("dve", lambda e: e.tensor_tensor(out=tmp_f, in0=tmp_f, in1=out_ap, op=ALU.add), reads=[shape_key], writes=[shape_key])
            P.op("act", lambda e: e.activation(out=out_ap, in_=tmp_f, func=AF.Sin, scale=TWO_PI), reads=[shape_key], writes=[shape_key])

        tmpi = ssb("tmpi", [128, 32 * 64], mybir.dt.int32)
        tmpf = ssb("tmpf", [128, 32 * 64], F32)
        P.op("act", lambda e: e.activation(out=s5t[:, DT_, :], in_=ldt[:], func=AF.Exp), reads=["ldt"], writes=["s5t"])
        s5op(lambda e: e.tensor_tensor(out=s5t[:, TH, :], in0=lim[:], in1=s5t[:, DT_, :], op=ALU.mult), ["s5t"])
        s5op(lambda e: e.tensor_tensor(out=s5t[:, T0, :], in0=lre[:], in1=s5t[:, DT_, :], op=ALU.mult), ["s5t"])
        P.op("act", lambda e: e.activation(out=s5t[:, RHO, :], in_=s5t[:, T0, :], func=AF.Exp), reads=["s5t"], writes=["s5t"])
        frac_sin(s5t[:, T1, :], s5t[:, TH, :], tmpf[:, 0:32], tmpi[:, 0:32], "s5t", 1)
        frac_sin(s5t[:, T2, :], s5t[:, TH, :], tmpf[:, 0:32], tmpi[:, 0:32], "s5t", 0)
        s5op(lambda e: e.tensor_tensor(out=s5t[:, AR, :], in0=s5t[:, RHO, :], in1=s5t[:, T1, :], op=ALU.mult), ["s5t"])
        s5op(lambda e: e.tensor_tensor(out=s5t[:, AI, :], in0=s5t[:, RHO, :], in1=s5t[:, T2, :], op=ALU.mult), ["s5t"])
        s5op(lambda e: e.tensor_tensor(out=s5t[:, T0, :], in0=lre[:], in1=lre[:], op=ALU.mult), ["s5t"])
        s5op(lambda e: e.tensor_tensor(out=s5t[:, T1, :], in0=lim[:], in1=lim[:], op=ALU.mult), ["s5t"])
        s5op(lambda e: e.tensor_tensor(out=s5t[:, DEN, :], in0=s5t[:, T0, :], in1=s5t[:, T1, :], op=ALU.add), ["s5t"])
        s5op(lambda e: e.reciprocal(out=s5t[:, DEN, :], in_=s5t[:, DEN, :]), ["s5t"])
        s5op(lambda e: e.tensor_scalar(out=s5t[:, T0, :], in0=s5t[:, AR, :], scalar1=-1.0, scalar2=None, op0=ALU.add), ["s5t"])
        s5op(lambda e: e.tensor_tensor(out=s5t[:, T1, :], in0=s5t[:, T0, :], in1=lre[:], op=ALU.mult), ["s5t"])
        s5op(lambda e: e.tensor_tensor(out=s5t[:, T2, :], in0=s5t[:, AI, :], in1=lim[:], op=ALU.mult), ["s5t"])
        s5op(lambda e: e.tensor_tensor(out=s5t[:, T1, :], in0=s5t[:, T1, :], in1=s5t[:, T2, :], op=ALU.add), ["s5t"])
        s5op(lambda e: e.tensor_tensor(out=s5t[:, FR, :], in0=s5t[:, T1, :], in1=s5t[:, DEN, :], op=ALU.mult), ["s5t"])
        s5op(lambda e: e.tensor_tensor(out=s5t[:, T1, :], in0=s5t[:, AI, :], in1=lre[:], op=ALU.mult), ["s5t"])
        s5op(lambda e: e.tensor_tensor(out=s5t[:, T2, :], in0=s5t[:, T0, :], in1=lim[:], op=ALU.mult), ["s5t"])
        s5op(lambda e: e.tensor_tensor(out=s5t[:, T1, :], in0=s5t[:, T1, :], in1=s5t[:, T2, :], op=ALU.subtract), ["s5t"])
        s5op(lambda e: e.tensor_tensor(out=s5t[:, FI, :], in0=s5t[:, T1, :], in1=s5t[:, DEN, :], op=ALU.mult), ["s5t"])

        if STOP == "s2":
            dump("s5t", s5t[:].rearrange("p a b -> p (a b)"), "s5t", [128, 12 * 32])
            return finish()
        cosT = sb("cosT", [128, 32, 64], F32)
        sinT = sb("sinT", [128, 32, 64], F32)
        ang = ssb("ang", [128, 32, 64], F32)
        iot = ssb("iot", [128, 64], F32)
        P.op("pool", lambda e: e.iota(iot[:], [[1, 64]], base=1, channel_multiplier=0, allow_small_or_imprecise_dtypes=True),
             writes=["iot"])
        P.op("dve", lambda e: e.tensor_tensor(out=ang[:], in0=s5t[:, TH, :].unsqueeze(2).to_broadcast([128, 32, 64]),
                                              in1=iot[:].unsqueeze(1).to_broadcast([128, 32, 64]), op=ALU.mult),
             reads=["s5t", "iot"], writes=["ang"])
        angf = ang[:].rearrange("p a b -> p (a b)")
        P.op("dve", lambda e: e.tensor_copy(out=tmpf[:, 0:1], in_=tmpf[:, 0:1]), reads=["ang", "s5t"], writes=["tab"])
        frac_sin(cosT[:].rearrange("p a b -> p (a b)"), angf, tmpf[:], tmpi[:], "tab", 1)
        frac_sin(sinT[:].rearrange("p a b -> p (a b)"), angf, tmpf[:], tmpi[:], "tab", 0)

        if STOP == "s3":
            dump("cosT", cosT[:].rearrange("p a b -> p (a b)"), "tab", [128, 2048])
            return finish()
        bre = ssb("bre", [128, 32, 16], F32)
        bim = ssb("bim", [128, 32, 16], F32)
        P.dma("sp", bre[:], dr["s5_b_re"].rearrange("(q gl) n i -> (gl n) q i", gl=2), "bre", writes=["bre"])
        P.dma("sp", bim[:], dr["s5_b_im"].rearrange("(q gl) n i -> (gl n) q i", gl=2), "bim", writes=["bim"])
        bbr = ssb("bbr", [128, 32, 16], F32)
        bbi = ssb("bbi", [128, 32, 16], F32)
        bt = ssb("bt", [128, 32, 16], F32)
        frb = s5t[:, FR, :].unsqueeze(2).to_broadcast([128, 32, 16])
        fib = s5t[:, FI, :].unsqueeze(2).to_broadcast([128, 32, 16])
        RB = ["bre", "bim", "s5t", "bt", "bbr", "bbi"]
        P.op("dve", lambda e: e.tensor_tensor(out=bbr[:], in0=bre[:], in1=frb, op=ALU.mult), reads=RB, writes=["bbr"])
        P.op("dve", lambda e: e.tensor_tensor(out=bt[:], in0=bim[:], in1=fib, op=ALU.mult), reads=RB, writes=["bt"])
        P.op("dve", lambda e: e.tensor_tensor(out=bbr[:], in0=bbr[:], in1=bt[:], op=ALU.subtract), reads=RB, writes=["bbr"])
        P.op("dve", lambda e: e.tensor_tensor(out=bbi[:], in0=bim[:], in1=frb, op=ALU.mult), reads=RB, writes=["bbi"])
        P.op("dve", lambda e: e.tensor_tensor(out=bt[:], in0=bre[:], in1=fib, op=ALU.mult), reads=RB, writes=["bt"])
        P.op("dve", lambda e: e.tensor_tensor(out=bbi[:], in0=bbi[:], in1=bt[:], op=ALU.add), reads=RB, writes=["bbi"])
        wb = sb("wb", [128, 8, 2, 128], BF16)
        x4 = ssb("x4", [128, 4, 2, 16], F32)
        P.op("pool", lambda e: e.memset(x4[:].rearrange("p a b c -> p (a b c)"), 0.0), writes=["x4"])
        for G in range(8):
            for ri, bb in enumerate((bbr, bbi)):
                bk = "bbr" if ri == 0 else "bbi"
                for gl in range(2):
                    P.op("dve", lambda e, G=G, bb=bb, gl=gl: e.tensor_copy(out=x4[gl * 64:(gl + 1) * 64, :, gl, :],
                                                                           in_=bb[gl * 64:(gl + 1) * 64, 4 * G:4 * G + 4, :]),
                         reads=[bk], writes=["x4"])
                pt, pk = next_ps()
                P.op("pe", lambda e, pt=pt: e.transpose(pt[:, 0:128], x4[:].rearrange("p a b c -> p (a b c)"), ident[:]),
                     reads=["x4", "ident"], writes=[pk])
                P.op("act", lambda e, pt=pt, G=G, ri=ri: e.copy(out=wb[:, G, ri, :], in_=pt[:, 0:128]), reads=[pk], writes=["wb"])
        ctr_ = ssb("ctr_", [128, 32, 16], F32)
        cti_ = ssb("cti_", [128, 32, 16], F32)
        zc = ssb("zc", [128, 2, 64], F32)
        for (dst, dk, src) in ((ctr_, "ctr_", dr["s5_c_re"]), (cti_, "cti_", dr["s5_c_im"])):
            for qb in range(4):
                v = src.rearrange("(qq gl) j n -> qq j gl n", gl=2)[8 * qb:8 * qb + 8]
                for qq in range(8):
                    P.dma("sp", zc[16 * qq:16 * qq + 16, :, :], v[qq], "zc", writes=["zc"])
                pt, pk = next_ps()
                P.op("pe", lambda e, pt=pt: e.transpose(pt[:, 0:128], zc[:].rearrange("p a b -> p (a b)"), ident[:]),
                     reads=["zc", "ident"], writes=[pk])
                P.op("act", lambda e, pt=pt, dst=dst, qb=qb: e.copy(out=dst[:, 8 * qb:8 * qb + 8, :],
                                                                   in_=pt[:, 0:128].rearrange("p (a b) -> p a b", b=16)),
                     reads=[pk], writes=[dk])
        wd_ = sb("wd_", [128, 32, 2, 32], BF16)
        P.op("pool", lambda e: e.memset(wd_[:].rearrange("p a b c -> p (a b c)"), 0.0), writes=["wd_"])
        for q in range(32):
            for gl in range(2):
                c0 = 16 * gl
                P.op("dve", lambda e, q=q, gl=gl, c0=c0: e.tensor_copy(out=wd_[gl * 64:(gl + 1) * 64, q, 0, c0:c0 + 16],
                                                                       in_=ctr_[gl * 64:(gl + 1) * 64, q, :]),
                     reads=["ctr_"], writes=["wd_"])
                P.op("dve", lambda e, q=q, gl=gl, c0=c0: e.tensor_scalar(out=wd_[gl * 64:(gl + 1) * 64, q, 1, c0:c0 + 16],
                                                                         in0=cti_[gl * 64:(gl + 1) * 64, q, :], scalar1=-1.0,
                                                                         scalar2=None, op0=ALU.mult),
                     reads=["cti_"], writes=["wd_"])
        P.emit_phase()
        if STOP == "setup":
            dump("cosT", cosT[:].rearrange("p a b -> p (a b)"), "tab", [128, 2048])
            dump("sinT", sinT[:].rearrange("p a b -> p (a b)"), "tab", [128, 2048])
            dump("s5t", s5t[:].rearrange("p a b -> p (a b)"), "s5t", [128, 12 * 32])
            dump("mods", mods[:].rearrange("p a b -> p (a b)"), "mods", [NS, 6 * D])
            return finish()
        bump['hi'] = ARW
        print('arena after setup: lo=%d words' % bump['lo'])

        s5c = sb("s5c", [128, 32, 2], F32)
        P.op("pool", lambda e: e.memset(s5c[:].rearrange("p a b -> p (a b)"), 0.0), writes=["s5c"])
        uz = sb("uz", [128, 4, T], BF16)
        P.op("pool", lambda e: e.memset(uz[:].rearrange("p a b -> p (a b)"), 0.0), writes=["uz"])
        hT = sb("hT", [128, 4, 512], F32)
        hTb = sb("hTb", [128, 4, 512], BF16)
        P.op("pool", lambda e: e.memset(hT[:].rearrange("p a b -> p (a b)"), 0.0), writes=["hT"])
        P.op("pool", lambda e: e.memset(hTb[:].rearrange("p a b -> p (a b)"), 0.0), writes=["hTb"])
        xbcT = sb("xbcT", [128, 24, T + 3], BF16)
        P.op("pool", lambda e: e.memset(xbcT[:].rearrange("p a b -> p (a b)"), 0.0), writes=["xbcT"])

        x_tm = sb("x_tm", [128, 1, D], F32)
        hT_ = sb("hT_", [128, 8, T], BF16)
        uT = sb("uT", [128, 8, T], BF16)
        zs = sb("zs", [128, 1, 2048], BF16)
        gaT = sb("gaT", [128, 8, T], BF16)
        gbT = sb("gbT", [128, 8, T], BF16)
        dtr = sb("dtr", [128, 1, 32], F32)
        yaT = sb("yaT", [128, 8, T], BF16)
        yaoT = sb("yaoT", [128, 8, T], BF16)
        yBT = sb("yBT", [128, 16, T], BF16)
        st1 = sb("st1", [128, 8], F32)
        ys5 = sb("ys5", [128, T], F32)
        cv = [sb("cv%d" % i, [128, T], F32) for i in range(2)]
        sm = sb("sm", [128, 8, 32], F32)
        ytm = sb("ytm", [128, 2048], F32)
        yBtm = sb("yBtm", [128, 2048], BF16)
        hn = sb("hn", [128, D], BF16)
        uT_1 = sb("uT_1", [128, 8, T], BF16)
        zs_1 = sb("zs_1", [128, 1, 2048], BF16)
        gaT_1 = sb("gaT_1", [128, 8, T], BF16)
        gbT_1 = sb("gbT_1", [128, 8, T], BF16)
        dtr_1 = sb("dtr_1", [128, 1, 32], F32)
        yaoT_1 = sb("yaoT_1", [128, 8, T], BF16)
        yBT_1 = sb("yBT_1", [128, 16, T], BF16)
        xbcT_1 = sb("xbcT_1", [128, 24, T + 3], BF16)
        P.op("pool", lambda e: e.memset(xbcT_1[:].rearrange("p a b -> p (a b)"), 0.0), writes=["xbcT1"])
        PB = [dict(uT=uT, zs=zs, xbcT=xbcT, dtr=dtr, gaT=gaT, gbT=gbT, yaoT=yaoT, yBT=yBT),
              dict(uT=uT_1, zs=zs_1, xbcT=xbcT_1, dtr=dtr_1, gaT=gaT_1, gbT=gbT_1, yaoT=yaoT_1, yBT=yBT_1)]
        M1 = bump["lo"]
        actT = sb("actT", [128, 24, T], BF16)
        mrg = sb("mrg", [128, 8, T], BF16)
        mrgT = sb("mrgT", [128, 8, T], BF16)
        factT = sb("factT", [128, 22, T], BF16)
        wk2 = sb("wk2", [128, D], F32)
        M2 = bump["lo"]
        xtail = sb("xtail", [128, 24, 4], F32)
        s5S = [sb("s5S%d" % i, [128, T // 64, 2, 4, 64], F32) for i in range(2)]
        s5t2 = [sb("s5t2%d" % i, [128, 4, T], F32) for i in range(2)]
        rzs = [sb("rz%d" % i, [128, 2, 4, 64], F32) for i in range(2)]
        t8s = [sb("t8%d" % i, [128, 2, 4], F32) for i in range(2)]
        hch = [sb("hch%d" % i, [128, 2, 4, 64], F32) for i in range(2)]
        hbf = [sb("hbf%d" % i, [128, 4, 2, T], BF16) for i in range(2)]
        xtm = sb("xtm", [128, 2048], BF16)
        btm = sb("btm", [128, 4, 128], BF16)
        Rb = sb("Rb", [128, 8, 128], F32)
        LT = sb("LT", [128, 8, 128], BF16)
        MT = sb("MT", [128, 8, 128], BF16)
        CBm = sb("CBm", [128, 128], BF16)
        xdt = sb("xdt", [128, 512], BF16)
        X2 = sb("X2", [128, 512], BF16)
        yt1 = sb("yt1", [128, 512], F32)
        cps = yt1
        if PROBE_WB:
            WB_EXTRA.append(ytm[:, :].bitcast(BF16))
        print("arena: M1=%d M2=%d end=%d of %d" % (M1, M2, bump["lo"], ARW))

        def rms_mod(ti, rows, mt, jA, jB, mkey, tok0):
            xv = x_tm[0:rows, ti, :]
            P.op("act", lambda e: e.activation(out=hn[0:rows, 0:D], in_=xv, func=AF.Square, accum_out=st1[0:rows, 0:1]),
                 reads=["x_tm"], writes=["hn", "st1"])
            P.op("act", lambda e: e.activation(out=st1[0:rows, 1:2], in_=st1[0:rows, 0:1], func=AF.Sqrt, scale=1.0 / D, bias=EPS),
                 reads=["st1"], writes=["st1"])
            P.op("dve", lambda e: e.reciprocal(out=st1[0:rows, 2:3], in_=st1[0:rows, 1:2]), reads=["st1"], writes=["st1"])
            P.op("dve", lambda e: e.scalar_tensor_tensor(out=wk2[0:rows, 0:D], in0=xv, scalar=st1[0:rows, 2:3], op0=ALU.mult,
                                                         in1=mt[0:rows, jA, :], op1=ALU.mult),
                 reads=["x_tm", "st1", mkey], writes=["wk2"])
            P.op("dve", lambda e: e.tensor_tensor(out=hn[0:rows, :], in0=wk2[0:rows, 0:D], in1=mt[0:rows, jB, :], op=ALU.add),
                 reads=["wk2", mkey], writes=["hn"])
            transpose_to(lambda i0, cnt: hT_[:, i0:i0 + cnt, tok0:tok0 + rows], "hT_",
                         lambda i: hn[0:rows, i * 128:(i + 1) * 128], "hn", 8, rows, 128, BF16)

        def rms_mod_p(jA, jB):
            xv = x_tm[:, 0, :]
            P.op("act", lambda e: e.activation(out=hn[:, 0:D], in_=xv, func=AF.Square, accum_out=st1[:, 0:1]),
                 reads=["x_tm"], writes=["hn", "st1"])
            P.op("act", lambda e: e.activation(out=st1[:, 1:2], in_=st1[:, 0:1], func=AF.Sqrt, scale=1.0 / D, bias=EPS),
                 reads=["st1"], writes=["st1"])
            P.op("dve", lambda e: e.reciprocal(out=st1[:, 2:3], in_=st1[:, 1:2]), reads=["st1"], writes=["st1"])
            P.op("dve", lambda e: e.tensor_scalar(out=hn[:, :], in0=xv, scalar1=st1[:, 2:3], scalar2=None, op0=ALU.mult),
                 reads=["x_tm", "st1"], writes=["hn"])
            for i0 in range(0, 8, 4):
                pt, pk = next_psb()

                def tr(e, pt=pt, i0=i0):
                    last = None
                    for a in range(4):
                        last = e.transpose(pt[:, a * 128:(a + 1) * 128], hn[:, (i0 + a) * 128:(i0 + a + 1) * 128], identb[:])
                    return last

                P.op("pe", tr, reads=["hn", "identb"], writes=[pk])
                for a in range(4):
                    kt = i0 + a
                    P.op("act", lambda e, pt=pt, a=a, kt=kt: e.activation(out=hT_[:, kt, 0:128], in_=pt[:, a * 128:(a + 1) * 128], func=AF.Identity,
                                                                         scale=modcol[:, jA, kt:kt + 1], bias=modcol[:, jB, kt:kt + 1]),
                         reads=[pk, "modcol"], writes=["hT_"])

        def resid_gate(src_tm, skey, ti, rows, mt, jG, mkey):
            P.op("act", lambda e: e.activation(out=hn[0:rows, 0:D], in_=src_tm, func=AF.Square, accum_out=st1[0:rows, 4:5]),
                 reads=[skey], writes=["hn", "st1"])
            P.op("act", lambda e: e.activation(out=st1[0:rows, 5:6], in_=st1[0:rows, 4:5], func=AF.Sqrt, scale=1.0 / D, bias=EPS),
                 reads=["st1"], writes=["st1"])
            P.op("dve", lambda e: e.reciprocal(out=st1[0:rows, 6:7], in_=st1[0:rows, 5:6]), reads=["st1"], writes=["st1"])
            P.op("dve", lambda e: e.scalar_tensor_tensor(out=src_tm, in0=src_tm, scalar=st1[0:rows, 6:7], op0=ALU.mult,
                                                         in1=mt[0:rows, jG, :], op1=ALU.mult),
                 reads=[skey, "st1", mkey], writes=[skey])
            P.op("dve", lambda e: e.tensor_tensor(out=x_tm[0:rows, ti, :], in0=x_tm[0:rows, ti, :], in1=src_tm, op=ALU.add),
                 reads=[skey, "x_tm"], writes=["x_tm"])


        def s5_load_u(G, tt, par=0):
            (uT,) = [PB[par][n_] for n_ in ("uT",)]
            kk = lambda n_: n_ if par == 0 else n_ + "1"
            for r in range(4):
                P.op("pool", lambda e, r=r: e.tensor_copy(out=uz[32 * r:32 * r + 32, r, 0:tt], in_=uT[32 * r:32 * r + 32, G, 0:tt]),
                     reads=[kk("uT")], writes=["uz"])

        def s5_rot_in(G, tt, par=0):
            gp = G % 2
            S, Tm = s5S[gp], s5t2[gp]
            sk, tk = "s5S%d" % gp, "s5t2%d" % gp
            nch = tt // 64
            s5_load_u(G, tt, par)
            pr, pkr = next_ps()
            pi_, pki = next_ps()

            def mm(e):
                last = None
                for r in range(4):
                    e.matmul(pr[:, r * tt:(r + 1) * tt], lhsT=wb[:, G, 0, :], rhs=uz[:, r, 0:tt], start=True, stop=True)
                    last = e.matmul(pi_[:, r * tt:(r + 1) * tt], lhsT=wb[:, G, 1, :], rhs=uz[:, r, 0:tt], start=True, stop=True)
                return last

            P.op("pe", mm, reads=["wb", "uz"], writes=[pkr, pki])
            pv = lambda p_: p_[:, 0:4 * tt].rearrange("p (q c t) -> p q c t", q=4, t=64)
            So = lambda ri: S[:, 0:nch, ri].rearrange("p c r t -> p r c t")
            Tv = Tm[:, :, 0:tt].rearrange("p q (c t) -> p q c t", t=64)
            cb_ = cosT[:, 4 * G:4 * G + 4, :].unsqueeze(2).to_broadcast([128, 4, nch, 64])
            sb_ = sinT[:, 4 * G:4 * G + 4, :].unsqueeze(2).to_broadcast([128, 4, nch, 64])
            RK = [pkr, pki, "cosT", "sinT", sk, tk]
            P.op("dve", lambda e: e.tensor_tensor(out=So(0), in0=pv(pr), in1=cb_, op=ALU.mult), reads=RK, writes=[sk])
            P.op("dve", lambda e: e.tensor_tensor(out=Tv, in0=pv(pi_), in1=sb_, op=ALU.mult), reads=RK, writes=[tk])
            P.op("dve", lambda e: e.tensor_tensor(out=So(0), in0=So(0), in1=Tv, op=ALU.add), reads=RK, writes=[sk])
            P.op("dve", lambda e: e.tensor_tensor(out=So(1), in0=pv(pi_), in1=cb_, op=ALU.mult), reads=RK, writes=[sk])
            P.op("dve", lambda e: e.tensor_tensor(out=Tv, in0=pv(pr), in1=sb_, op=ALU.mult), reads=RK, writes=[tk])
            P.op("dve", lambda e: e.tensor_tensor(out=So(1), in0=So(1), in1=Tv, op=ALU.subtract), reads=RK, writes=[sk])

        def s5_scan_chunk(G, c):
            gp = G % 2
            S = s5S[gp]
            sk = "s5S%d" % gp
            hprev = hch[gp]
            hpk = "hch%d" % gp
            rz, t8 = rzs[gp], t8s[gp]
            rzk, t8k = "rz%d" % gp, "t8%d" % gp
            rho4 = s5t[:, RHO, 4 * G:4 * G + 4]
            if c == 0:
                P.op("pool", lambda e: e.tensor_copy(out=rz[:], in_=rho4.unsqueeze(1).unsqueeze(3).to_broadcast([128, 2, 4, 64])),
                     reads=["s5t"], writes=[rzk])
                P.op("pool", lambda e: e.memset(rz[:, :, :, 0:1], 0.0), reads=[rzk], writes=[rzk])
                carry = s5c[:, 4 * G:4 * G + 4, :].rearrange("p r i -> p i r")
            else:
                carry = hprev[:, :, :, 63]
            P.op("dve", lambda e: e.tensor_tensor(out=t8[:], in0=carry, in1=rho4.unsqueeze(1).to_broadcast([128, 2, 4]), op=ALU.mult),
                 reads=["s5c", "s5t", hpk], writes=[t8k])
            P.op("dve", lambda e: e.tensor_tensor(out=S[:, c, :, :, 0], in0=S[:, c, :, :, 0], in1=t8[:], op=ALU.add), reads=[t8k, sk], writes=[sk])
            flat = S[:, c].rearrange("p i r t -> p (i r t)")
            P.op("dve", lambda e: e.tensor_tensor_scan(out=flat, data0=rz[:].rearrange("p i r t -> p (i r t)"), data1=flat,
                                                       initial=0.0, op0=ALU.mult, op1=ALU.add), reads=[sk, rzk], writes=[sk])

        def s5_rot_out(G, c, tt, last):
            gp = G % 2
            S, Tm = s5S[gp], s5t2[gp]
            sk, tk = "s5S%d" % gp, "s5t2%d" % gp
            hc = hch[gp]
            hk = "hch%d" % gp
            sl = slice(c * 64, (c + 1) * 64)
            co, si = cosT[:, 4 * G:4 * G + 4, :], sinT[:, 4 * G:4 * G + 4, :]
            RK = [sk, tk, hk, "cosT", "sinT"]
            E = "pool"
            Tc = Tm[:, :, sl]
            P.op(E, lambda e: e.tensor_tensor(out=hc[:, 0], in0=S[:, c, 0], in1=co, op=ALU.mult), reads=RK, writes=[hk])
            P.op(E, lambda e: e.tensor_tensor(out=Tc, in0=S[:, c, 1], in1=si, op=ALU.mult), reads=RK, writes=[tk])
            P.op(E, lambda e: e.tensor_tensor(out=hc[:, 0], in0=hc[:, 0], in1=Tc, op=ALU.subtract), reads=RK, writes=[hk])
            P.op(E, lambda e: e.tensor_tensor(out=hc[:, 1], in0=S[:, c, 1], in1=co, op=ALU.mult), reads=RK, writes=[hk])
            P.op(E, lambda e: e.tensor_tensor(out=Tc, in0=S[:, c, 0], in1=si, op=ALU.mult), reads=RK, writes=[tk])
            P.op(E, lambda e: e.tensor_tensor(out=hc[:, 1], in0=hc[:, 1], in1=Tc, op=ALU.add), reads=RK, writes=[hk])
            hb = hbf[gp]
            hbk = "hbf%d" % gp
            for ri in range(2):
                P.op("act", lambda e, ri=ri: e.copy(out=hb[:, :, ri, sl], in_=hc[:, ri]), reads=[hk], writes=[hbk])
            if last:
                for ri in range(2):
                    P.op(E, lambda e, ri=ri: e.tensor_copy(out=s5c[:, 4 * G:4 * G + 4, ri], in_=hc[:, ri, :, 63]), reads=[hk], writes=["s5c"])

        def s5_half_g(tt, par, gp):
            nch = tt // 64
            for G in range(gp, 8, 2):
                s5_rot_in(G, tt, par)
                yield
                for c in range(nch):
                    s5_scan_chunk(G, c)
                    yield
                    s5_rot_out(G, c, tt, c == nch - 1)
                    yield
                s5_readout(G, tt, hbf[gp], "hbf%d" % gp, None, par)
                yield

        def s5_prompt_g(tt, par=0):
            live = [s5_half_g(tt, par, 0), s5_half_g(tt, par, 1)]
            while live:
                for g in list(live):
                    try:
                        next(g)
                        yield
                    except StopIteration:
                        live.remove(g)
            yield from s5_glu_g(tt, par)

        def s5_readout(G, tt, hb, hbk, hsel=None, par=0):
            (uT,) = [PB[par][n_] for n_ in ("uT",)]
            kk = lambda n_: n_ if par == 0 else n_ + "1"
            py, pky = next_ps()

            def mm(e):
                last = None
                for r in range(4):
                    for ri in range(2):
                        rhs = hb[:, r, ri, 0:tt] if hsel is None else hsel(r, ri)
                        last = e.matmul(py[32 * r:32 * r + 32, 0:tt], lhsT=wd_[:, 4 * G + r, ri, :], rhs=rhs,
                                        start=(ri == 0), stop=(ri == 1), tile_position=(0, 32 * r))
                return last

            P.op("pe", mm, reads=["wd_", hbk], writes=[pky])
            P.op("dve", lambda e: e.scalar_tensor_tensor(out=ys5[:, 0:tt], in0=uT[:, G, 0:tt], scalar=s5d[:, G:G + 1], op0=ALU.mult,
                                                         in1=py[:, 0:tt], op1=ALU.add), reads=[pky, kk("uT"), "s5d"], writes=["ys5"])
            P.op("act", lambda e: e.activation(out=yaT[:, G, 0:tt], in_=ys5[:, 0:tt], func=AF.Gelu_apprx_tanh), reads=["ys5"], writes=["yaT"])

        def s5_glu(tt):
            drain(s5_glu_g(tt))

        def s5_glu_g(tt, par=0):
            (yaoT,) = [PB[par][n_] for n_ in ("yaoT",)]
            kk = lambda n_: n_ if par == 0 else n_ + "1"
            def ev(pt, pk, ct):
                P.op("act", lambda e: e.activation(out=ys5[:, 0:tt], in_=pt[:, 0:tt], func=AF.Sigmoid, bias=bglu[:, ct:ct + 1], scale=1.0),
                     reads=[pk, "bglu"], writes=["ys5"])
                P.op("dve", lambda e: e.tensor_tensor(out=yaoT[:, ct, 0:tt], in0=yaT[:, ct, 0:tt], in1=ys5[:, 0:tt], op=ALU.mult),
                     reads=["ys5", "yaT"], writes=[kk("yaoT")])

            yield from linear_fm_g(yaT, "yaT", D, dr["s5_w_glu"], 0, D, tt, ev)


        def ssd_dt(rows, ti, par=0):
            (dtr,) = [PB[par][n_] for n_ in ("dtr",)]
            kk = lambda n_: n_ if par == 0 else n_ + "1"
            P.op("dve", lambda e: e.tensor_tensor(out=sm[0:rows, 7, :], in0=dtr[0:rows, ti, :], in1=dtb[0:rows, :], op=ALU.add),
                 reads=[kk("dtr"), "dtb"], writes=["sm"])
            P.op("act", lambda e: e.activation(out=sm[0:rows, 7, :], in_=sm[0:rows, 7, :], func=AF.Exp), reads=["sm"], writes=["sm"])
            P.op("act", lambda e: e.activation(out=sm[0:rows, 0, :], in_=sm[0:rows, 7, :], func=AF.Ln, bias=1.0, scale=1.0),
                 reads=["sm"], writes=["sm"])
            P.op("dve", lambda e: e.tensor_tensor(out=sm[0:rows, 1, :], in0=sm[0:rows, 0, :], in1=arow[0:rows, :], op=ALU.mult),
                 reads=["sm", "arow"], writes=["sm"])

        def conv_prompt_g(tt, par=0):
            (xbcT,) = [PB[par][n_] for n_ in ("xbcT",)]
            kk = lambda n_: n_ if par == 0 else n_ + "1"
            for ct in range(24):
                t_ = cv[ct % 2]
                tk = "cv%d" % (ct % 2)
                P.op("dve", lambda e, ct=ct, t_=t_: e.tensor_scalar(out=t_[:, 0:tt], in0=xbcT[:, ct, 0:tt], scalar1=cw[:, 0, ct:ct + 1],
                                                                    scalar2=None, op0=ALU.mult), reads=[kk("xbcT"), "cw"], writes=[tk])
                for k in range(1, 4):
                    P.op("dve", lambda e, ct=ct, t_=t_, k=k: e.scalar_tensor_tensor(out=t_[:, 0:tt], in0=xbcT[:, ct, k:k + tt],
                                                                                    scalar=cw[:, k, ct:ct + 1], op0=ALU.mult,
                                                                                    in1=t_[:, 0:tt], op1=ALU.add),
                         reads=[kk("xbcT"), "cw", tk], writes=[tk])
                P.op("act", lambda e, ct=ct, t_=t_: e.activation(out=actT[:, ct, 0:tt], in_=t_[:, 0:tt], func=AF.Silu, bias=cb[:, ct:ct + 1], scale=1.0),
                     reads=[tk, "cb"], writes=["actT"])
                yield

        def gate_norm_out(rows, ti, tok0):
            drain(gate_norm_out_g(rows, ti, tok0))

        def gate_norm_out_g(rows, ti, tok0, par=0):
            zs, yBT = [PB[par][n_] for n_ in ("zs", "yBT",)]
            kk = lambda n_: n_ if par == 0 else n_ + "1"
            P.op("dve", lambda e: e.tensor_tensor(out=ytm[0:rows, :], in0=ytm[0:rows, :], in1=zs[0:rows, ti, :], op=ALU.mult),
                 reads=["ytm", kk("zs")], writes=["ytm"])
            P.op("act", lambda e: e.activation(out=yBtm[0:rows, :], in_=ytm[0:rows, :], func=AF.Square, accum_out=st1[0:rows, 3:4]),
                 reads=["ytm"], writes=["yBtm", "st1b"])
            P.op("act", lambda e: e.activation(out=st1[0:rows, 7:8], in_=st1[0:rows, 3:4], func=AF.Sqrt, scale=1.0 / 2048, bias=EPS),
                 reads=["st1b"], writes=["st1b"])
            P.op("dve", lambda e: e.reciprocal(out=st1[0:rows, 3:4], in_=st1[0:rows, 7:8]), reads=["st1b"], writes=["st1b"])
            P.op("dve", lambda e: e.scalar_tensor_tensor(out=yBtm[0:rows, :], in0=ytm[0:rows, :], scalar=st1[0:rows, 3:4], op0=ALU.mult,
                                                         in1=mnorm[0:rows, :], op1=ALU.mult), reads=["ytm", "st1b", "mnorm"], writes=["yBtm"])
            yield
            yield from transpose_to_g(lambda i0, cnt: yBT[:, i0:i0 + cnt, tok0:tok0 + rows], kk("yBT"),
                                      lambda i: yBtm[0:rows, i * 128:(i + 1) * 128], "yBtm", 16, rows, 128, BF16)

        def ssd_prompt_g(tt, par=0):
            kk = lambda n_: n_ if par == 0 else n_ + "1"
            yield from conv_prompt_g(tt, par)
            for c in range(tt // 128):
                cs_ = slice(c * 128, (c + 1) * 128)
                yield from transpose_to_g(lambda i0, cnt: xtm[:, i0 * 128:(i0 + cnt) * 128].rearrange("p (a r) -> p a r", r=128), "xtm",
                                          lambda i: actT[:, i, cs_], "actT", 16, 128, 128, BF16)
                yield from transpose_to_g(lambda i0, cnt: btm[:, i0:i0 + cnt, :], "btm", lambda i: actT[:, 16 + i, cs_], "actT", 4, 128, 128, BF16)
                ssd_dt(128, c, par)
                yield
                pt, pk = next_ps()
                P.op("pe", lambda e, pt=pt: e.matmul(pt[:, 0:32], lhsT=tri[:], rhs=sm[:, 1, :], start=True, stop=True),
                     reads=["tri", "sm"], writes=[pk])
                P.op("pe", lambda e, pt=pt: e.matmul(pt[:, 32:64], lhsT=ones[:], rhs=sm[:, 1, :], start=True, stop=True),
                     reads=["ones", "sm", pk], writes=[pk])
                P.op("act", lambda e, pt=pt: e.copy(out=sm[:, 2:4, :], in_=pt[:, 0:64].rearrange("p (a b) -> p a b", b=32)),
                     reads=[pk], writes=["sm"])
                P.op("act", lambda e: e.activation(out=sm[:, 4, :], in_=sm[:, 2, :], func=AF.Exp), reads=["sm"], writes=["sm"])
                P.op("act", lambda e: e.activation(out=sm[:, 6, :], in_=sm[:, 3, :], func=AF.Exp), reads=["sm"], writes=["sm"])
                P.op("dve", lambda e: e.tensor_tensor(out=sm[:, 7, :], in0=sm[:, 3, :], in1=sm[:, 2, :], op=ALU.subtract), reads=["sm"], writes=["sm"])
                P.op("act", lambda e: e.activation(out=sm[:, 7, :], in_=sm[:, 7, :], func=AF.Exp), reads=["sm"], writes=["sm"])
                P.op("dve", lambda e: e.tensor_tensor(out=sm[:, 5, :], in0=sm[:, 7, :], in1=sm[:, 0, :], op=ALU.mult), reads=["sm"], writes=["sm"])
                yield
                for g in range(4):
                    hs = slice(8 * g, 8 * g + 8)
                    P.op("dve", lambda e, hs=hs: e.tensor_tensor(out=Rb[:], in0=tri[:].unsqueeze(1).to_broadcast([128, 8, 128]),
                                                                 in1=sm[:, 1, hs].unsqueeze(2).to_broadcast([128, 8, 128]), op=ALU.mult),
                         reads=["tri", "sm"], writes=["Rb"])
                    for hh in range(2):
                        pa, pka = next_ps()
                        P.op("pe", lambda e, pa=pa, hh=hh: e.matmul(pa[:, :], lhsT=su[:], rhs=Rb[:, 4 * hh:4 * hh + 4, :].rearrange("p a b -> p (a b)"),
                                                                    start=True, stop=True), reads=["su", "Rb"], writes=[pka])
                        P.op("act", lambda e, pa=pa, hh=hh: e.activation(out=LT[:, 4 * hh:4 * hh + 4, :].rearrange("p a b -> p (a b)"),
                                                                         in_=pa[:, :], func=AF.Exp), reads=[pka], writes=["LT"])
                    pc, pkc = next_ps()
                    P.op("pe", lambda e, pc=pc, g=g: e.matmul(pc[:, 0:128], lhsT=actT[:, 16 + g, cs_], rhs=actT[:, 20 + g, cs_], start=True, stop=True),
                         reads=["actT"], writes=[pkc])
                    P.op("dve", lambda e, pc=pc: e.tensor_tensor(out=CBm[:], in0=pc[:, 0:128], in1=tri[:], op=ALU.mult), reads=[pkc, "tri"], writes=["CBm"])
                    P.op("dve", lambda e: e.tensor_tensor(out=MT[:], in0=LT[:], in1=CBm[:].unsqueeze(1).to_broadcast([128, 8, 128]), op=ALU.mult),
                         reads=["LT", "CBm"], writes=["MT"])
                    xg = xtm[:, 512 * g:512 * g + 512].rearrange("p (j d) -> p j d", d=64)
                    P.op("dve", lambda e, hs=hs, xg=xg: e.tensor_tensor(out=xdt[:].rearrange("p (j d) -> p j d", d=64), in0=xg,
                                                                        in1=sm[:, 0, hs].unsqueeze(2).to_broadcast([128, 8, 64]), op=ALU.mult),
                         reads=["xtm", "sm"], writes=["xdt"])
                    P.op("dve", lambda e, hs=hs, xg=xg: e.tensor_tensor(out=X2[:].rearrange("p (j d) -> p j d", d=64), in0=xg,
                                                                        in1=sm[:, 5, hs].unsqueeze(2).to_broadcast([128, 8, 64]), op=ALU.mult),
                         reads=["xtm", "sm"], writes=["X2"])
                    pyd, pkyd = next_ps()

                    def ydm(e, pyd=pyd, g=g):
                        last = None
                        for j in range(8):
                            last = e.matmul(pyd[:, 64 * j:64 * j + 64], lhsT=MT[:, j, :], rhs=xdt[:, 64 * j:64 * j + 64], start=True, stop=True)
                        return last

                    P.op("pe", ydm, reads=["MT", "xdt"], writes=[pkyd])
                    pyo, pkyo = next_ps()
                    P.op("pe", lambda e, pyo=pyo, g=g: e.matmul(pyo[:, :], lhsT=actT[:, 20 + g, cs_], rhs=hTb[:, g, :], start=True, stop=True),
                         reads=["actT", "hTb"], writes=[pkyo])
                    P.op("dve", lambda e, pyo=pyo, hs=hs: e.tensor_tensor(out=yt1[:].rearrange("p (j d) -> p j d", d=64),
                                                                          in0=pyo[:, :].rearrange("p (j d) -> p j d", d=64),
                                                                          in1=sm[:, 4, hs].unsqueeze(2).to_broadcast([128, 8, 64]), op=ALU.mult),
                         reads=[pkyo, "sm"], writes=["yt1"])
                    P.op("dve", lambda e, pyd=pyd, g=g: e.tensor_tensor(out=ytm[:, 512 * g:512 * g + 512], in0=yt1[:], in1=pyd[:, :], op=ALU.add),
                         reads=[pkyd, "yt1"], writes=["ytm"])
                    P.op("dve", lambda e, hs=hs, xg=xg: e.tensor_tensor(out=yt1[:].rearrange("p (j d) -> p j d", d=64), in0=xg,
                                                                        in1=mdr[:, hs].unsqueeze(2).to_broadcast([128, 8, 64]), op=ALU.mult),
                         reads=["xtm", "mdr", "ytm"], writes=["yt1"])
                    P.op("dve", lambda e, g=g: e.tensor_tensor(out=ytm[:, 512 * g:512 * g + 512], in0=ytm[:, 512 * g:512 * g + 512], in1=yt1[:], op=ALU.add),
                         reads=["yt1", "ytm"], writes=["ytm"])
                    pst, pkst = next_ps()
                    P.op("pe", lambda e, pst=pst, g=g: e.matmul(pst[:, :], lhsT=btm[:, g, :], rhs=X2[:], start=True, stop=True),
                         reads=["btm", "X2"], writes=[pkst])
                    hv = hT[:, g, :].rearrange("p (j d) -> p j d", d=64)
                    P.op("dve", lambda e, hv=hv, hs=hs: e.tensor_tensor(out=hv, in0=hv, in1=sm[:, 6, hs].unsqueeze(2).to_broadcast([128, 8, 64]), op=ALU.mult),
                         reads=["hT", "sm"], writes=["hT"])
                    P.op("dve", lambda e, pst=pst, g=g: e.tensor_tensor(out=hT[:, g, :], in0=hT[:, g, :], in1=pst[:, :], op=ALU.add),
                         reads=["hT", pkst], writes=["hT"])
                    P.op("act", lambda e, g=g: e.copy(out=hTb[:, g, :], in_=hT[:, g, :]), reads=["hT"], writes=["hTb"])
                    yield
                yield from gate_norm_out_g(128, c, c * 128, par)

        def in_proj(*a, **k):
            drain(in_proj_g(*a, **k))

        def in_proj_g(tt, tiles, want_xbc_tm, par=0, prompt=False):
            uT, zs, xbcT, dtr, gaT, gbT = [PB[par][n_] for n_ in ("uT", "zs", "xbcT", "dtr", "gaT", "gbT")]
            kk = lambda n_: n_ if par == 0 else n_ + "1"
            def ev_u(pt, pk, ct):
                copy_evac(uT[:, ct, 0:tt], kk("uT"), pt[:, 0:tt], pk)

            yield from linear_fm_g(hT_, "hT_", D, dr["w_in"], 0, 1024, tt, ev_u)

            def ev_z(pt, pk, ti, cc, cwid):
                rows = tiles[ti][1]
                P.op("act", lambda e: e.activation(out=zs[0:rows, ti, cc:cc + cwid], in_=pt[0:rows, 0:cwid], func=AF.Silu), reads=[pk], writes=[kk("zs")])

            yield from linear_tm_g(hT_, "hT_", D, dr["w_in"], OFF_Z, 2048, tiles, ev_z)

            def ev_x(pt, pk, ct):
                copy_evac(xbcT[:, ct, 3:3 + tt], kk("xbcT"), pt[:, 0:tt], pk)
                if want_xbc_tm:
                    P.op("dve", lambda e: e.tensor_copy(out=xtail[:, ct, 0:3], in_=pt[:, tt - 3:tt]), reads=[pk, kk("xbcT")], writes=["xtail"])
                if "xs32" in SAMPLE:
                    P.op("dve", lambda e: e.tensor_copy(out=SAMPLE["xs32"][:, ct, :], in_=pt[:, 0:tt]), reads=[pk, kk("xbcT")], writes=["xs32"])

            yield from linear_fm_g(hT_, "hT_", D, dr["w_in"], OFF_XBC, 3072, tt, ev_x)
            if prompt:
                ox = PB[1 - par]["xbcT"]
                okey = "xbcT" if par == 1 else "xbcT1"
                P.op("act", lambda e: e.copy(out=xbcT[:, :, 0:3], in_=ox[:, :, tt:tt + 3]), reads=[okey], writes=[kk("xbcT")])
            def ev_dt(pt, pk, ti, cc, cwid):
                rows = tiles[ti][1]
                copy_evac(dtr[0:rows, ti, :], kk("dtr"), pt[0:rows, 0:32], pk)

            yield from linear_tm_g(hT_, "hT_", D, dr["w_in"], OFF_DT, 32, tiles, ev_dt)

            def ev_g(dst, dk):
                flat = dst[:].rearrange("p a b -> p (a b)")

                def f(pt, pk, ti, cc, cwid):
                    rows = tiles[ti][1]
                    P.op("act", lambda e: e.activation(out=flat[0:rows, cc:cc + cwid], in_=pt[0:rows, 0:cwid], func=AF.Sigmoid), reads=[pk], writes=[dk])
                return f

            yield from linear_tm_g(hT_, "hT_", D, dr["w_in"], OFF_GA, 1024, tiles, ev_g(gaT, kk("gaT")))
            yield from linear_tm_g(hT_, "hT_", D, dr["w_in"], OFF_GB, 1024, tiles, ev_g(gbT, kk("gbT")))

        def merge_ffn(*a, **k):
            drain(merge_ffn_g(*a, **k))

        def merge_ffn_g(tt, tiles, mt, mkey, y_dram, par=0, x_src=None, prompt=False):
            jG1, jG2 = (0, 1) if prompt else (2, 5)
            yaoT, yBT, gaT, gbT = [PB[par][n_] for n_ in ("yaoT", "yBT", "gaT", "gbT")]
            kk = lambda n_: n_ if par == 0 else n_ + "1"
            if x_src is not None:
                for ti, (t0, rows) in enumerate(tiles):
                    P.dma("act", x_tm[0:rows, ti, :], x_src[t0:t0 + rows, :], "xin", reads=["yout"], writes=["x_tm"])
            ga_f = gaT[:].rearrange("p a b -> p (a b)")
            gb_f = gbT[:].rearrange("p a b -> p (a b)")
            mrg_f = mrg[:].rearrange("p a b -> p (a b)")
            mrgT_f = mrgT[:].rearrange("p a b -> p (a b)")

            def ev_a(pt, pk, ti, cc, cwid):
                rows = tiles[ti][1]
                P.op("dve", lambda e: e.tensor_tensor(out=mrg_f[0:rows, cc:cc + cwid], in0=pt[0:rows, 0:cwid], in1=ga_f[0:rows, cc:cc + cwid], op=ALU.mult),
                     reads=[pk, kk("gaT")], writes=["mrg"])

            yield from linear_tm_g(yaoT, kk("yaoT"), D, dr["w_branch_s5"], 0, D, tiles, ev_a)

            def ev_b(pt, pk, ti, cc, cwid):
                rows = tiles[ti][1]
                P.op("dve", lambda e: e.tensor_tensor(out=wk2[0:rows, 0:cwid], in0=pt[0:rows, 0:cwid], in1=gb_f[0:rows, cc:cc + cwid], op=ALU.mult),
                     reads=[pk, kk("gbT")], writes=["wk2"])
                P.op("dve", lambda e: e.tensor_tensor(out=mrgT_f[0:rows, cc:cc + cwid], in0=wk2[0:rows, 0:cwid], in1=mrg_f[0:rows, cc:cc + cwid], op=ALU.add),
                     reads=["wk2", "mrg"], writes=["mrgT"])

            yield from linear_tm_g(yBT, kk("yBT"), 2048, dr["w_branch_ssd"], 0, D, tiles, ev_b)
            for ti, (t0, rows) in enumerate(tiles):
                yield from transpose_to_g(lambda i0, cnt, t0=t0, rows=rows: mrg[:, i0:i0 + cnt, t0:t0 + rows], "mrg",
                                          lambda i, rows=rows: mrgT_f[0:rows, i * 128:(i + 1) * 128], "mrgT", 8, rows, 128, BF16)
            yield from linear_tm_seq_g(mrg, "mrg", D, dr["w_out"], D, tiles, jG1, mt, mkey)
            for ti, (t0, rows) in enumerate(tiles):
                if prompt:
                    rms_mod_p(4, 3)
                else:
                    rms_mod(ti, rows, mt, 4, 3, mkey, t0)

            def ev_gate(pt, pk, ct):
                if ct < 22:
                    P.op("act", lambda e: e.activation(out=factT[:, ct, 0:tt], in_=pt[:, 0:tt], func=AF.Silu), reads=[pk], writes=["factT"])
                else:
                    P.op("dve", lambda e: e.tensor_tensor(out=factT[:, ct - 22, 0:tt], in0=pt[:, 0:tt], in1=factT[:, ct - 22, 0:tt], op=ALU.mult),
                         reads=[pk, "factT"], writes=["factT"])

            yield from linear_fm_g(hT_, "hT_", D, dr["w_ffn_in"], 0, 2 * D_FF, tt, ev_gate)
            yield from linear_tm_seq_g(factT, "factT", D_FF, dr["w_ffn_out"], D, tiles, jG2, mt, mkey)
            for ti, (t0, rows) in enumerate(tiles):
                P.dma("act", y_dram[t0:t0 + rows, :], x_tm[0:rows, ti, :], "yout", reads=["x_tm"], writes=["yout"])

        wk2b = [wk2]

        def linear_tm_seq_g(inT, in_key, K, wd, ncols, tiles, jG, mt, mkey):
            def ev(pt, pk, ti, cc, cwid):
                rows = tiles[ti][1]
                copy_evac(wk2b[ti][0:rows, cc:cc + cwid], "wk2", pt[0:rows, 0:cwid], pk)

            yield from linear_tm_g(inT, in_key, K, wd, 0, ncols, tiles, ev)
            for ti, (t0, rows) in enumerate(tiles):
                resid_gate(wk2b[ti][0:rows, :], "wk2", ti, rows, mt, jG, mkey)

        tiles_p = [(0, 128)]

        def dense_in_g(si):
            P.dma("act", x_tm[:, 0, :], dr["xp"][si * T:(si + 1) * T, :], "xin", reads=["yout"], writes=["x_tm"])
            rms_mod_p(1, 0)
            yield
            yield from in_proj_g(T, tiles_p, want_xbc_tm=(si == NST_RUN - 1), par=si % 2, prompt=True)

        def dense_out_g(si):
            yield from merge_ffn_g(T, tiles_p, modp, "modp", dr["y_p"][si * T:(si + 1) * T, :], par=si % 2,
                                   x_src=dr["xp"][si * T:(si + 1) * T, :], prompt=True)

        def chain(*gs):
            for g in gs:
                yield from g

        drain(dense_in_g(0))
        for si in range(NST_RUN):
            dg_ = []
            if si >= 1:
                dg_.append(dense_out_g(si - 1))
            if si + 1 < NST_RUN:
                dg_.append(dense_in_g(si + 1))
            interleave([chain(*dg_), s5_prompt_g(T, si % 2), ssd_prompt_g(T, si % 2)], ILW)
        drain(dense_out_g(NST_RUN - 1))
        if STOP == "p_1":
            return finish()
        if STOP == "pb1":
            for n_ in ("yaoT", "gaT", "gbT", "uT"):
                dump(n_, PB[1][n_][:].rearrange("p a b -> p (a b)"), n_ + "1", [128, 8 * T])
            dump("yBT", PB[1]["yBT"][:].rearrange("p a b -> p (a b)"), "yBT1", [128, 16 * T])
            dump("zs", PB[1]["zs"][:, 0, :], "zs1", [128, 2048])
            return finish()

        for ri, nm in enumerate(("s5re_p", "s5im_p")):
            pt, pk = next_ps()
            P.op("pe", lambda e, pt=pt, ri=ri: e.transpose(pt[0:32, 0:128], s5c[:, :, ri], ident[:]), reads=["s5c", "ident"], writes=[pk])
            P.op("act", lambda e, pt=pt, ri=ri: e.copy(out=wk2[0:32, ri * 128:(ri + 1) * 128], in_=pt[0:32, 0:128]), reads=[pk], writes=["wk2"])
            P.dma("sp", dr[nm], wk2[0:32, ri * 128:(ri + 1) * 128], "o_" + nm, reads=["wk2"], writes=["o_" + nm])
        if STOP == "e1":
            return finish()
        hout = ytm[:, :].rearrange("p (a b) -> p a b", b=128)
        transpose_to(lambda i0, cnt: hout[:, i0:i0 + cnt, :], "ytm",
                     lambda i: hT[:, i // 4, (i % 4) * 128:(i % 4 + 1) * 128], "hT", 16, 128, 128, F32)
        P.dma("sp", dr["ssm_p"].rearrange("(a p) n -> p a n", p=128), hout, "o_ssm_p", reads=["ytm"], writes=["o_ssm_p"])
        if STOP == "e2":
            return finish()
        for i0 in range(0, 24, 4):
            pt, pk = next_ps()

            def trx(e, pt=pt, i0=i0):
                last = None
                for a in range(4):
                    last = e.transpose(pt[0:3, a * 128:(a + 1) * 128], xtail[:, i0 + a, 0:3], ident[:])
                return last

            P.op("pe", trx, reads=["xtail", "ident"], writes=[pk])
            P.op("act", lambda e: e.copy(out=cps[0:3, :], in_=pt[0:3, :]), reads=[pk], writes=["yt1"])
            P.dma("sp", dr["conv_p"][:, i0 * 128:(i0 + 4) * 128], cps[0:3, :], "o_conv_p", reads=["yt1"], writes=["o_conv_p"])
        if STOP == "prompt_only":
            return finish()
        P.emit_phase()
        tiles_s = [(0, NS)]
        bump["lo"] = M1
        bump["hi"] = ARW
        mods = ssb("mods", [NS, 6, D], BF16)
        xs32 = ssb("xs32", [128, 24, NS], F32)
        ada_compute(sb, False, mods)
        P.emit_phase()
        bump["lo"] = M1
        P.dma("pool", x_tm[0:NS, 0, :], dr["xs"], "xin", writes=["x_tm"])
        rms_mod(0, NS, mods, 1, 0, "mods", 0)
        SAMPLE["xs32"] = xs32
        in_proj(NS, tiles_s, want_xbc_tm=False)
        P.emit_phase()
        bump["lo"] = M1
        hst = cosT[:].rearrange("p a b -> p (a b)").rearrange("p (a b) -> p a b", b=128)
        tmp3 = sinT[:].rearrange("p a b -> p (a b)").rearrange("p (a b) -> p a b", b=128)
        stg = sb("stg", [48, 4096], F32)
        h0T = sb("h0T", [128, 2, 32, NS], F32)
        for ri, nm in enumerate(("s5re_in", "s5im_in")):
            P.dma("sp", stg[0:NS, :], dr[nm], "stg", writes=["stg"])
            transpose_to(lambda i0, cnt, ri=ri: h0T[:, ri, i0:i0 + cnt, :], "h0T",
                         lambda i: stg[0:NS, i * 128:(i + 1) * 128], "stg", 32, NS, 128, F32)
        pbr, pkbr = next_ps()
        pbi, pkbi = next_ps()

        for G in range(8):
            s5_load_u(G, NS)

            def bus(e, G=G):
                last = None
                for r in range(4):
                    q = 4 * G + r
                    e.matmul(pbr[:, q * NS:(q + 1) * NS], lhsT=wb[:, G, 0, :], rhs=uz[:, r, 0:NS], start=True, stop=True)
                    last = e.matmul(pbi[:, q * NS:(q + 1) * NS], lhsT=wb[:, G, 1, :], rhs=uz[:, r, 0:NS], start=True, stop=True)
                return last

            P.op("pe", bus, reads=["wb", "uz", pkbr, pkbi], writes=[pkbr, pkbi])
        hn5 = sb("hn5", [128, 2, 32, NS], F32)
        t5 = sb("t5", [128, 2, 32, NS], F32)
        arb = s5t[:, AR, :].unsqueeze(2).to_broadcast([128, 32, NS])
        aib = s5t[:, AI, :].unsqueeze(2).to_broadcast([128, 32, NS])
        K5 = ["h0T", "s5t", "t5", "hn5", pkbr, pkbi]
        pv5 = lambda p_: p_[:, 0:32 * NS].rearrange("p (q b) -> p q b", b=NS)
        P.op("dve", lambda e: e.tensor_tensor(out=t5[:, 0], in0=h0T[:, 0], in1=arb, op=ALU.mult), reads=K5, writes=["t5"])
        P.op("dve", lambda e: e.tensor_tensor(out=t5[:, 1], in0=h0T[:, 1], in1=aib, op=ALU.mult), reads=K5, writes=["t5"])
        P.op("dve", lambda e: e.tensor_tensor(out=t5[:, 0], in0=t5[:, 0], in1=t5[:, 1], op=ALU.subtract), reads=K5, writes=["t5"])
        P.op("dve", lambda e: e.tensor_tensor(out=hn5[:, 0], in0=t5[:, 0], in1=pv5(pbr), op=ALU.add), reads=K5, writes=["hn5"])
        P.op("dve", lambda e: e.tensor_tensor(out=t5[:, 0], in0=h0T[:, 1], in1=arb, op=ALU.mult), reads=K5, writes=["t5"])
        P.op("dve", lambda e: e.tensor_tensor(out=t5[:, 1], in0=h0T[:, 0], in1=aib, op=ALU.mult), reads=K5, writes=["t5"])
        P.op("dve", lambda e: e.tensor_tensor(out=t5[:, 0], in0=t5[:, 0], in1=t5[:, 1], op=ALU.add), reads=K5, writes=["t5"])
        P.op("dve", lambda e: e.tensor_tensor(out=hn5[:, 1], in0=t5[:, 0], in1=pv5(pbi), op=ALU.add), reads=K5, writes=["hn5"])
        hb5 = sb("hb5", [128, 2, 32, NS], BF16)
        P.op("act", lambda e: e.copy(out=hb5[:].rearrange("p a b c -> p (a b c)"), in_=hn5[:].rearrange("p a b c -> p (a b c)")),
             reads=["hn5"], writes=["hb5"])
        for G in range(8):
            s5_readout(G, NS, None, "hb5", hsel=lambda r, ri, G=G: hb5[:, ri, 4 * G + r, :])
        s5_glu(NS)
        for ri, nm in enumerate(("s5re_s", "s5im_s")):
            transpose_to(lambda i0, cnt: stg[0:NS, i0 * 128:(i0 + cnt) * 128].rearrange("p (a r) -> p a r", r=128), "stg",
                         lambda i, ri=ri: hn5[:, ri, i, :], "hn5", 32, 128, NS, F32)
            P.dma("sp", dr[nm], stg[0:NS, :], "o_" + nm, reads=["stg"], writes=["o_" + nm, "stg"])

        histT = sb("histT", [128, 24, NS * 3], F32)
        P.dma("sp", stg[0:NS * 3, 0:3072], dr["conv_in"], "stg", writes=["stg"])
        transpose_to(lambda i0, cnt: histT[:, i0:i0 + cnt, :], "histT", lambda i: stg[0:NS * 3, i * 128:(i + 1) * 128], "stg",
                     24, NS * 3, 128, F32)
        actS = sb("actS", [128, 24, NS], F32)
        for ct in range(24):
            t_ = cv[ct % 2]
            tk = "cv%d" % (ct % 2)
            hv_ = histT[:, ct, :].rearrange("p (b k) -> p b k", k=3)
            P.op("dve", lambda e, ct=ct, t_=t_: e.tensor_scalar(out=t_[:, 0:NS], in0=xs32[:, ct, :], scalar1=cw[:, 3, ct:ct + 1], scalar2=None,
                                                                op0=ALU.mult), reads=["xs32", "cw"], writes=[tk])
            for k in range(3):
                P.op("dve", lambda e, ct=ct, t_=t_, k=k, hv_=hv_: e.scalar_tensor_tensor(out=t_[:, 0:NS], in0=hv_[:, :, k], scalar=cw[:, k, ct:ct + 1],
                                                                                         op0=ALU.mult, in1=t_[:, 0:NS], op1=ALU.add),
                     reads=["histT", "cw", tk], writes=[tk])
            P.op("act", lambda e, ct=ct, t_=t_: e.activation(out=actS[:, ct, :], in_=t_[:, 0:NS], func=AF.Silu, bias=cb[:, ct:ct + 1], scale=1.0),
                 reads=[tk, "cb"], writes=["actS"])
        P.dma("sp", dr["conv_s"][:, 0:2, :], dr["conv_in"].rearrange("(b k) c -> b k c", k=3)[:, 1:3, :], "o_conv_s", writes=["o_conv_s"])
        transpose_to(lambda i0, cnt: stg[0:NS, i0 * 128:(i0 + cnt) * 128].rearrange("p (a r) -> p a r", r=128), "stg",
                     lambda i: xs32[:, i, :], "xs32", 24, 128, NS, F32)
        P.dma("sp", dr["conv_s"][:, 2, :], stg[0:NS, 0:3072], "o_conv_s2", reads=["stg"], writes=["o_conv_s2", "stg"])

        ssd_dt(NS, 0)
        P.op("act", lambda e: e.activation(out=sm[0:NS, 2, :], in_=sm[0:NS, 1, :], func=AF.Exp), reads=["sm"], writes=["sm"])
        dfm = sb("dfm", [32, 2, NS], F32)
        for k, col in enumerate((0, 2)):
            pt, pk = next_ps()
            P.op("pe", lambda e, pt=pt, col=col: e.transpose(pt[0:32, 0:NS], sm[0:NS, col, :], ident[0:NS, 0:NS]), reads=["sm", "ident"], writes=[pk])
            P.op("act", lambda e, pt=pt, k=k: e.copy(out=dfm[:, k, :], in_=pt[0:32, 0:NS]), reads=[pk], writes=["dfm"])
        esel = [sb("esel%d" % i, [32, 128], F32) for i in range(2)]
        dex = sb("dex", [128, 16, 2, NS], F32)
        dexp = sb("dexp", [128, 16], F32)
        for hl in range(2):
            P.dma("sp", dexp[64 * hl:64 * hl + 64, :], dr["m_d"][0, :].rearrange("(hp hl) -> hl hp", hl=2)[hl, :].partition_broadcast(64),
                  "dexp", writes=["dexp"])
        for hp in range(16):
            es = esel[hp % 2]
            ek = "esel%d" % (hp % 2)
            P.op("dve", lambda e, es=es, hp=hp: e.tensor_copy(out=es[:].rearrange("h (b c) -> h b c", c=64),
                                                              in_=ident[0:32, 2 * hp:2 * hp + 2].unsqueeze(2).to_broadcast([32, 2, 64])),
                 reads=["ident"], writes=[ek])
            pt, pk = next_ps()
            P.op("pe", lambda e, pt=pt, es=es: e.matmul(pt[:, 0:2 * NS], lhsT=es[:], rhs=dfm[:].rearrange("h a b -> h (a b)"), start=True, stop=True),
                 reads=[ek, "dfm"], writes=[pk])
            copy_evac(dex[:, hp, :, :], "dex", pt[:, 0:2 * NS].rearrange("p (a b) -> p a b", b=NS), pk)
        dtx = sb("dtx", [128, 16, NS], F32)
        P.op("dve", lambda e: e.tensor_tensor(out=dtx[:], in0=dex[:, :, 0, :], in1=actS[:, 0:16, :], op=ALU.mult), reads=["dex", "actS"], writes=["dtx"])
        ysT = sb("ysT", [128, 16, NS], F32)
        dg = sb("dg", [128, 8, 128], F32)
        bcb = sb("bcb", [128, 8, 128], F32)
        red = sb("red", [128, 16], F32)
        print("sample arena end=%d of %d" % (bump["lo"], ARW))
        for b in range(NS):
            hs_ = hst
            hk = "hst"
            P.dma("sp", hs_, dr["ssm_in"][b].rearrange("(hp hl) p n -> (hl p) hp n", hl=2), hk, reads=["o_hst"], writes=[hk])
            for k in range(8):
                P.op("dve", lambda e, k=k, b=b: e.tensor_scalar(out=dg[:, k, :], in0=ident[:], scalar1=actS[:, 16 + k, b:b + 1], scalar2=None, op0=ALU.mult),
                     reads=["ident", "actS"], writes=["dg"])
            for hh in range(2):
                pt, pk = next_ps()
                P.op("pe", lambda e, pt=pt, hh=hh: e.matmul(pt[:, :], lhsT=ones[:], rhs=dg[:, 4 * hh:4 * hh + 4, :].rearrange("p a b -> p (a b)"),
                                                            start=True, stop=True), reads=["ones", "dg"], writes=[pk])
                copy_evac(bcb[:, 4 * hh:4 * hh + 4, :].rearrange("p a b -> p (a b)"), "bcb", pt[:, :], pk)
            P.op("dve", lambda e, b=b: e.tensor_tensor(out=hs_, in0=hs_, in1=dex[:, :, 1, b:b + 1].to_broadcast([128, 16, 128]), op=ALU.mult),
                 reads=[hk, "dex"], writes=[hk])
            P.op("dve", lambda e, b=b: e.tensor_tensor(out=tmp3.rearrange("p (g r) n -> p g r n", r=4),
                                                       in0=bcb[:, 0:4, :].unsqueeze(2).to_broadcast([128, 4, 4, 128]),
                                                       in1=dtx[:, :, b:b + 1].rearrange("p (g r) o -> p g r o", r=4).to_broadcast([128, 4, 4, 128]), op=ALU.mult),
                 reads=["bcb", "dtx"], writes=["tmp3"])
            P.op("dve", lambda e: e.tensor_tensor(out=hs_, in0=hs_, in1=tmp3, op=ALU.add), reads=[hk, "tmp3"], writes=[hk])
            P.dma("sp", dr["ssm_s"][b].rearrange("(hp hl) p n -> (hl p) hp n", hl=2), hs_, "o_hst", reads=[hk], writes=["o_hst"])
            P.op("dve", lambda e: e.tensor_tensor(out=tmp3.rearrange("p (g r) n -> p g r n", r=4),
                                                  in0=hs_.rearrange("p (g r) n -> p g r n", r=4),
                                                  in1=bcb[:, 4:8, :].unsqueeze(2).to_broadcast([128, 4, 4, 128]), op=ALU.mult),
                 reads=[hk, "bcb"], writes=["tmp3"])
            P.op("dve", lambda e: e.tensor_reduce(out=red[:], in_=tmp3, axis=AX.X, op=ALU.add), reads=["tmp3"], writes=["red"])
            P.op("dve", lambda e, b=b: e.tensor_tensor(out=ysT[:, :, b], in0=dexp[:], in1=actS[:, 0:16, b], op=ALU.mult), reads=["dexp", "actS"], writes=["ysT"])
            P.op("dve", lambda e, b=b: e.tensor_tensor(out=ysT[:, :, b], in0=ysT[:, :, b], in1=red[:], op=ALU.add), reads=["red", "ysT"], writes=["ysT"])
        transpose_to(lambda i0, cnt: ytm[0:NS, i0 * 128:(i0 + cnt) * 128].rearrange("p (a r) -> p a r", r=128), "ytm",
                     lambda i: ysT[:, i, :], "ysT", 16, 128, NS, F32)
        gate_norm_out(NS, 0, 0)
        if STOP == "s_mix":
            dump("yaoT", yaoT[:].rearrange("p a b -> p (a b)"), "yaoT", [128, 8 * T])
            dump("yBT", yBT[:].rearrange("p a b -> p (a b)"), "yBT", [128, 16 * T])
            return finish()
        P.emit_phase()
        merge_ffn(NS, tiles_s, mods, "mods", dr["y_s"])

        P.final_wait("sp", [k for k in P.lastw if str(k).startswith(("o_", "yout", "dbg_"))])
        P.emit()
    return nc, dbg_out


_NC_CACHE = {}


def _prep_inputs(inputs):
    f = lambda a: np.ascontiguousarray(np.asarray(a, dtype=np.float32))
    w = {}
    for n, s in WEIGHT_SHAPES.items():
        w[n] = f(inputs[n]).reshape(s)
    maps = []
    for i in range(NCORES):
        m = dict(w)
        sl = slice(NS * i, NS * (i + 1))
        m["xp"] = f(inputs["x_prompt"][i])
        m["xs"] = f(inputs["x_sample"][sl, 0, :])
        m["c17"] = f(np.concatenate([np.asarray(inputs["c_sample"])[sl], np.asarray(inputs["c_prompt"])[i:i + 1]], axis=0))
        m["s5re_in"] = f(np.asarray(inputs["state_s5_re"])[0, sl].reshape(NS, 4096))
        m["s5im_in"] = f(np.asarray(inputs["state_s5_im"])[0, sl].reshape(NS, 4096))
        m["ssm_in"] = f(np.asarray(inputs["state_ssm"])[0, sl])
        m["conv_in"] = f(np.asarray(inputs["state_conv"])[0, sl].reshape(NS * 3, 3072))
        maps.append(m)
    return maps


def kernel(**inputs):
    if "nc" not in _NC_CACHE:
        _NC_CACHE["nc"] = build_nc(tuple(DEBUG.get("names", ())))
    nc, dbg = _NC_CACHE["nc"]
    maps = _prep_inputs(inputs)
    res = run_bass_kernel_spmd(nc, maps, core_ids=list(range(NCORES)))
    R = res.results
    if DEBUG.get("names"):
        DEBUG["out"] = [{k: r["dbg_" + k] for k in dbg} for r in R]
    cat = lambda n: np.stack([np.asarray(R[i][n]) for i in range(NCORES)], 0)
    y_p = cat("y_p").reshape(8, SEQ, D)
    y_s = cat("y_s").reshape(128, 1, D)
    s5re_p = cat("s5re_p").reshape(1, 8, 64, 64)
    s5im_p = cat("s5im_p").reshape(1, 8, 64, 64)
    ssm_p = cat("ssm_p").reshape(1, 8, 32, 64, 128)
    conv_p = cat("conv_p").reshape(1, 8, 3, 3072)
    s5re_s = cat("s5re_s").reshape(1, 128, 64, 64)
    s5im_s = cat("s5im_s").reshape(1, 128, 64, 64)
    ssm_s = cat("ssm_s").reshape(1, 128, 32, 64, 128)
    conv_s = cat("conv_s").reshape(1, 128, 3, 3072)
    return tuple(np.ascontiguousarray(a, dtype=np.float32) for a in
                 (y_p, y_s, s5re_p, s5im_p, ssm_p, conv_p, s5re_s, s5im_s, ssm_s, conv_s))
```

```python
import math
import numpy as np
from contextlib import ExitStack
import concourse.bass as bass
import concourse.mybir as mybir
from concourse.bass_utils import run_bass_kernel_spmd

F32 = mybir.dt.float32
BF16 = mybir.dt.bfloat16
AF = mybir.ActivationFunctionType
ALU = mybir.AluOpType
AX = mybir.AxisListType

ENGS = ("pe", "act", "dve", "pool", "sp")
NCORES = 8
D = 1024
SEQ = 2048
NS = 16
T = 128
NST = SEQ // T
D_FF = 2816
IN_COLS = 8224
OFF_Z, OFF_XBC, OFF_DT, OFF_GA, OFF_GB = 1024, 3072, 6144, 6176, 7200
EPS = 1e-6
TWO_PI = 2.0 * math.pi
DEBUG = {}
SAMPLE = {}
STOP = None
NST_RUN = NST
ILW = [1, 2, 1]
PROBE_WB = False


class Prog:
    def __init__(self, nc, stack):
        self.nc = nc
        self.stack = stack
        self.streams = {e: [] for e in ENGS}
        self.cnt = {e: 0 for e in ENGS}
        self.sems = {e: stack.enter_context(nc.semaphore("s_" + e)) for e in ENGS}
        self.waited = {e: {} for e in ENGS}
        self.lastw = {}
        self.readers = {}
        self.chan_sem = {}
        self.chan_cnt = {}

    def _deps(self, reads, writes):
        ev = []
        for r in reads:
            if r in self.lastw:
                ev.append(self.lastw[r])
        for w in writes:
            if w in self.lastw:
                ev.append(self.lastw[w])
            ev.extend(self.readers.get(w, ()))
        return ev

    def _commit(self, reads, writes, event):
        for w in writes:
            self.lastw[w] = event
            self.readers[w] = []
        for r in reads:
            if r in writes:
                continue
            self.readers.setdefault(r, []).append(event)

    def _emit_waits(self, eng, deps):
        need = {}
        for (src, val) in deps:
            if val > need.get(src, 0):
                need[src] = val
        out = []
        for src, val in need.items():
            if self.waited[eng].get(src, 0) >= val:
                continue
            self.waited[eng][src] = val
            sem = self.sems[src] if src in self.sems else self.chan_sem[src]
            out.append((sem, val))
        return out

    def op(self, eng, fn, reads=(), writes=()):
        reads = list(reads)
        writes = list(writes)
        deps = self._deps(reads, writes)
        waits = self._emit_waits(eng, deps)
        self.cnt[eng] += 1
        sem = self.sems[eng]
        calls = []

        class _Rec:
            def __getattr__(self_, name):
                def f(*a, **k):
                    calls.append((name, a, k))
                    return None
                return f

        fn(_Rec())
        assert calls

        def run(e, calls=calls, waits=waits, sem=sem):
            for (s, v) in waits:
                e.wait_ge(s, v)
            last = None
            for (name, a, k) in calls:
                last = getattr(e, name)(*a, **k)
            last.then_inc(sem, 1)

        self.streams[eng].append(run)
        self._commit(reads, writes, (eng, self.cnt[eng]))

    def dma(self, queue, out, in_, chan, reads=(), writes=(), **kw):
        reads = list(reads)
        writes = list(writes)
        if chan not in self.chan_sem:
            self.chan_sem[chan] = self.stack.enter_context(self.nc.semaphore("c_" + str(chan)))
            self.chan_cnt[chan] = 0
        deps = self._deps(reads, writes)
        waits = self._emit_waits(queue, deps)
        self.chan_cnt[chan] += 16
        val = self.chan_cnt[chan]
        csem = self.chan_sem[chan]

        def run(e, waits=waits, csem=csem, out=out, in_=in_, kw=kw):
            for (s, v) in waits:
                e.wait_ge(s, v)
            e.dma_start(out=out, in_=in_, **kw).then_inc(csem, 16)

        self.streams[queue].append(run)
        self._commit(reads, writes, (chan, val))

    def barrier(self):
        evs = [(e, self.cnt[e]) for e in ENGS if self.cnt[e] > 0]
        evs += [(c, v) for c, v in self.chan_cnt.items() if v > 0]
        for eng in ENGS:
            waits = self._emit_waits(eng, evs)

            def run(e, waits=waits):
                for (s, v) in waits:
                    e.wait_ge(s, v)

            self.streams[eng].append(run)

    def emit_phase(self):
        self.barrier()
        self.emit()
        self.streams = {e: [] for e in ENGS}

    def final_wait(self, eng, keys):
        deps = [self.lastw[k] for k in keys if k in self.lastw]
        waits = self._emit_waits(eng, deps)

        def run(e, waits=waits):
            for (s, v) in waits:
                e.wait_ge(s, v)

        self.streams[eng].append(run)

    def emit(self):
        nc = self.nc
        with nc.Block() as block:
            @block.tensor
            def _(e):
                for f in self.streams["pe"]:
                    f(e)

            @block.scalar
            def _(e):
                for f in self.streams["act"]:
                    f(e)

            @block.vector
            def _(e):
                for f in self.streams["dve"]:
                    f(e)

            @block.gpsimd
            def _(e):
                for f in self.streams["pool"]:
                    f(e)

            @block.sync
            def _(e):
                for f in self.streams["sp"]:
                    f(e)


WEIGHT_SHAPES = {
    "w_ada": [D, 6 * D], "b_ada": [1, 6 * D],
    "norm_pre_mix": [1, D], "norm_post_mix": [1, D], "norm_pre_ffn": [1, D], "norm_post_ffn": [1, D],
    "w_in": [D, IN_COLS],
    "s5_lam_re": [64, 64], "s5_lam_im": [64, 64], "s5_log_dt": [1, 64],
    "s5_b_re": [64, 64, 16], "s5_b_im": [64, 64, 16], "s5_c_re": [64, 16, 64], "s5_c_im": [64, 16, 64],
    "s5_d": [1, D], "s5_w_glu": [D, D], "s5_b_glu": [1, D],
    "m_conv_w": [4, 3072], "m_conv_b": [1, 3072], "m_dt_bias": [1, 32], "m_a_log": [1, 32], "m_d": [1, 32],
    "m_norm": [1, 2048],
    "w_branch_s5": [D, D], "w_branch_ssd": [2048, D], "w_out": [D, D],
    "w_ffn_in": [D, 2 * D_FF], "w_ffn_out": [D_FF, D],
}
IN_SHAPES = {
    "xp": [SEQ, D], "xs": [NS, D], "c17": [NS + 1, D],
    "s5re_in": [NS, 4096], "s5im_in": [NS, 4096], "ssm_in": [NS, 32, 64, 128], "conv_in": [NS * 3, 3072],
}
OUT_SHAPES = {
    "y_p": [SEQ, D], "y_s": [NS, D], "s5re_p": [32, 128], "s5im_p": [32, 128], "ssm_p": [2048, 128],
    "conv_p": [3, 3072], "s5re_s": [NS, 4096], "s5im_s": [NS, 4096], "ssm_s": [NS, 32, 64, 128],
    "conv_s": [NS, 3, 3072],
}


def build_nc(debug_names=(), stop=None, nst=None):
    global STOP, NST_RUN
    STOP = stop
    NST_RUN = nst or NST
    SAMPLE.clear()
    nc = bass.Bass("TRN2", target_bir_lowering=False)
    dr = {}
    for n, s in list(IN_SHAPES.items()) + list(WEIGHT_SHAPES.items()):
        dr[n] = nc.dram_tensor(n, s, F32, kind="ExternalInput").ap()
    for n, s in OUT_SHAPES.items():
        dr[n] = nc.dram_tensor(n, s, F32, kind="ExternalOutput").ap()
    dbg_out = {}

    with ExitStack() as st:
        st.enter_context(nc.allow_non_contiguous_dma(reason="small strided parameter loads"))
        P = Prog(nc, st)

        ARW = 52800
        arena = st.enter_context(nc.sbuf_tensor("arena", [128, ARW], F32))
        bump = {"lo": 0, "hi": ARW, "peak": 0}

        def _carve(off, shape, dt):
            esz = 2 if dt == BF16 else 4
            n = 1
            for d_ in shape[1:]:
                n *= d_
            words = (n * esz + 3) // 4
            v = arena[0:shape[0], off:off + words]
            if dt != F32:
                v = v.bitcast(dt)
            v = v[:, 0:n]
            if len(shape) == 3:
                v = v.rearrange("p (a b) -> p a b", a=shape[1])
            elif len(shape) == 4:
                v = v.rearrange("p (a b c) -> p a b c", a=shape[1], b=shape[2])
            elif len(shape) == 5:
                v = v.rearrange("p (a b c d) -> p a b c d", a=shape[1], b=shape[2], c=shape[3])
            return v, words

        def sb(name, shape, dt=F32):
            v, words = _carve(bump["lo"], shape, dt)
            bump["lo"] += words
            assert bump["lo"] <= bump["hi"], ("SBUF arena overflow at", name, bump)
            bump["peak"] = max(bump["peak"], bump["lo"])
            return v

        def ssb(name, shape, dt=F32):
            esz = 2 if dt == BF16 else 4
            n = 1
            for d_ in shape[1:]:
                n *= d_
            words = (n * esz + 3) // 4
            bump["hi"] -= words
            assert bump["lo"] <= bump["hi"], ("SBUF arena overflow (scratch) at", name, bump)
            v, _ = _carve(bump["hi"], shape, dt)
            return v

        def psum(name, shape, dt=F32):
            return st.enter_context(nc.psum_tensor(name, shape, dt))

        def dump(name, ap, key, shape):
            if name not in debug_names:
                return
            t = nc.dram_tensor("dbg_" + name, shape, F32, kind="ExternalOutput").ap()
            dbg_out[name] = t
            P.dma("sp" if ap.dtype == F32 else "pool", t, ap, "dbg_" + name, reads=[key], writes=["dbg_" + name])

        def finish():
            P.final_wait("sp", [k for k in P.lastw if str(k).startswith(("o_", "yout", "dbg_"))])
            P.emit()
            return nc, dbg_out

        NPS = 6
        ps_f = [psum("psf%d" % i, [128, 512], F32) for i in range(NPS)]
        ps_b = [psum("psb%d" % i, [128, 1024], BF16) for i in range(2)]
        ps_ctr = [0, 0]

        def next_ps():
            i = ps_ctr[0] % NPS
            ps_ctr[0] += 1
            return ps_f[i], "psf%d" % i

        def next_psb():
            i = ps_ctr[1] % 2
            ps_ctr[1] += 1
            return ps_b[i], "psb%d" % i

        ident = sb("ident", [128, 128], F32)
        identb = sb("identb", [128, 128], BF16)
        ones = sb("ones", [128, 128], F32)
        tri = sb("tri", [128, 128], F32)
        su = sb("su", [128, 128], F32)
        P.op("pool", lambda e: e.memset(ones[:], 1.0), writes=["ones"])
        P.op("pool", lambda e: e.affine_select(out=ident[:], in_=ones[:], pattern=[[-1, 128]], compare_op=ALU.is_equal,
                                               fill=0.0, base=0, channel_multiplier=1), reads=["ones"], writes=["ident"])
        P.op("pool", lambda e: e.affine_select(out=tri[:], in_=ones[:], pattern=[[1, 128]], compare_op=ALU.is_ge,
                                               fill=0.0, base=0, channel_multiplier=-1), reads=["ones"], writes=["tri"])
        P.op("pool", lambda e: e.affine_select(out=su[:], in_=ones[:], pattern=[[-1, 128]], compare_op=ALU.is_gt,
                                               fill=0.0, base=0, channel_multiplier=1), reads=["ones"], writes=["su"])
        P.op("dve", lambda e: e.tensor_copy(out=identb[:], in_=ident[:]), reads=["ident"], writes=["identb"])

        def bc_row(name, src, n, alloc=None):
            t = (alloc or sb)(name, [128, n], F32)
            P.dma("sp", t[:], src.partition_broadcast(128), name, writes=[name])
            return t

        mnorm = sb("mnorm", [128, 2048], BF16)
        P.dma("pool", mnorm[:], dr["m_norm"][0, :].partition_broadcast(128), "mnorm", writes=["mnorm"])
        dtb = bc_row("dtb", dr["m_dt_bias"][0, :], 32)
        alog = bc_row("alog", dr["m_a_log"][0, :], 32, ssb)
        mdr = bc_row("mdr", dr["m_d"][0, :], 32)
        arow = sb("arow", [128, 32], F32)
        P.op("act", lambda e: e.activation(out=arow[:], in_=alog[:], func=AF.Exp), reads=["alog"], writes=["arow"])
        P.op("dve", lambda e: e.tensor_scalar(out=arow[:], in0=arow[:], scalar1=-1.0, scalar2=None, op0=ALU.mult),
             reads=["arow"], writes=["arow"])
        s5d = sb("s5d", [128, 8], F32)
        bglu = sb("bglu", [128, 8], F32)
        cw = sb("cw", [128, 4, 24], F32)
        cb = sb("cb", [128, 24], F32)
        P.dma("sp", s5d[:], dr["s5_d"][0, :].rearrange("(a p) -> p a", p=128), "s5d", writes=["s5d"])
        P.dma("sp", bglu[:], dr["s5_b_glu"][0, :].rearrange("(a p) -> p a", p=128), "bglu", writes=["bglu"])
        P.dma("sp", cb[:], dr["m_conv_b"][0, :].rearrange("(a p) -> p a", p=128), "cb", writes=["cb"])
        for k in range(4):
            P.dma("sp", cw[:, k, :], dr["m_conv_w"][k, :].rearrange("(a p) -> p a", p=128), "cw", writes=["cw"])

        if STOP == "s0":
            dump("cw", cw[:].rearrange("p a b -> p (a b)"), "cw", [128, 96])
            dump("mnorm", mnorm[:], "mnorm", [128, 2048])
            dump("tri", tri[:], "tri", [128, 128])
            return finish()
        NWB = 3
        WBE = 4096
        scratch = {}
        cast_prev = {}
        wbufs = [sb("wbuf%d" % i, [128, WBE], BF16) for i in range(NWB)]
        WB_EXTRA = []

        def scratch_chunk(wd, K, c0, cwid):
            nm = wd.name
            if nm == "w_ada":
                return None, None
            key = (nm, c0, cwid)
            if key not in scratch:
                KT = K // 128
                t = nc.dram_tensor("scr_%s_%d_%d" % (nm, c0, cwid), [128, KT * cwid], BF16).ap()
                sk = "scr_%s_%d" % (nm, c0)
                src = wd.rearrange("(k p) c -> p k c", p=128)[:, :, c0:c0 + cwid]
                ch = len(scratch) % 8
                prev = cast_prev.get(ch)
                P.dma("pool", t.rearrange("p (k c) -> p k c", c=cwid), src, "cast%d" % ch, reads=([prev] if prev else []), writes=[sk])
                cast_prev[ch] = sk
                scratch[key] = (t, sk)
            return scratch[key]

        _cw_dummy = None

        def _cw(KT, ncols):
            c = min(512, ncols)
            while c * KT > WBE:
                c //= 2
            return c
        w_ctr = [0]

        def load_w(wd, K, c0, cwid):
            KT = (K + 127) // 128
            i = w_ctr[0] % (NWB + len(WB_EXTRA))
            w_ctr[0] += 1
            key = "wbuf%d" % i
            flat = (wbufs + WB_EXTRA)[i][:, 0:KT * cwid]
            view = flat.rearrange("p (k c) -> p k c", c=cwid)
            sc, sk = scratch_chunk(wd, K, c0, cwid)
            if sc is None:
                src = wd.rearrange("(k p) c -> p k c", p=128)[:, :, c0:c0 + cwid]
                P.dma("pool", view, src, key, writes=[key])
            else:
                P.dma("sp", flat, sc, key, reads=[sk], writes=[key])
            return view, key

        def drain(g):
            for _ in g:
                pass

        def interleave(gens, weights=None):
            gens = list(gens)
            weights = list(weights or [1] * len(gens))
            live = [True] * len(gens)
            while any(live):
                for i, g in enumerate(gens):
                    if not live[i]:
                        continue
                    for _ in range(weights[i]):
                        try:
                            next(g)
                        except StopIteration:
                            live[i] = False
                            break

        def linear_fm(*a, **k):
            drain(linear_fm_g(*a, **k))

        def linear_tm(*a, **k):
            drain(linear_tm_g(*a, **k))

        def transpose_to(*a, **k):
            drain(transpose_to_g(*a, **k))

        def linear_fm_g(inT, in_key, K, wd, c0, ncols, tt, evac, cwid=512):
            KT = K // 128
            cwid = _cw(KT, ncols)
            ct = 0
            for cc in range(0, ncols, cwid):
                wv, wk = load_w(wd, K, c0 + cc, cwid)
                for j in range(0, cwid, 128):
                    pt, pk = next_ps()

                    def mm(e, pt=pt, wv=wv, j=j):
                        last = None
                        for kt in range(KT):
                            last = e.matmul(pt[:, 0:tt], lhsT=wv[:, kt, j:j + 128], rhs=inT[:, kt, 0:tt],
                                            start=(kt == 0), stop=(kt == KT - 1))
                        return last

                    P.op("pe", mm, reads=[wk, in_key], writes=[pk])
                    evac(pt, pk, ct)
                    ct += 1
                yield

        def linear_tm_g(inT, in_key, K, wd, c0, ncols, tiles, evac, cwid=512):
            KT = K // 128
            cwid = _cw(KT, ncols)
            for cc in range(0, ncols, cwid):
                wv, wk = load_w(wd, K, c0 + cc, cwid)
                for ti, (t0, rows) in enumerate(tiles):
                    pt, pk = next_ps()

                    def mm(e, pt=pt, wv=wv, t0=t0, rows=rows):
                        last = None
                        for kt in range(KT):
                            last = e.matmul(pt[0:rows, 0:cwid], lhsT=inT[:, kt, t0:t0 + rows], rhs=wv[:, kt, :],
                                            start=(kt == 0), stop=(kt == KT - 1))
                        return last

                    P.op("pe", mm, reads=[wk, in_key], writes=[pk])
                    evac(pt, pk, ti, cc, cwid)
                    yield

        ev_ctr = [0]

        def copy_evac(dst, dkey, src, skey):
            ev_ctr[0] += 1
            if True:
                P.op("act", lambda e: e.copy(out=dst, in_=src), reads=[skey], writes=[dkey])
            else:
                P.op("dve", lambda e: e.tensor_copy(out=dst, in_=src), reads=[skey], writes=[dkey])

        def transpose_to_g(dst_fn, dkey, src_fn, skey, n, rows, cols, dt):
            rp = (rows + 3) // 4 * 4
            for i0 in range(0, n, 4):
                cnt = min(4, n - i0)
                if dt == BF16:
                    pt, pk = next_psb()
                    idn = identb
                else:
                    pt, pk = next_ps()
                    idn = ident

                def tr(e, pt=pt, i0=i0, cnt=cnt, idn=idn):
                    last = None
                    for a in range(cnt):
                        last = e.transpose(pt[0:cols, a * rp:a * rp + rows], src_fn(i0 + a), idn[0:rows, 0:rows])
                    return last

                P.op("pe", tr, reads=[skey, "ident", "identb"], writes=[pk])
                src = pt[0:cols, 0:cnt * rp].rearrange("p (a r) -> p a r", r=rp)[:, :, 0:rows]
                copy_evac(dst_fn(i0, cnt), dkey, src, pk)
                yield

        modp = sb("modp", [128, 2, D], BF16)
        modcol = sb("modcol", [128, 6, 8], F32)

        def ada_compute(alloc, want_prompt, mods_t=None):
            c17 = alloc("c17", [NS + 1, D], F32)
            c17s = alloc("c17s", [NS + 1, D], BF16)
            c17T = alloc("c17T", [128, 8, NS + 1], BF16)
            badac = alloc("badac", [NS + 1, 512], F32)
            modc = alloc("modc", [NS + 1, 512], F32)
            nrows = {}
            for nm_, key_ in (("npm", "norm_pre_mix"), ("npo", "norm_post_mix"), ("npf", "norm_pre_ffn"), ("nqf", "norm_post_ffn")):
                t_ = alloc(nm_, [128, D], F32)
                P.dma("sp", t_[:], dr[key_][0, :].partition_broadcast(128), nm_, writes=[nm_])
                nrows[nm_] = t_
            P.dma("sp", c17[:], dr["c17"], "c17", writes=["c17"])
            P.op("act", lambda e: e.activation(out=c17s[:], in_=c17[:], func=AF.Silu), reads=["c17"], writes=["c17s"])
            transpose_to(lambda i0, cnt: c17T[:, i0:i0 + cnt, :], "c17T", lambda i: c17s[:, i * 128:(i + 1) * 128], "c17s",
                         8, NS + 1, 128, BF16)
            if want_prompt:
                sel16 = alloc("sel16", [NS + 1, 128], F32)
                P.op("pool", lambda e: e.affine_select(out=sel16[:], in_=ones[0:NS + 1, :], pattern=[[0, 128]], compare_op=ALU.is_equal,
                                                       fill=0.0, base=-NS, channel_multiplier=1), reads=["ones"], writes=["sel16"])

            def ada_evac(pt, pk, ti, cc, cwid):
                j, off = cc // D, cc % D
                P.dma("sp", badac[:, 0:cwid], dr["b_ada"][0, cc:cc + cwid].partition_broadcast(NS + 1), "badac", writes=["badac"])
                P.op("dve", lambda e: e.tensor_tensor(out=modc[:, 0:cwid], in0=pt[0:NS + 1, 0:cwid], in1=badac[:, 0:cwid], op=ALU.add),
                     reads=[pk, "badac"], writes=["modc"])
                if want_prompt and j in (2, 5):
                    p2, pk2 = next_ps()
                    P.op("pe", lambda e: e.matmul(p2[:, 0:cwid], lhsT=sel16[:, :], rhs=modc[:, 0:cwid], start=True, stop=True),
                         reads=["sel16", "modc"], writes=[pk2])
                    P.op("act", lambda e: e.copy(out=modp[:, j // 3, off:off + cwid], in_=p2[:, 0:cwid]), reads=[pk2], writes=["modp"])
                elif want_prompt:
                    p2, pk2 = next_ps()

                    def trm(e):
                        last = None
                        for b_ in range(cwid // 128):
                            last = e.transpose(p2[:, b_ * 32:b_ * 32 + NS + 1], modc[0:NS + 1, b_ * 128:(b_ + 1) * 128], ident[0:NS + 1, 0:NS + 1])
                        return last

                    P.op("pe", trm, reads=["modc", "ident"], writes=[pk2])
                    kt0 = off // 128
                    P.op("act", lambda e: e.copy(out=modcol[:, j, kt0:kt0 + cwid // 128],
                                                 in_=p2[:, 0:(cwid // 128) * 32].rearrange("p (b c) -> p b c", c=32)[:, :, NS]),
                         reads=[pk2], writes=["modcol"])
                else:
                    P.op("dve", lambda e: e.tensor_copy(out=mods_t[:, j, off:off + cwid], in_=modc[0:NS, 0:cwid]), reads=["modc"], writes=["mods"])

            linear_tm(c17T, "c17T", D, dr["w_ada"], 0, 6 * D, [(0, NS + 1)], ada_evac)
            if want_prompt:
                for (jsc, wname) in ((1, "norm_pre_mix"), (4, "norm_pre_ffn")):
                    ncol = alloc("ncol%d" % jsc, [128, 8], F32)
                    P.dma("sp", ncol[:], dr[wname][0, :].rearrange("(a p) -> p a", p=128), "ncol%d" % jsc, writes=["ncol%d" % jsc])
                    P.op("dve", lambda e, jsc=jsc, ncol=ncol: e.scalar_tensor_tensor(
                        out=modcol[:, jsc, :], in0=modcol[:, jsc, :], scalar=1.0, op0=ALU.add, in1=ncol[:], op1=ALU.mult),
                        reads=["modcol", "ncol%d" % jsc], writes=["modcol"])
                for (gi, nwk) in ((0, "npo"), (1, "nqf")):
                    nw = nrows[nwk]
                    P.op("dve", lambda e, gi=gi, nw=nw: e.tensor_tensor(out=modp[:, gi, :], in0=modp[:, gi, :], in1=nw[:, :], op=ALU.mult),
                         reads=["modp", nwk], writes=["modp"])
            else:
                mt, rows, key = mods_t, NS, "mods"
                for (jsc, nwk) in ((1, "npm"), (4, "npf")):
                    nw = nrows[nwk]
                    P.op("dve", lambda e, jsc=jsc, nw=nw: e.scalar_tensor_tensor(
                        out=mt[:, jsc, :], in0=mt[:, jsc, :], scalar=1.0, op0=ALU.add, in1=nw[0:rows, :], op1=ALU.mult),
                        reads=[key, nwk], writes=[key])
                for (jg, nwk) in ((2, "npo"), (5, "nqf")):
                    nw = nrows[nwk]
                    P.op("dve", lambda e, jg=jg, nw=nw: e.tensor_tensor(
                        out=mt[:, jg, :], in0=mt[:, jg, :], in1=nw[0:rows, :], op=ALU.mult), reads=[key, nwk], writes=[key])

        ada_compute(ssb, True)

        if STOP == "s1":
            dump("modp", modp[:].rearrange("p a b -> p (a b)"), "modp", [128, 2 * D])
            dump("modcol", modcol[:].rearrange("p a b -> p (a b)"), "modcol", [128, 48])
            return finish()
        lre = ssb("lre", [128, 32], F32)
        lim = ssb("lim", [128, 32], F32)
        ldt = ssb("ldt", [128, 32], F32)
        P.dma("sp", lre[:], dr["s5_lam_re"].rearrange("(q gl) n -> (gl n) q", gl=2), "lre", writes=["lre"])
        P.dma("sp", lim[:], dr["s5_lam_im"].rearrange("(q gl) n -> (gl n) q", gl=2), "lim", writes=["lim"])
        for gl in range(2):
            P.dma("sp", ldt[gl * 64:(gl + 1) * 64, :],
                  dr["s5_log_dt"][0, :].rearrange("(q gl) -> gl q", gl=2)[gl, :].partition_broadcast(64), "ldt", writes=["ldt"])
        s5t = sb("s5t", [128, 12, 32], F32)
        DT_, TH, RHO, AR, AI, FR, FI, DEN, T0, T1, T2, T3 = range(12)

        def s5op(fn, wr):
            P.op("dve", fn, reads=["lre", "lim", "ldt", "s5t"], writes=wr)

        def frac_sin(out_ap, ang_ap, tmp_f, tmp_i, shape_key, quarter):
            P.op("dve", lambda e: e.tensor_scalar(out=tmp_f, in0=ang_ap, scalar1=1.0 / TWO_PI, scalar2=0.25 * quarter,
                                                  op0=ALU.mult, op1=ALU.add), reads=[shape_key], writes=[shape_key])
            P.op("dve", lambda e: e.tensor_copy(out=tmp_i, in_=tmp_f), reads=[shape_key], writes=[shape_key])
            P.op("dve", lambda e: e.tensor_copy(out=out_ap, in_=tmp_i), reads=[shape_key], writes=[shape_key])
            P.op("dve", lambda e: e.tensor_tensor(out=tmp_f, in0=tmp_f, in1=out_ap, op=ALU.subtract), reads=[shape_key], writes=[shape_key])
            P.op("dve", lambda e: e.tensor_scalar(out=out_ap, in0=tmp_f, scalar1=0.5, scalar2=None, op0=ALU.is_gt),
                 reads=[shape_key], writes=[shape_key])
            P.op("dve", lambda e: e.tensor_tensor(out=tmp_f, in0=tmp_f, in1=out_ap, op=ALU.subtract), reads=[shape_key], writes=[shape_key])
            P.op("dve", lambda e: e.tensor_scalar(out=out_ap, in0=tmp_f, scalar1=-0.5, scalar2=None, op0=ALU.is_lt),
                 reads=[shape_key], writes=[shape_key])
            P.op("dve", lambda e: e.tensor_tensor(out=tmp_f, in0=tmp_f, in1=out_ap, op=ALU.add), reads=[shape_key], writes=[shape_key])
            P.op("act", lambda e: e.activation(out=out_ap, in_=tmp_f, func=AF.Sin, scale=TWO_PI), reads=[shape_key], writes=[shape_key])

        tmpi = ssb("tmpi", [128, 32 * 64], mybir.dt.int32)
        tmpf = ssb("tmpf", [128, 32 * 64], F32)
        P.op("act", lambda e: e.activation(out=s5t[:, DT_, :], in_=ldt[:], func=AF.Exp), reads=["ldt"], writes=["s5t"])
        s5op(lambda e: e.tensor_tensor(out=s5t[:, TH, :], in0=lim[:], in1=s5t[:, DT_, :], op=ALU.mult), ["s5t"])
        s5op(lambda e: e.tensor_tensor(out=s5t[:, T0, :], in0=lre[:], in1=s5t[:, DT_, :], op=ALU.mult), ["s5t"])
        P.op("act", lambda e: e.activation(out=s5t[:, RHO, :], in_=s5t[:, T0, :], func=AF.Exp), reads=["s5t"], writes=["s5t"])
        frac_sin(s5t[:, T1, :], s5t[:, TH, :], tmpf[:, 0:32], tmpi[:, 0:32], "s5t", 1)
        frac_sin(s5t[:, T2, :], s5t[:, TH, :], tmpf[:, 0:32], tmpi[:, 0:32], "s5t", 0)
        s5op(lambda e: e.tensor_tensor(out=s5t[:, AR, :], in0=s5t[:, RHO, :], in1=s5t[:, T1, :], op=ALU.mult), ["s5t"])
        s5op(lambda e: e.tensor_tensor(out=s5t[:, AI, :], in0=s5t[:, RHO, :], in1=s5t[:, T2, :], op=ALU.mult), ["s5t"])
        s5op(lambda e: e.tensor_tensor(out=s5t[:, T0, :], in0=lre[:], in1=lre[:], op=ALU.mult), ["s5t"])
        s5op(lambda e: e.tensor_tensor(out=s5t[:, T1, :], in0=lim[:], in1=lim[:], op=ALU.mult), ["s5t"])
        s5op(lambda e: e.tensor_tensor(out=s5t[:, DEN, :], in0=s5t[:, T0, :], in1=s5t[:, T1, :], op=ALU.add), ["s5t"])
        s5op(lambda e: e.reciprocal(out=s5t[:, DEN, :], in_=s5t[:, DEN, :]), ["s5t"])
        s5op(lambda e: e.tensor_scalar(out=s5t[:, T0, :], in0=s5t[:, AR, :], scalar1=-1.0, scalar2=None, op0=ALU.add), ["s5t"])
        s5op(lambda e: e.tensor_tensor(out=s5t[:, T1, :], in0=s5t[:, T0, :], in1=lre[:], op=ALU.mult), ["s5t"])
        s5op(lambda e: e.tensor_tensor(out=s5t[:, T2, :], in0=s5t[:, AI, :], in1=lim[:], op=ALU.mult), ["s5t"])
        s5op(lambda e: e.tensor_tensor(out=s5t[:, T1, :], in0=s5t[:, T1, :], in1=s5t[:, T2, :], op=ALU.add), ["s5t"])
        s5op(lambda e: e.tensor_tensor(out=s5t[:, FR, :], in0=s5t[:, T1, :], in1=s5t[:, DEN, :], op=ALU.mult), ["s5t"])
        s5op(lambda e: e.tensor_tensor(out=s5t[:, T1, :], in0=s5t[:, AI, :], in1=lre[:], op=ALU.mult), ["s5t"])
        s5op(lambda e: e.tensor_tensor(out=s5t[:, T2, :], in0=s5t[:, T0, :], in1=lim[:], op=ALU.mult), ["s5t"])
        s5op(lambda e: e.tensor_tensor(out=s5t[:, T1, :], in0=s5t[:, T1, :], in1=s5t[:, T2, :], op=ALU.subtract), ["s5t"])
        s5op(lambda e: e.tensor_tensor(out=s5t[:, FI, :], in0=s5t[:, T1, :], in1=s5t[:, DEN, :], op=ALU.mult), ["s5t"])

        if STOP == "s2":
            dump("s5t", s5t[:].rearrange("p a b -> p (a b)"), "s5t", [128, 12 * 32])
            return finish()
        cosT = sb("cosT", [128, 32, 64], F32)
        sinT = sb("sinT", [128, 32, 64], F32)
        ang = ssb("ang", [128, 32, 64], F32)
        iot = ssb("iot", [128, 64], F32)
        P.op("pool", lambda e: e.iota(iot[:], [[1, 64]], base=1, channel_multiplier=0, allow_small_or_imprecise_dtypes=True),
             writes=["iot"])
        P.op("dve", lambda e: e.tensor_tensor(out=ang[:], in0=s5t[:, TH, :].unsqueeze(2).to_broadcast([128, 32, 64]),
                                              in1=iot[:].unsqueeze(1).to_broadcast([128, 32, 64]), op=ALU.mult),
             reads=["s5t", "iot"], writes=["ang"])
        angf = ang[:].rearrange("p a b -> p (a b)")
        P.op("dve", lambda e: e.tensor_copy(out=tmpf[:, 0:1], in_=tmpf[:, 0:1]), reads=["ang", "s5t"], writes=["tab"])
        frac_sin(cosT[:].rearrange("p a b -> p (a b)"), angf, tmpf[:], tmpi[:], "tab", 1)
        frac_sin(sinT[:].rearrange("p a b -> p (a b)"), angf, tmpf[:], tmpi[:], "tab", 0)

        if STOP == "s3":
            dump("cosT", cosT[:].rearrange("p a b -> p (a b)"), "tab", [128, 2048])
            return finish()
        bre = ssb("bre", [128, 32, 16], F32)
        bim = ssb("bim", [128, 32, 16], F32)
        P.dma("sp", bre[:], dr["s5_b_re"].rearrange("(q gl) n i -> (gl n) q i", gl=2), "bre", writes=["bre"])
        P.dma("sp", bim[:], dr["s5_b_im"].rearrange("(q gl) n i -> (gl n) q i", gl=2), "bim", writes=["bim"])
        bbr = ssb("bbr", [128, 32, 16], F32)
        bbi = ssb("bbi", [128, 32, 16], F32)
        bt = ssb("bt", [128, 32, 16], F32)
        frb = s5t[:, FR, :].unsqueeze(2).to_broadcast([128, 32, 16])
        fib = s5t[:, FI, :].unsqueeze(2).to_broadcast([128, 32, 16])
        RB = ["bre", "bim", "s5t", "bt", "bbr", "bbi"]
        P.op("dve", lambda e: e.tensor_tensor(out=bbr[:], in0=bre[:], in1=frb, op=ALU.mult), reads=RB, writes=["bbr"])
        P.op("dve", lambda e: e.tensor_tensor(out=bt[:], in0=bim[:], in1=fib, op=ALU.mult), reads=RB, writes=["bt"])
        P.op("dve", lambda e: e.tensor_tensor(out=bbr[:], in0=bbr[:], in1=bt[:], op=ALU.subtract), reads=RB, writes=["bbr"])
        P.op("dve", lambda e: e.tensor_tensor(out=bbi[:], in0=bim[:], in1=frb, op=ALU.mult), reads=RB, writes=["bbi"])
        P.op("dve", lambda e: e.tensor_tensor(out=bt[:], in0=bre[:], in1=fib, op=ALU.mult), reads=RB, writes=["bt"])
        P.op("dve", lambda e: e.tensor_tensor(out=bbi[:], in0=bbi[:], in1=bt[:], op=ALU.add), reads=RB, writes=["bbi"])
        wb = sb("wb", [128, 8, 2, 128], BF16)
        x4 = ssb("x4", [128, 4, 2, 16], F32)
        P.op("pool", lambda e: e.memset(x4[:].rearrange("p a b c -> p (a b c)"), 0.0), writes=["x4"])
        for G in range(8):
            for ri, bb in enumerate((bbr, bbi)):
                bk = "bbr" if ri == 0 else "bbi"
                for gl in range(2):
                    P.op("dve", lambda e, G=G, bb=bb, gl=gl: e.tensor_copy(out=x4[gl * 64:(gl + 1) * 64, :, gl, :],
                                                                           in_=bb[gl * 64:(gl + 1) * 64, 4 * G:4 * G + 4, :]),
                         reads=[bk], writes=["x4"])
                pt, pk = next_ps()
                P.op("pe", lambda e, pt=pt: e.transpose(pt[:, 0:128], x4[:].rearrange("p a b c -> p (a b c)"), ident[:]),
                     reads=["x4", "ident"], writes=[pk])
                P.op("act", lambda e, pt=pt, G=G, ri=ri: e.copy(out=wb[:, G, ri, :], in_=pt[:, 0:128]), reads=[pk], writes=["wb"])
        ctr_ = ssb("ctr_", [128, 32, 16], F32)
        cti_ = ssb("cti_", [128, 32, 16], F32)
        zcs = [ssb("zc%d" % i, [128, 2, 64], F32) for i in range(8)]
        for ti_, (dst, dk, src) in enumerate(((ctr_, "ctr_", dr["s5_c_re"]), (cti_, "cti_", dr["s5_c_im"]))):
            for qb in range(4):
                zc = zcs[ti_ * 4 + qb]
                zk = "zc%d" % (ti_ * 4 + qb)
                v = src.rearrange("(qq gl) j n -> qq j gl n", gl=2)[8 * qb:8 * qb + 8]
                for qq in range(8):
                    P.dma("sp", zc[16 * qq:16 * qq + 16, :, :], v[qq], zk, writes=[zk])
                pt, pk = next_ps()
                P.op("pe", lambda e, pt=pt: e.transpose(pt[:, 0:128], zc[:].rearrange("p a b -> p (a b)"), ident[:]),
                     reads=[zk, "ident"], writes=[pk])
                P.op("act", lambda e, pt=pt, dst=dst, qb=qb: e.copy(out=dst[:, 8 * qb:8 * qb + 8, :],
                                                                   in_=pt[:, 0:128].rearrange("p (a b) -> p a b", b=16)),
                     reads=[pk], writes=[dk])
        wd_ = sb("wd_", [128, 32, 2, 32], BF16)
        P.op("pool", lambda e: e.memset(wd_[:].rearrange("p a b c -> p (a b c)"), 0.0), writes=["wd_"])
        for gl in range(2):
            c0 = 16 * gl
            P.op("dve", lambda e, gl=gl, c0=c0: e.tensor_copy(out=wd_[gl * 64:(gl + 1) * 64, :, 0, c0:c0 + 16],
                                                             in_=ctr_[gl * 64:(gl + 1) * 64, :, :]),
                 reads=["ctr_"], writes=["wd_"])
            P.op("dve", lambda e, gl=gl, c0=c0: e.tensor_scalar(out=wd_[gl * 64:(gl + 1) * 64, :, 1, c0:c0 + 16],
                                                               in0=cti_[gl * 64:(gl + 1) * 64, :, :], scalar1=-1.0,
                                                               scalar2=None, op0=ALU.mult),
                 reads=["cti_"], writes=["wd_"])
        P.emit_phase()
        if STOP == "setup":
            dump("cosT", cosT[:].rearrange("p a b -> p (a b)"), "tab", [128, 2048])
            dump("sinT", sinT[:].rearrange("p a b -> p (a b)"), "tab", [128, 2048])
            dump("s5t", s5t[:].rearrange("p a b -> p (a b)"), "s5t", [128, 12 * 32])
            dump("mods", mods[:].rearrange("p a b -> p (a b)"), "mods", [NS, 6 * D])
            return finish()
        bump['hi'] = ARW
        print('arena after setup: lo=%d words' % bump['lo'])

        s5c = sb("s5c", [128, 32, 2], F32)
        P.op("pool", lambda e: e.memset(s5c[:].rearrange("p a b -> p (a b)"), 0.0), writes=["s5c"])
        uz = sb("uz", [128, 4, T], BF16)
        P.op("pool", lambda e: e.memset(uz[:].rearrange("p a b -> p (a b)"), 0.0), writes=["uz"])
        hT = sb("hT", [128, 4, 512], F32)
        hTb = sb("hTb", [128, 4, 512], BF16)
        P.op("pool", lambda e: e.memset(hT[:].rearrange("p a b -> p (a b)"), 0.0), writes=["hT"])
        P.op("pool", lambda e: e.memset(hTb[:].rearrange("p a b -> p (a b)"), 0.0), writes=["hTb"])
        xbcT = sb("xbcT", [128, 24, T + 3], BF16)
        P.op("pool", lambda e: e.memset(xbcT[:].rearrange("p a b -> p (a b)"), 0.0), writes=["xbcT"])

        x_tm = sb("x_tm", [128, 1, D], F32)
        hT_ = sb("hT_", [128, 8, T], BF16)
        uT = sb("uT", [128, 8, T], BF16)
        zs = sb("zs", [128, 1, 2048], BF16)
        gaT = sb("gaT", [128, 8, T], BF16)
        gbT = sb("gbT", [128, 8, T], BF16)
        dtr = sb("dtr", [128, 1, 32], F32)
        yaT = sb("yaT", [128, 8, T], BF16)
        yaoT = sb("yaoT", [128, 8, T], BF16)
        yBT = sb("yBT", [128, 16, T], BF16)
        st1 = sb("st1", [128, 8], F32)
        ys5 = sb("ys5", [128, T], F32)
        cv = [sb("cv%d" % i, [128, T], F32) for i in range(2)]
        sm = sb("sm", [128, 8, 32], F32)
        ytm = sb("ytm", [128, 2048], F32)
        yBtm = sb("yBtm", [128, 2048], BF16)
        hn = sb("hn", [128, D], BF16)
        uT_1 = sb("uT_1", [128, 8, T], BF16)
        zs_1 = sb("zs_1", [128, 1, 2048], BF16)
        gaT_1 = sb("gaT_1", [128, 8, T], BF16)
        gbT_1 = sb("gbT_1", [128, 8, T], BF16)
        dtr_1 = sb("dtr_1", [128, 1, 32], F32)
        yaoT_1 = sb("yaoT_1", [128, 8, T], BF16)
        yBT_1 = sb("yBT_1", [128, 16, T], BF16)
        xbcT_1 = sb("xbcT_1", [128, 24, T + 3], BF16)
        P.op("pool", lambda e: e.memset(xbcT_1[:].rearrange("p a b -> p (a b)"), 0.0), writes=["xbcT1"])
        PB = [dict(uT=uT, zs=zs, xbcT=xbcT, dtr=dtr, gaT=gaT, gbT=gbT, yaoT=yaoT, yBT=yBT),
              dict(uT=uT_1, zs=zs_1, xbcT=xbcT_1, dtr=dtr_1, gaT=gaT_1, gbT=gbT_1, yaoT=yaoT_1, yBT=yBT_1)]
        M1 = bump["lo"]
        actT = sb("actT", [128, 24, T], BF16)
        mrg = sb("mrg", [128, 8, T], BF16)
        mrgT = sb("mrgT", [128, 8, T], BF16)
        factT = sb("factT", [128, 22, T], BF16)
        wk2 = sb("wk2", [128, D], F32)
        M2 = bump["lo"]
        xtail = sb("xtail", [128, 24, 4], F32)
        s5S = [sb("s5S%d" % i, [128, T // 64, 2, 4, 64], F32) for i in range(2)]
        s5t2 = [sb("s5t2%d" % i, [128, 4, T], F32) for i in range(2)]
        rzs = [sb("rz%d" % i, [128, 2, 4, 64], F32) for i in range(2)]
        t8s = [sb("t8%d" % i, [128, 2, 4], F32) for i in range(2)]
        hch = [sb("hch%d" % i, [128, 2, 4, 64], F32) for i in range(2)]
        hbf = [sb("hbf%d" % i, [128, 4, 2, T], BF16) for i in range(2)]
        xtm = sb("xtm", [128, 2048], BF16)
        btm = sb("btm", [128, 4, 128], BF16)
        Rb = sb("Rb", [128, 8, 128], F32)
        LT = sb("LT", [128, 8, 128], BF16)
        MT = sb("MT", [128, 8, 128], BF16)
        CBm = sb("CBm", [128, 128], BF16)
        xdt = sb("xdt", [128, 512], BF16)
        X2 = sb("X2", [128, 512], BF16)
        yt1 = sb("yt1", [128, 512], F32)
        cps = yt1
        if PROBE_WB:
            WB_EXTRA.append(ytm[:, :].bitcast(BF16))
        print("arena: M1=%d M2=%d end=%d of %d" % (M1, M2, bump["lo"], ARW))

        def rms_mod(ti, rows, mt, jA, jB, mkey, tok0):
            xv = x_tm[0:rows, ti, :]
            P.op("act", lambda e: e.activation(out=hn[0:rows, 0:D], in_=xv, func=AF.Square, accum_out=st1[0:rows, 0:1]),
                 reads=["x_tm"], writes=["hn", "st1"])
            P.op("act", lambda e: e.activation(out=st1[0:rows, 1:2], in_=st1[0:rows, 0:1], func=AF.Sqrt, scale=1.0 / D, bias=EPS),
                 reads=["st1"], writes=["st1"])
            P.op("dve", lambda e: e.reciprocal(out=st1[0:rows, 2:3], in_=st1[0:rows, 1:2]), reads=["st1"], writes=["st1"])
            P.op("dve", lambda e: e.scalar_tensor_tensor(out=wk2[0:rows, 0:D], in0=xv, scalar=st1[0:rows, 2:3], op0=ALU.mult,
                                                         in1=mt[0:rows, jA, :], op1=ALU.mult),
                 reads=["x_tm", "st1", mkey], writes=["wk2"])
            P.op("dve", lambda e: e.tensor_tensor(out=hn[0:rows, :], in0=wk2[0:rows, 0:D], in1=mt[0:rows, jB, :], op=ALU.add),
                 reads=["wk2", mkey], writes=["hn"])
            transpose_to(lambda i0, cnt: hT_[:, i0:i0 + cnt, tok0:tok0 + rows], "hT_",
                         lambda i: hn[0:rows, i * 128:(i + 1) * 128], "hn", 8, rows, 128, BF16)

        def rms_mod_p(jA, jB):
            xv = x_tm[:, 0, :]
            P.op("act", lambda e: e.activation(out=hn[:, 0:D], in_=xv, func=AF.Square, accum_out=st1[:, 0:1]),
                 reads=["x_tm"], writes=["hn", "st1"])
            P.op("act", lambda e: e.activation(out=st1[:, 1:2], in_=st1[:, 0:1], func=AF.Sqrt, scale=1.0 / D, bias=EPS),
                 reads=["st1"], writes=["st1"])
            P.op("dve", lambda e: e.reciprocal(out=st1[:, 2:3], in_=st1[:, 1:2]), reads=["st1"], writes=["st1"])
            P.op("dve", lambda e: e.tensor_scalar(out=hn[:, :], in0=xv, scalar1=st1[:, 2:3], scalar2=None, op0=ALU.mult),
                 reads=["x_tm", "st1"], writes=["hn"])
            for i0 in range(0, 8, 4):
                pt, pk = next_psb()

                def tr(e, pt=pt, i0=i0):
                    last = None
                    for a in range(4):
                        last = e.transpose(pt[:, a * 128:(a + 1) * 128], hn[:, (i0 + a) * 128:(i0 + a + 1) * 128], identb[:])
                    return last

                P.op("pe", tr, reads=["hn", "identb"], writes=[pk])
                for a in range(4):
                    kt = i0 + a
                    P.op("act", lambda e, pt=pt, a=a, kt=kt: e.activation(out=hT_[:, kt, 0:128], in_=pt[:, a * 128:(a + 1) * 128], func=AF.Identity,
                                                                         scale=modcol[:, jA, kt:kt + 1], bias=modcol[:, jB, kt:kt + 1]),
                         reads=[pk, "modcol"], writes=["hT_"])

        def resid_gate(src_tm, skey, ti, rows, mt, jG, mkey):
            P.op("act", lambda e: e.activation(out=hn[0:rows, 0:D], in_=src_tm, func=AF.Square, accum_out=st1[0:rows, 4:5]),
                 reads=[skey], writes=["hn", "st1"])
            P.op("act", lambda e: e.activation(out=st1[0:rows, 5:6], in_=st1[0:rows, 4:5], func=AF.Sqrt, scale=1.0 / D, bias=EPS),
                 reads=["st1"], writes=["st1"])
            P.op("dve", lambda e: e.reciprocal(out=st1[0:rows, 6:7], in_=st1[0:rows, 5:6]), reads=["st1"], writes=["st1"])
            P.op("dve", lambda e: e.scalar_tensor_tensor(out=src_tm, in0=src_tm, scalar=st1[0:rows, 6:7], op0=ALU.mult,
                                                         in1=mt[0:rows, jG, :], op1=ALU.mult),
                 reads=[skey, "st1", mkey], writes=[skey])
            P.op("dve", lambda e: e.tensor_tensor(out=x_tm[0:rows, ti, :], in0=x_tm[0:rows, ti, :], in1=src_tm, op=ALU.add),
                 reads=[skey, "x_tm"], writes=["x_tm"])


        def s5_load_u(G, tt, par=0):
            (uT,) = [PB[par][n_] for n_ in ("uT",)]
            kk = lambda n_: n_ if par == 0 else n_ + "1"
            for r in range(4):
                P.op("pool", lambda e, r=r: e.tensor_copy(out=uz[32 * r:32 * r + 32, r, 0:tt], in_=uT[32 * r:32 * r + 32, G, 0:tt]),
                     reads=[kk("uT")], writes=["uz"])

        def s5_rot_in(G, tt, par=0):
            gp = G % 2
            S, Tm = s5S[gp], s5t2[gp]
            sk, tk = "s5S%d" % gp, "s5t2%d" % gp
            nch = tt // 64
            s5_load_u(G, tt, par)
            pr, pkr = next_ps()
            pi_, pki = next_ps()

            def mm(e):
                last = None
                for r in range(4):
                    e.matmul(pr[:, r * tt:(r + 1) * tt], lhsT=wb[:, G, 0, :], rhs=uz[:, r, 0:tt], start=True, stop=True)
                    last = e.matmul(pi_[:, r * tt:(r + 1) * tt], lhsT=wb[:, G, 1, :], rhs=uz[:, r, 0:tt], start=True, stop=True)
                return last

            P.op("pe", mm, reads=["wb", "uz"], writes=[pkr, pki])
            pv = lambda p_: p_[:, 0:4 * tt].rearrange("p (q c t) -> p q c t", q=4, t=64)
            So = lambda ri: S[:, 0:nch, ri].rearrange("p c r t -> p r c t")
            Tv = Tm[:, :, 0:tt].rearrange("p q (c t) -> p q c t", t=64)
            cb_ = cosT[:, 4 * G:4 * G + 4, :].unsqueeze(2).to_broadcast([128, 4, nch, 64])
            sb_ = sinT[:, 4 * G:4 * G + 4, :].unsqueeze(2).to_broadcast([128, 4, nch, 64])
            RK = [pkr, pki, "cosT", "sinT", sk, tk]
            P.op("dve", lambda e: e.tensor_tensor(out=So(0), in0=pv(pr), in1=cb_, op=ALU.mult), reads=RK, writes=[sk])
            P.op("dve", lambda e: e.tensor_tensor(out=Tv, in0=pv(pi_), in1=sb_, op=ALU.mult), reads=RK, writes=[tk])
            P.op("dve", lambda e: e.tensor_tensor(out=So(0), in0=So(0), in1=Tv, op=ALU.add), reads=RK, writes=[sk])
            P.op("dve", lambda e: e.tensor_tensor(out=So(1), in0=pv(pi_), in1=cb_, op=ALU.mult), reads=RK, writes=[sk])
            P.op("dve", lambda e: e.tensor_tensor(out=Tv, in0=pv(pr), in1=sb_, op=ALU.mult), reads=RK, writes=[tk])
            P.op("dve", lambda e: e.tensor_tensor(out=So(1), in0=So(1), in1=Tv, op=ALU.subtract), reads=RK, writes=[sk])

        def s5_scan_chunk(G, c):
            gp = G % 2
            S = s5S[gp]
            sk = "s5S%d" % gp
            hprev = hch[gp]
            hpk = "hch%d" % gp
            rz, t8 = rzs[gp], t8s[gp]
            rzk, t8k = "rz%d" % gp, "t8%d" % gp
            rho4 = s5t[:, RHO, 4 * G:4 * G + 4]
            if c == 0:
                P.op("pool", lambda e: e.tensor_copy(out=rz[:], in_=rho4.unsqueeze(1).unsqueeze(3).to_broadcast([128, 2, 4, 64])),
                     reads=["s5t"], writes=[rzk])
                P.op("pool", lambda e: e.memset(rz[:, :, :, 0:1], 0.0), reads=[rzk], writes=[rzk])
                carry = s5c[:, 4 * G:4 * G + 4, :].rearrange("p r i -> p i r")
            else:
                carry = hprev[:, :, :, 63]
            P.op("dve", lambda e: e.tensor_tensor(out=t8[:], in0=carry, in1=rho4.unsqueeze(1).to_broadcast([128, 2, 4]), op=ALU.mult),
                 reads=["s5c", "s5t", hpk], writes=[t8k])
            P.op("dve", lambda e: e.tensor_tensor(out=S[:, c, :, :, 0], in0=S[:, c, :, :, 0], in1=t8[:], op=ALU.add), reads=[t8k, sk], writes=[sk])
            flat = S[:, c].rearrange("p i r t -> p (i r t)")
            P.op("dve", lambda e: e.tensor_tensor_scan(out=flat, data0=rz[:].rearrange("p i r t -> p (i r t)"), data1=flat,
                                                       initial=0.0, op0=ALU.mult, op1=ALU.add), reads=[sk, rzk], writes=[sk])

        def s5_rot_out(G, c, tt, last):
            gp = G % 2
            S, Tm = s5S[gp], s5t2[gp]
            sk, tk = "s5S%d" % gp, "s5t2%d" % gp
            hc = hch[gp]
            hk = "hch%d" % gp
            sl = slice(c * 64, (c + 1) * 64)
            co, si = cosT[:, 4 * G:4 * G + 4, :], sinT[:, 4 * G:4 * G + 4, :]
            RK = [sk, tk, hk, "cosT", "sinT"]
            E = "pool"
            Tc = Tm[:, :, sl]
            P.op(E, lambda e: e.tensor_tensor(out=hc[:, 0], in0=S[:, c, 0], in1=co, op=ALU.mult), reads=RK, writes=[hk])
            P.op(E, lambda e: e.tensor_tensor(out=Tc, in0=S[:, c, 1], in1=si, op=ALU.mult), reads=RK, writes=[tk])
            P.op(E, lambda e: e.tensor_tensor(out=hc[:, 0], in0=hc[:, 0], in1=Tc, op=ALU.subtract), reads=RK, writes=[hk])
            P.op(E, lambda e: e.tensor_tensor(out=hc[:, 1], in0=S[:, c, 1], in1=co, op=ALU.mult), reads=RK, writes=[hk])
            P.op(E, lambda e: e.tensor_tensor(out=Tc, in0=S[:, c, 0], in1=si, op=ALU.mult), reads=RK, writes=[tk])
            P.op(E, lambda e: e.tensor_tensor(out=hc[:, 1], in0=hc[:, 1], in1=Tc, op=ALU.add), reads=RK, writes=[hk])
            hb = hbf[gp]
            hbk = "hbf%d" % gp
            for ri in range(2):
                P.op("act", lambda e, ri=ri: e.copy(out=hb[:, :, ri, sl], in_=hc[:, ri]), reads=[hk], writes=[hbk])
            if last:
                for ri in range(2):
                    P.op(E, lambda e, ri=ri: e.tensor_copy(out=s5c[:, 4 * G:4 * G + 4, ri], in_=hc[:, ri, :, 63]), reads=[hk], writes=["s5c"])

        def s5_half_g(tt, par, gp):
            nch = tt // 64
            for G in range(gp, 8, 2):
                s5_rot_in(G, tt, par)
                yield
                for c in range(nch):
                    s5_scan_chunk(G, c)
                    yield
                    s5_rot_out(G, c, tt, c == nch - 1)
                    yield
                s5_readout(G, tt, hbf[gp], "hbf%d" % gp, None, par)
                yield

        def s5_prompt_g(tt, par=0):
            live = [s5_half_g(tt, par, 0), s5_half_g(tt, par, 1)]
            while live:
                for g in list(live):
                    try:
                        next(g)
                        yield
                    except StopIteration:
                        live.remove(g)
            yield from s5_glu_g(tt, par)

        def s5_readout(G, tt, hb, hbk, hsel=None, par=0):
            (uT,) = [PB[par][n_] for n_ in ("uT",)]
            kk = lambda n_: n_ if par == 0 else n_ + "1"
            py, pky = next_ps()

            def mm(e):
                last = None
                for r in range(4):
                    for ri in range(2):
                        rhs = hb[:, r, ri, 0:tt] if hsel is None else hsel(r, ri)
                        last = e.matmul(py[32 * r:32 * r + 32, 0:tt], lhsT=wd_[:, 4 * G + r, ri, :], rhs=rhs,
                                        start=(ri == 0), stop=(ri == 1), tile_position=(0, 32 * r))
                return last

            P.op("pe", mm, reads=["wd_", hbk], writes=[pky])
            P.op("dve", lambda e: e.scalar_tensor_tensor(out=ys5[:, 0:tt], in0=uT[:, G, 0:tt], scalar=s5d[:, G:G + 1], op0=ALU.mult,
                                                         in1=py[:, 0:tt], op1=ALU.add), reads=[pky, kk("uT"), "s5d"], writes=["ys5"])
            P.op("act", lambda e: e.activation(out=yaT[:, G, 0:tt], in_=ys5[:, 0:tt], func=AF.Gelu_apprx_tanh), reads=["ys5"], writes=["yaT"])

        def s5_glu(tt):
            drain(s5_glu_g(tt))

        def s5_glu_g(tt, par=0):
            (yaoT,) = [PB[par][n_] for n_ in ("yaoT",)]
            kk = lambda n_: n_ if par == 0 else n_ + "1"
            def ev(pt, pk, ct):
                P.op("act", lambda e: e.activation(out=ys5[:, 0:tt], in_=pt[:, 0:tt], func=AF.Sigmoid, bias=bglu[:, ct:ct + 1], scale=1.0),
                     reads=[pk, "bglu"], writes=["ys5"])
                P.op("dve", lambda e: e.tensor_tensor(out=yaoT[:, ct, 0:tt], in0=yaT[:, ct, 0:tt], in1=ys5[:, 0:tt], op=ALU.mult),
                     reads=["ys5", "yaT"], writes=[kk("yaoT")])

            yield from linear_fm_g(yaT, "yaT", D, dr["s5_w_glu"], 0, D, tt, ev)


        def ssd_dt(rows, ti, par=0):
            (dtr,) = [PB[par][n_] for n_ in ("dtr",)]
            kk = lambda n_: n_ if par == 0 else n_ + "1"
            P.op("dve", lambda e: e.tensor_tensor(out=sm[0:rows, 7, :], in0=dtr[0:rows, ti, :], in1=dtb[0:rows, :], op=ALU.add),
                 reads=[kk("dtr"), "dtb"], writes=["sm"])
            P.op("act", lambda e: e.activation(out=sm[0:rows, 7, :], in_=sm[0:rows, 7, :], func=AF.Exp), reads=["sm"], writes=["sm"])
            P.op("act", lambda e: e.activation(out=sm[0:rows, 0, :], in_=sm[0:rows, 7, :], func=AF.Ln, bias=1.0, scale=1.0),
                 reads=["sm"], writes=["sm"])
            P.op("dve", lambda e: e.tensor_tensor(out=sm[0:rows, 1, :], in0=sm[0:rows, 0, :], in1=arow[0:rows, :], op=ALU.mult),
                 reads=["sm", "arow"], writes=["sm"])

        def conv_prompt_g(tt, par=0):
            (xbcT,) = [PB[par][n_] for n_ in ("xbcT",)]
            kk = lambda n_: n_ if par == 0 else n_ + "1"
            for ct in range(24):
                t_ = cv[ct % 2]
                tk = "cv%d" % (ct % 2)
                P.op("dve", lambda e, ct=ct, t_=t_: e.tensor_scalar(out=t_[:, 0:tt], in0=xbcT[:, ct, 0:tt], scalar1=cw[:, 0, ct:ct + 1],
                                                                    scalar2=None, op0=ALU.mult), reads=[kk("xbcT"), "cw"], writes=[tk])
                for k in range(1, 4):
                    P.op("dve", lambda e, ct=ct, t_=t_, k=k: e.scalar_tensor_tensor(out=t_[:, 0:tt], in0=xbcT[:, ct, k:k + tt],
                                                                                    scalar=cw[:, k, ct:ct + 1], op0=ALU.mult,
                                                                                    in1=t_[:, 0:tt], op1=ALU.add),
                         reads=[kk("xbcT"), "cw", tk], writes=[tk])
                P.op("act", lambda e, ct=ct, t_=t_: e.activation(out=actT[:, ct, 0:tt], in_=t_[:, 0:tt], func=AF.Silu, bias=cb[:, ct:ct + 1], scale=1.0),
                     reads=[tk, "cb"], writes=["actT"])
                yield

        def gate_norm_out(rows, ti, tok0):
            drain(gate_norm_out_g(rows, ti, tok0))

        def gate_norm_out_g(rows, ti, tok0, par=0):
            zs, yBT = [PB[par][n_] for n_ in ("zs", "yBT",)]
            kk = lambda n_: n_ if par == 0 else n_ + "1"
            P.op("dve", lambda e: e.tensor_tensor(out=ytm[0:rows, :], in0=ytm[0:rows, :], in1=zs[0:rows, ti, :], op=ALU.mult),
                 reads=["ytm", kk("zs")], writes=["ytm"])
            P.op("act", lambda e: e.activation(out=yBtm[0:rows, :], in_=ytm[0:rows, :], func=AF.Square, accum_out=st1[0:rows, 3:4]),
                 reads=["ytm"], writes=["yBtm", "st1b"])
            P.op("act", lambda e: e.activation(out=st1[0:rows, 7:8], in_=st1[0:rows, 3:4], func=AF.Sqrt, scale=1.0 / 2048, bias=EPS),
                 reads=["st1b"], writes=["st1b"])
            P.op("dve", lambda e: e.reciprocal(out=st1[0:rows, 3:4], in_=st1[0:rows, 7:8]), reads=["st1b"], writes=["st1b"])
            P.op("dve", lambda e: e.scalar_tensor_tensor(out=yBtm[0:rows, :], in0=ytm[0:rows, :], scalar=st1[0:rows, 3:4], op0=ALU.mult,
                                                         in1=mnorm[0:rows, :], op1=ALU.mult), reads=["ytm", "st1b", "mnorm"], writes=["yBtm"])
            yield
            yield from transpose_to_g(lambda i0, cnt: yBT[:, i0:i0 + cnt, tok0:tok0 + rows], kk("yBT"),
                                      lambda i: yBtm[0:rows, i * 128:(i + 1) * 128], "yBtm", 16, rows, 128, BF16)

        def ssd_prompt_g(tt, par=0):
            kk = lambda n_: n_ if par == 0 else n_ + "1"
            yield from conv_prompt_g(tt, par)
            for c in range(tt // 128):
                cs_ = slice(c * 128, (c + 1) * 128)
                yield from transpose_to_g(lambda i0, cnt: xtm[:, i0 * 128:(i0 + cnt) * 128].rearrange("p (a r) -> p a r", r=128), "xtm",
                                          lambda i: actT[:, i, cs_], "actT", 16, 128, 128, BF16)
                yield from transpose_to_g(lambda i0, cnt: btm[:, i0:i0 + cnt, :], "btm", lambda i: actT[:, 16 + i, cs_], "actT", 4, 128, 128, BF16)
                ssd_dt(128, c, par)
                yield
                pt, pk = next_ps()
                P.op("pe", lambda e, pt=pt: e.matmul(pt[:, 0:32], lhsT=tri[:], rhs=sm[:, 1, :], start=True, stop=True),
                     reads=["tri", "sm"], writes=[pk])
                P.op("pe", lambda e, pt=pt: e.matmul(pt[:, 32:64], lhsT=ones[:], rhs=sm[:, 1, :], start=True, stop=True),
                     reads=["ones", "sm", pk], writes=[pk])
                P.op("act", lambda e, pt=pt: e.copy(out=sm[:, 2:4, :], in_=pt[:, 0:64].rearrange("p (a b) -> p a b", b=32)),
                     reads=[pk], writes=["sm"])
                P.op("act", lambda e: e.activation(out=sm[:, 4, :], in_=sm[:, 2, :], func=AF.Exp), reads=["sm"], writes=["sm"])
                P.op("act", lambda e: e.activation(out=sm[:, 6, :], in_=sm[:, 3, :], func=AF.Exp), reads=["sm"], writes=["sm"])
                P.op("dve", lambda e: e.tensor_tensor(out=sm[:, 7, :], in0=sm[:, 3, :], in1=sm[:, 2, :], op=ALU.subtract), reads=["sm"], writes=["sm"])
                P.op("act", lambda e: e.activation(out=sm[:, 7, :], in_=sm[:, 7, :], func=AF.Exp), reads=["sm"], writes=["sm"])
                P.op("dve", lambda e: e.tensor_tensor(out=sm[:, 5, :], in0=sm[:, 7, :], in1=sm[:, 0, :], op=ALU.mult), reads=["sm"], writes=["sm"])
                yield
                for g in range(4):
                    hs = slice(8 * g, 8 * g + 8)
                    P.op("dve", lambda e, hs=hs: e.tensor_tensor(out=Rb[:], in0=tri[:].unsqueeze(1).to_broadcast([128, 8, 128]),
                                                                 in1=sm[:, 1, hs].unsqueeze(2).to_broadcast([128, 8, 128]), op=ALU.mult),
                         reads=["tri", "sm"], writes=["Rb"])
                    for hh in range(2):
                        pa, pka = next_ps()
                        P.op("pe", lambda e, pa=pa, hh=hh: e.matmul(pa[:, :], lhsT=su[:], rhs=Rb[:, 4 * hh:4 * hh + 4, :].rearrange("p a b -> p (a b)"),
                                                                    start=True, stop=True), reads=["su", "Rb"], writes=[pka])
                        P.op("act", lambda e, pa=pa, hh=hh: e.activation(out=LT[:, 4 * hh:4 * hh + 4, :].rearrange("p a b -> p (a b)"),
                                                                         in_=pa[:, :], func=AF.Exp), reads=[pka], writes=["LT"])
                    pc, pkc = next_ps()
                    P.op("pe", lambda e, pc=pc, g=g: e.matmul(pc[:, 0:128], lhsT=actT[:, 16 + g, cs_], rhs=actT[:, 20 + g, cs_], start=True, stop=True),
                         reads=["actT"], writes=[pkc])
                    P.op("dve", lambda e, pc=pc: e.tensor_tensor(out=CBm[:], in0=pc[:, 0:128], in1=tri[:], op=ALU.mult), reads=[pkc, "tri"], writes=["CBm"])
                    P.op("dve", lambda e: e.tensor_tensor(out=MT[:], in0=LT[:], in1=CBm[:].unsqueeze(1).to_broadcast([128, 8, 128]), op=ALU.mult),
                         reads=["LT", "CBm"], writes=["MT"])
                    xg = xtm[:, 512 * g:512 * g + 512].rearrange("p (j d) -> p j d", d=64)
                    P.op("dve", lambda e, hs=hs, xg=xg: e.tensor_tensor(out=xdt[:].rearrange("p (j d) -> p j d", d=64), in0=xg,
                                                                        in1=sm[:, 0, hs].unsqueeze(2).to_broadcast([128, 8, 64]), op=ALU.mult),
                         reads=["xtm", "sm"], writes=["xdt"])
                    P.op("dve", lambda e, hs=hs, xg=xg: e.tensor_tensor(out=X2[:].rearrange("p (j d) -> p j d", d=64), in0=xg,
                                                                        in1=sm[:, 5, hs].unsqueeze(2).to_broadcast([128, 8, 64]), op=ALU.mult),
                         reads=["xtm", "sm"], writes=["X2"])
                    pyd, pkyd = next_ps()

                    def ydm(e, pyd=pyd, g=g):
                        last = None
                        for j in range(8):
                            last = e.matmul(pyd[:, 64 * j:64 * j + 64], lhsT=MT[:, j, :], rhs=xdt[:, 64 * j:64 * j + 64], start=True, stop=True)
                        return last

                    P.op("pe", ydm, reads=["MT", "xdt"], writes=[pkyd])
                    pyo, pkyo = next_ps()
                    P.op("pe", lambda e, pyo=pyo, g=g: e.matmul(pyo[:, :], lhsT=actT[:, 20 + g, cs_], rhs=hTb[:, g, :], start=True, stop=True),
                         reads=["actT", "hTb"], writes=[pkyo])
                    P.op("dve", lambda e, pyo=pyo, hs=hs: e.tensor_tensor(out=yt1[:].rearrange("p (j d) -> p j d", d=64),
                                                                          in0=pyo[:, :].rearrange("p (j d) -> p j d", d=64),
                                                                          in1=sm[:, 4, hs].unsqueeze(2).to_broadcast([128, 8, 64]), op=ALU.mult),
                         reads=[pkyo, "sm"], writes=["yt1"])
                    P.op("dve", lambda e, pyd=pyd, g=g: e.tensor_tensor(out=ytm[:, 512 * g:512 * g + 512], in0=yt1[:], in1=pyd[:, :], op=ALU.add),
                         reads=[pkyd, "yt1"], writes=["ytm"])
                    P.op("dve", lambda e, hs=hs, xg=xg: e.tensor_tensor(out=yt1[:].rearrange("p (j d) -> p j d", d=64), in0=xg,
                                                                        in1=mdr[:, hs].unsqueeze(2).to_broadcast([128, 8, 64]), op=ALU.mult),
                         reads=["xtm", "mdr", "ytm"], writes=["yt1"])
                    P.op("dve", lambda e, g=g: e.tensor_tensor(out=ytm[:, 512 * g:512 * g + 512], in0=ytm[:, 512 * g:512 * g + 512], in1=yt1[:], op=ALU.add),
                         reads=["yt1", "ytm"], writes=["ytm"])
                    pst, pkst = next_ps()
                    P.op("pe", lambda e, pst=pst, g=g: e.matmul(pst[:, :], lhsT=btm[:, g, :], rhs=X2[:], start=True, stop=True),
                         reads=["btm", "X2"], writes=[pkst])
                    hv = hT[:, g, :].rearrange("p (j d) -> p j d", d=64)
                    P.op("dve", lambda e, hv=hv, hs=hs: e.tensor_tensor(out=hv, in0=hv, in1=sm[:, 6, hs].unsqueeze(2).to_broadcast([128, 8, 64]), op=ALU.mult),
                         reads=["hT", "sm"], writes=["hT"])
                    P.op("dve", lambda e, pst=pst, g=g: e.tensor_tensor(out=hT[:, g, :], in0=hT[:, g, :], in1=pst[:, :], op=ALU.add),
                         reads=["hT", pkst], writes=["hT"])
                    P.op("act", lambda e, g=g: e.copy(out=hTb[:, g, :], in_=hT[:, g, :]), reads=["hT"], writes=["hTb"])
                    yield
                yield from gate_norm_out_g(128, c, c * 128, par)

        def in_proj(*a, **k):
            drain(in_proj_g(*a, **k))

        def in_proj_g(tt, tiles, want_xbc_tm, par=0, prompt=False):
            uT, zs, xbcT, dtr, gaT, gbT = [PB[par][n_] for n_ in ("uT", "zs", "xbcT", "dtr", "gaT", "gbT")]
            kk = lambda n_: n_ if par == 0 else n_ + "1"
            def ev_u(pt, pk, ct):
                copy_evac(uT[:, ct, 0:tt], kk("uT"), pt[:, 0:tt], pk)

            yield from linear_fm_g(hT_, "hT_", D, dr["w_in"], 0, 1024, tt, ev_u)

            def ev_z(pt, pk, ti, cc, cwid):
                rows = tiles[ti][1]
                P.op("act", lambda e: e.activation(out=zs[0:rows, ti, cc:cc + cwid], in_=pt[0:rows, 0:cwid], func=AF.Silu), reads=[pk], writes=[kk("zs")])

            yield from linear_tm_g(hT_, "hT_", D, dr["w_in"], OFF_Z, 2048, tiles, ev_z)

            def ev_x(pt, pk, ct):
                copy_evac(xbcT[:, ct, 3:3 + tt], kk("xbcT"), pt[:, 0:tt], pk)
                if want_xbc_tm:
                    P.op("dve", lambda e: e.tensor_copy(out=xtail[:, ct, 0:3], in_=pt[:, tt - 3:tt]), reads=[pk, kk("xbcT")], writes=["xtail"])
                if "xs32" in SAMPLE:
                    P.op("dve", lambda e: e.tensor_copy(out=SAMPLE["xs32"][:, ct, :], in_=pt[:, 0:tt]), reads=[pk, kk("xbcT")], writes=["xs32"])

            yield from linear_fm_g(hT_, "hT_", D, dr["w_in"], OFF_XBC, 3072, tt, ev_x)
            if prompt:
                ox = PB[1 - par]["xbcT"]
                okey = "xbcT" if par == 1 else "xbcT1"
                P.op("act", lambda e: e.copy(out=xbcT[:, :, 0:3], in_=ox[:, :, tt:tt + 3]), reads=[okey], writes=[kk("xbcT")])
            def ev_dt(pt, pk, ti, cc, cwid):
                rows = tiles[ti][1]
                copy_evac(dtr[0:rows, ti, :], kk("dtr"), pt[0:rows, 0:32], pk)

            yield from linear_tm_g(hT_, "hT_", D, dr["w_in"], OFF_DT, 32, tiles, ev_dt)

            def ev_g(dst, dk):
                flat = dst[:].rearrange("p a b -> p (a b)")

                def f(pt, pk, ti, cc, cwid):
                    rows = tiles[ti][1]
                    P.op("act", lambda e: e.activation(out=flat[0:rows, cc:cc + cwid], in_=pt[0:rows, 0:cwid], func=AF.Sigmoid), reads=[pk], writes=[dk])
                return f

            yield from linear_tm_g(hT_, "hT_", D, dr["w_in"], OFF_GA, 1024, tiles, ev_g(gaT, kk("gaT")))
            yield from linear_tm_g(hT_, "hT_", D, dr["w_in"], OFF_GB, 1024, tiles, ev_g(gbT, kk("gbT")))

        def merge_ffn(*a, **k):
            drain(merge_ffn_g(*a, **k))

        def merge_ffn_g(tt, tiles, mt, mkey, y_dram, par=0, x_src=None, prompt=False):
            jG1, jG2 = (0, 1) if prompt else (2, 5)
            yaoT, yBT, gaT, gbT = [PB[par][n_] for n_ in ("yaoT", "yBT", "gaT", "gbT")]
            kk = lambda n_: n_ if par == 0 else n_ + "1"
            if x_src is not None:
                for ti, (t0, rows) in enumerate(tiles):
                    P.dma("pool", x_tm[0:rows, ti, :], x_src[t0:t0 + rows, :], "xin", reads=["yout"], writes=["x_tm"])
            ga_f = gaT[:].rearrange("p a b -> p (a b)")
            gb_f = gbT[:].rearrange("p a b -> p (a b)")
            mrg_f = mrg[:].rearrange("p a b -> p (a b)")
            mrgT_f = mrgT[:].rearrange("p a b -> p (a b)")

            def ev_a(pt, pk, ti, cc, cwid):
                rows = tiles[ti][1]
                P.op("dve", lambda e: e.tensor_tensor(out=mrg_f[0:rows, cc:cc + cwid], in0=pt[0:rows, 0:cwid], in1=ga_f[0:rows, cc:cc + cwid], op=ALU.mult),
                     reads=[pk, kk("gaT")], writes=["mrg"])

            yield from linear_tm_g(yaoT, kk("yaoT"), D, dr["w_branch_s5"], 0, D, tiles, ev_a)

            def ev_b(pt, pk, ti, cc, cwid):
                rows = tiles[ti][1]
                P.op("dve", lambda e: e.tensor_tensor(out=wk2[0:rows, 0:cwid], in0=pt[0:rows, 0:cwid], in1=gb_f[0:rows, cc:cc + cwid], op=ALU.mult),
                     reads=[pk, kk("gbT")], writes=["wk2"])
                P.op("dve", lambda e: e.tensor_tensor(out=mrgT_f[0:rows, cc:cc + cwid], in0=wk2[0:rows, 0:cwid], in1=mrg_f[0:rows, cc:cc + cwid], op=ALU.add),
                     reads=["wk2", "mrg"], writes=["mrgT"])

            yield from linear_tm_g(yBT, kk("yBT"), 2048, dr["w_branch_ssd"], 0, D, tiles, ev_b)
            for ti, (t0, rows) in enumerate(tiles):
                yield from transpose_to_g(lambda i0, cnt, t0=t0, rows=rows: mrg[:, i0:i0 + cnt, t0:t0 + rows], "mrg",
                                          lambda i, rows=rows: mrgT_f[0:rows, i * 128:(i + 1) * 128], "mrgT", 8, rows, 128, BF16)
            yield from linear_tm_seq_g(mrg, "mrg", D, dr["w_out"], D, tiles, jG1, mt, mkey)
            for ti, (t0, rows) in enumerate(tiles):
                if prompt:
                    rms_mod_p(4, 3)
                else:
                    rms_mod(ti, rows, mt, 4, 3, mkey, t0)

            def ev_gate(pt, pk, ct):
                if ct < 22:
                    P.op("act", lambda e: e.activation(out=factT[:, ct, 0:tt], in_=pt[:, 0:tt], func=AF.Silu), reads=[pk], writes=["factT"])
                else:
                    P.op("dve", lambda e: e.tensor_tensor(out=factT[:, ct - 22, 0:tt], in0=pt[:, 0:tt], in1=factT[:, ct - 22, 0:tt], op=ALU.mult),
                         reads=[pk, "factT"], writes=["factT"])

            yield from linear_fm_g(hT_, "hT_", D, dr["w_ffn_in"], 0, 2 * D_FF, tt, ev_gate)
            yield from linear_tm_seq_g(factT, "factT", D_FF, dr["w_ffn_out"], D, tiles, jG2, mt, mkey)
            for ti, (t0, rows) in enumerate(tiles):
                P.dma("pool", y_dram[t0:t0 + rows, :], x_tm[0:rows, ti, :], "yout", reads=["x_tm"], writes=["yout"])

        wk2b = [wk2]

        def linear_tm_seq_g(inT, in_key, K, wd, ncols, tiles, jG, mt, mkey):
            def ev(pt, pk, ti, cc, cwid):
                rows = tiles[ti][1]
                copy_evac(wk2b[ti][0:rows, cc:cc + cwid], "wk2", pt[0:rows, 0:cwid], pk)

            yield from linear_tm_g(inT, in_key, K, wd, 0, ncols, tiles, ev)
            for ti, (t0, rows) in enumerate(tiles):
                resid_gate(wk2b[ti][0:rows, :], "wk2", ti, rows, mt, jG, mkey)

        tiles_p = [(0, 128)]

        def dense_in_g(si):
            P.dma("pool", x_tm[:, 0, :], dr["xp"][si * T:(si + 1) * T, :], "xin", reads=["yout"], writes=["x_tm"])
            rms_mod_p(1, 0)
            yield
            yield from in_proj_g(T, tiles_p, want_xbc_tm=(si == NST_RUN - 1), par=si % 2, prompt=True)

        def dense_out_g(si):
            yield from merge_ffn_g(T, tiles_p, modp, "modp", dr["y_p"][si * T:(si + 1) * T, :], par=si % 2,
                                   x_src=dr["xp"][si * T:(si + 1) * T, :], prompt=True)

        def chain(*gs):
            for g in gs:
                yield from g

        drain(dense_in_g(0))
        for si in range(NST_RUN):
            dg_ = []
            if si >= 1:
                dg_.append(dense_out_g(si - 1))
            if si + 1 < NST_RUN:
                dg_.append(dense_in_g(si + 1))
            interleave([chain(*dg_), s5_prompt_g(T, si % 2), ssd_prompt_g(T, si % 2)], ILW)
        drain(dense_out_g(NST_RUN - 1))
        if STOP == "p_1":
            return finish()
        if STOP == "pb1":
            for n_ in ("yaoT", "gaT", "gbT", "uT"):
                dump(n_, PB[1][n_][:].rearrange("p a b -> p (a b)"), n_ + "1", [128, 8 * T])
            dump("yBT", PB[1]["yBT"][:].rearrange("p a b -> p (a b)"), "yBT1", [128, 16 * T])
            dump("zs", PB[1]["zs"][:, 0, :], "zs1", [128, 2048])
            return finish()

        for ri, nm in enumerate(("s5re_p", "s5im_p")):
            pt, pk = next_ps()
            P.op("pe", lambda e, pt=pt, ri=ri: e.transpose(pt[0:32, 0:128], s5c[:, :, ri], ident[:]), reads=["s5c", "ident"], writes=[pk])
            P.op("act", lambda e, pt=pt, ri=ri: e.copy(out=wk2[0:32, ri * 128:(ri + 1) * 128], in_=pt[0:32, 0:128]), reads=[pk], writes=["wk2"])
            P.dma("sp", dr[nm], wk2[0:32, ri * 128:(ri + 1) * 128], "o_" + nm, reads=["wk2"], writes=["o_" + nm])
        if STOP == "e1":
            return finish()
        hout = ytm[:, :].rearrange("p (a b) -> p a b", b=128)
        transpose_to(lambda i0, cnt: hout[:, i0:i0 + cnt, :], "ytm",
                     lambda i: hT[:, i // 4, (i % 4) * 128:(i % 4 + 1) * 128], "hT", 16, 128, 128, F32)
        P.dma("sp", dr["ssm_p"].rearrange("(a p) n -> p a n", p=128), hout, "o_ssm_p", reads=["ytm"], writes=["o_ssm_p"])
        if STOP == "e2":
            return finish()
        for i0 in range(0, 24, 4):
            pt, pk = next_ps()

            def trx(e, pt=pt, i0=i0):
                last = None
                for a in range(4):
                    last = e.transpose(pt[0:3, a * 128:(a + 1) * 128], xtail[:, i0 + a, 0:3], ident[:])
                return last

            P.op("pe", trx, reads=["xtail", "ident"], writes=[pk])
            P.op("act", lambda e: e.copy(out=cps[0:3, :], in_=pt[0:3, :]), reads=[pk], writes=["yt1"])
            P.dma("sp", dr["conv_p"][:, i0 * 128:(i0 + 4) * 128], cps[0:3, :], "o_conv_p", reads=["yt1"], writes=["o_conv_p"])
        if STOP == "prompt_only":
            return finish()
        P.emit_phase()
        tiles_s = [(0, NS)]
        bump["lo"] = M1
        bump["hi"] = ARW
        mods = ssb("mods", [NS, 6, D], BF16)
        xs32 = ssb("xs32", [128, 24, NS], F32)
        ada_compute(sb, False, mods)
        P.emit_phase()
        bump["lo"] = M1
        P.dma("pool", x_tm[0:NS, 0, :], dr["xs"], "xin", writes=["x_tm"])
        rms_mod(0, NS, mods, 1, 0, "mods", 0)
        SAMPLE["xs32"] = xs32
        in_proj(NS, tiles_s, want_xbc_tm=False)
        P.emit_phase()
        bump["lo"] = M1
        hst = cosT[:].rearrange("p a b -> p (a b)").rearrange("p (a b) -> p a b", b=128)
        tmp3 = sinT[:].rearrange("p a b -> p (a b)").rearrange("p (a b) -> p a b", b=128)
        stg = sb("stg", [48, 4096], F32)
        h0T = sb("h0T", [128, 2, 32, NS], F32)
        for ri, nm in enumerate(("s5re_in", "s5im_in")):
            P.dma("sp", stg[0:NS, :], dr[nm], "stg", writes=["stg"])
            transpose_to(lambda i0, cnt, ri=ri: h0T[:, ri, i0:i0 + cnt, :], "h0T",
                         lambda i: stg[0:NS, i * 128:(i + 1) * 128], "stg", 32, NS, 128, F32)
        pbr, pkbr = next_ps()
        pbi, pkbi = next_ps()

        for G in range(8):
            s5_load_u(G, NS)

            def bus(e, G=G):
                last = None
                for r in range(4):
                    q = 4 * G + r
                    e.matmul(pbr[:, q * NS:(q + 1) * NS], lhsT=wb[:, G, 0, :], rhs=uz[:, r, 0:NS], start=True, stop=True)
                    last = e.matmul(pbi[:, q * NS:(q + 1) * NS], lhsT=wb[:, G, 1, :], rhs=uz[:, r, 0:NS], start=True, stop=True)
                return last

            P.op("pe", bus, reads=["wb", "uz", pkbr, pkbi], writes=[pkbr, pkbi])
        hn5 = sb("hn5", [128, 2, 32, NS], F32)
        t5 = sb("t5", [128, 2, 32, NS], F32)
        arb = s5t[:, AR, :].unsqueeze(2).to_broadcast([128, 32, NS])
        aib = s5t[:, AI, :].unsqueeze(2).to_broadcast([128, 32, NS])
        K5 = ["h0T", "s5t", "t5", "hn5", pkbr, pkbi]
        pv5 = lambda p_: p_[:, 0:32 * NS].rearrange("p (q b) -> p q b", b=NS)
        P.op("dve", lambda e: e.tensor_tensor(out=t5[:, 0], in0=h0T[:, 0], in1=arb, op=ALU.mult), reads=K5, writes=["t5"])
        P.op("dve", lambda e: e.tensor_tensor(out=t5[:, 1], in0=h0T[:, 1], in1=aib, op=ALU.mult), reads=K5, writes=["t5"])
        P.op("dve", lambda e: e.tensor_tensor(out=t5[:, 0], in0=t5[:, 0], in1=t5[:, 1], op=ALU.subtract), reads=K5, writes=["t5"])
        P.op("dve", lambda e: e.tensor_tensor(out=hn5[:, 0], in0=t5[:, 0], in1=pv5(pbr), op=ALU.add), reads=K5, writes=["hn5"])
        P.op("dve", lambda e: e.tensor_tensor(out=t5[:, 0], in0=h0T[:, 1], in1=arb, op=ALU.mult), reads=K5, writes=["t5"])
        P.op("dve", lambda e: e.tensor_tensor(out=t5[:, 1], in0=h0T[:, 0], in1=aib, op=ALU.mult), reads=K5, writes=["t5"])
        P.op("dve", lambda e: e.tensor_tensor(out=t5[:, 0], in0=t5[:, 0], in1=t5[:, 1], op=ALU.add), reads=K5, writes=["t5"])
        P.op("dve", lambda e: e.tensor_tensor(out=hn5[:, 1], in0=t5[:, 0], in1=pv5(pbi), op=ALU.add), reads=K5, writes=["hn5"])
        hb5 = sb("hb5", [128, 2, 32, NS], BF16)
        P.op("act", lambda e: e.copy(out=hb5[:].rearrange("p a b c -> p (a b c)"), in_=hn5[:].rearrange("p a b c -> p (a b c)")),
             reads=["hn5"], writes=["hb5"])
        for G in range(8):
            s5_readout(G, NS, None, "hb5", hsel=lambda r, ri, G=G: hb5[:, ri, 4 * G + r, :])
        s5_glu(NS)
        for ri, nm in enumerate(("s5re_s", "s5im_s")):
            transpose_to(lambda i0, cnt: stg[0:NS, i0 * 128:(i0 + cnt) * 128].rearrange("p (a r) -> p a r", r=128), "stg",
                         lambda i, ri=ri: hn5[:, ri, i, :], "hn5", 32, 128, NS, F32)
            P.dma("sp", dr[nm], stg[0:NS, :], "o_" + nm, reads=["stg"], writes=["o_" + nm, "stg"])

        histT = sb("histT", [128, 24, NS * 3], F32)
        P.dma("sp", stg[0:NS * 3, 0:3072], dr["conv_in"], "stg", writes=["stg"])
        transpose_to(lambda i0, cnt: histT[:, i0:i0 + cnt, :], "histT", lambda i: stg[0:NS * 3, i * 128:(i + 1) * 128], "stg",
                     24, NS * 3, 128, F32)
        actS = sb("actS", [128, 24, NS], F32)
        for ct in range(24):
            t_ = cv[ct % 2]
            tk = "cv%d" % (ct % 2)
            hv_ = histT[:, ct, :].rearrange("p (b k) -> p b k", k=3)
            P.op("dve", lambda e, ct=ct, t_=t_: e.tensor_scalar(out=t_[:, 0:NS], in0=xs32[:, ct, :], scalar1=cw[:, 3, ct:ct + 1], scalar2=None,
                                                                op0=ALU.mult), reads=["xs32", "cw"], writes=[tk])
            for k in range(3):
                P.op("dve", lambda e, ct=ct, t_=t_, k=k, hv_=hv_: e.scalar_tensor_tensor(out=t_[:, 0:NS], in0=hv_[:, :, k], scalar=cw[:, k, ct:ct + 1],
                                                                                         op0=ALU.mult, in1=t_[:, 0:NS], op1=ALU.add),
                     reads=["histT", "cw", tk], writes=[tk])
            P.op("act", lambda e, ct=ct, t_=t_: e.activation(out=actS[:, ct, :], in_=t_[:, 0:NS], func=AF.Silu, bias=cb[:, ct:ct + 1], scale=1.0),
                 reads=[tk, "cb"], writes=["actS"])
        P.dma("sp", dr["conv_s"][:, 0:2, :], dr["conv_in"].rearrange("(b k) c -> b k c", k=3)[:, 1:3, :], "o_conv_s", writes=["o_conv_s"])
        transpose_to(lambda i0, cnt: stg[0:NS, i0 * 128:(i0 + cnt) * 128].rearrange("p (a r) -> p a r", r=128), "stg",
                     lambda i: xs32[:, i, :], "xs32", 24, 128, NS, F32)
        P.dma("sp", dr["conv_s"][:, 2, :], stg[0:NS, 0:3072], "o_conv_s2", reads=["stg"], writes=["o_conv_s2", "stg"])

        ssd_dt(NS, 0)
        P.op("act", lambda e: e.activation(out=sm[0:NS, 2, :], in_=sm[0:NS, 1, :], func=AF.Exp), reads=["sm"], writes=["sm"])
        dfm = sb("dfm", [32, 2, NS], F32)
        for k, col in enumerate((0, 2)):
            pt, pk = next_ps()
            P.op("pe", lambda e, pt=pt, col=col: e.transpose(pt[0:32, 0:NS], sm[0:NS, col, :], ident[0:NS, 0:NS]), reads=["sm", "ident"], writes=[pk])
            P.op("act", lambda e, pt=pt, k=k: e.copy(out=dfm[:, k, :], in_=pt[0:32, 0:NS]), reads=[pk], writes=["dfm"])
        esel = [sb("esel%d" % i, [32, 128], F32) for i in range(2)]
        dex = sb("dex", [128, 16, 2, NS], F32)
        dexp = sb("dexp", [128, 16], F32)
        for hl in range(2):
            P.dma("sp", dexp[64 * hl:64 * hl + 64, :], dr["m_d"][0, :].rearrange("(hp hl) -> hl hp", hl=2)[hl, :].partition_broadcast(64),
                  "dexp", writes=["dexp"])
        for hp in range(16):
            es = esel[hp % 2]
            ek = "esel%d" % (hp % 2)
            P.op("dve", lambda e, es=es, hp=hp: e.tensor_copy(out=es[:].rearrange("h (b c) -> h b c", c=64),
                                                              in_=ident[0:32, 2 * hp:2 * hp + 2].unsqueeze(2).to_broadcast([32, 2, 64])),
                 reads=["ident"], writes=[ek])
            pt, pk = next_ps()
            P.op("pe", lambda e, pt=pt, es=es: e.matmul(pt[:, 0:2 * NS], lhsT=es[:], rhs=dfm[:].rearrange("h a b -> h (a b)"), start=True, stop=True),
                 reads=[ek, "dfm"], writes=[pk])
            copy_evac(dex[:, hp, :, :], "dex", pt[:, 0:2 * NS].rearrange("p (a b) -> p a b", b=NS), pk)
        dtx = sb("dtx", [128, 16, NS], F32)
        P.op("dve", lambda e: e.tensor_tensor(out=dtx[:], in0=dex[:, :, 0, :], in1=actS[:, 0:16, :], op=ALU.mult), reads=["dex", "actS"], writes=["dtx"])
        ysT = sb("ysT", [128, 16, NS], F32)
        dg = sb("dg", [128, 8, 128], F32)
        bcb = sb("bcb", [128, 8, 128], F32)
        red = sb("red", [128, 16], F32)
        print("sample arena end=%d of %d" % (bump["lo"], ARW))
        for b in range(NS):
            hs_ = hst
            hk = "hst"
            P.dma("sp", hs_, dr["ssm_in"][b].rearrange("(hp hl) p n -> (hl p) hp n", hl=2), hk, reads=["o_hst"], writes=[hk])
            for k in range(8):
                P.op("dve", lambda e, k=k, b=b: e.tensor_scalar(out=dg[:, k, :], in0=ident[:], scalar1=actS[:, 16 + k, b:b + 1], scalar2=None, op0=ALU.mult),
                     reads=["ident", "actS"], writes=["dg"])
            for hh in range(2):
                pt, pk = next_ps()
                P.op("pe", lambda e, pt=pt, hh=hh: e.matmul(pt[:, :], lhsT=ones[:], rhs=dg[:, 4 * hh:4 * hh + 4, :].rearrange("p a b -> p (a b)"),
                                                            start=True, stop=True), reads=["ones", "dg"], writes=[pk])
                copy_evac(bcb[:, 4 * hh:4 * hh + 4, :].rearrange("p a b -> p (a b)"), "bcb", pt[:, :], pk)
            P.op("dve", lambda e, b=b: e.tensor_tensor(out=hs_, in0=hs_, in1=dex[:, :, 1, b:b + 1].to_broadcast([128, 16, 128]), op=ALU.mult),
                 reads=[hk, "dex"], writes=[hk])
            P.op("dve", lambda e, b=b: e.tensor_tensor(out=tmp3.rearrange("p (g r) n -> p g r n", r=4),
                                                       in0=bcb[:, 0:4, :].unsqueeze(2).to_broadcast([128, 4, 4, 128]),
                                                       in1=dtx[:, :, b:b + 1].rearrange("p (g r) o -> p g r o", r=4).to_broadcast([128, 4, 4, 128]), op=ALU.mult),
                 reads=["bcb", "dtx"], writes=["tmp3"])
            P.op("dve", lambda e: e.tensor_tensor(out=hs_, in0=hs_, in1=tmp3, op=ALU.add), reads=[hk, "tmp3"], writes=[hk])
            P.dma("sp", dr["ssm_s"][b].rearrange("(hp hl) p n -> (hl p) hp n", hl=2), hs_, "o_hst", reads=[hk], writes=["o_hst"])
            P.op("dve", lambda e: e.tensor_tensor(out=tmp3.rearrange("p (g r) n -> p g r n", r=4),
                                                  in0=hs_.rearrange("p (g r) n -> p g r n", r=4),
                                                  in1=bcb[:, 4:8, :].unsqueeze(2).to_broadcast([128, 4, 4, 128]), op=ALU.mult),
                 reads=[hk, "bcb"], writes=["tmp3"])
            P.op("dve", lambda e: e.tensor_reduce(out=red[:], in_=tmp3, axis=AX.X, op=ALU.add), reads=["tmp3"], writes=["red"])
            P.op("dve", lambda e, b=b: e.tensor_tensor(out=ysT[:, :, b], in0=dexp[:], in1=actS[:, 0:16, b], op=ALU.mult), reads=["dexp", "actS"], writes=["ysT"])
            P.op("dve", lambda e, b=b: e.tensor_tensor(out=ysT[:, :, b], in0=ysT[:, :, b], in1=red[:], op=ALU.add), reads=["red", "ysT"], writes=["ysT"])
        transpose_to(lambda i0, cnt: ytm[0:NS, i0 * 128:(i0 + cnt) * 128].rearrange("p (a r) -> p a r", r=128), "ytm",
                     lambda i: ysT[:, i, :], "ysT", 16, 128, NS, F32)
        gate_norm_out(NS, 0, 0)
        if STOP == "s_mix":
            dump("yaoT", yaoT[:].rearrange("p a b -> p (a b)"), "yaoT", [128, 8 * T])
            dump("yBT", yBT[:].rearrange("p a b -> p (a b)"), "yBT", [128, 16 * T])
            return finish()
        P.emit_phase()
        merge_ffn(NS, tiles_s, mods, "mods", dr["y_s"])

        P.final_wait("sp", [k for k in P.lastw if str(k).startswith(("o_", "yout", "dbg_"))])
        P.emit()
    return nc, dbg_out


_NC_CACHE = {}


def _prep_inputs(inputs):
    f = lambda a: np.ascontiguousarray(np.asarray(a, dtype=np.float32))
    w = {}
    for n, s in WEIGHT_SHAPES.items():
        w[n] = f(inputs[n]).reshape(s)
    maps = []
    for i in range(NCORES):
        m = dict(w)
        sl = slice(NS * i, NS * (i + 1))
        m["xp"] = f(inputs["x_prompt"][i])
        m["xs"] = f(inputs["x_sample"][sl, 0, :])
        m["c17"] = f(np.concatenate([np.asarray(inputs["c_sample"])[sl], np.asarray(inputs["c_prompt"])[i:i + 1]], axis=0))
        m["s5re_in"] = f(np.asarray(inputs["state_s5_re"])[0, sl].reshape(NS, 4096))
        m["s5im_in"] = f(np.asarray(inputs["state_s5_im"])[0, sl].reshape(NS, 4096))
        m["ssm_in"] = f(np.asarray(inputs["state_ssm"])[0, sl])
        m["conv_in"] = f(np.asarray(inputs["state_conv"])[0, sl].reshape(NS * 3, 3072))
        maps.append(m)
    return maps


def kernel(**inputs):
    if "nc" not in _NC_CACHE:
        _NC_CACHE["nc"] = build_nc(tuple(DEBUG.get("names", ())))
    nc, dbg = _NC_CACHE["nc"]
    maps = _prep_inputs(inputs)
    res = run_bass_kernel_spmd(nc, maps, core_ids=list(range(NCORES)))
    R = res.results
    if DEBUG.get("names"):
        DEBUG["out"] = [{k: r["dbg_" + k] for k in dbg} for r in R]
    cat = lambda n: np.stack([np.asarray(R[i][n]) for i in range(NCORES)], 0)
    y_p = cat("y_p").reshape(8, SEQ, D)
    y_s = cat("y_s").reshape(128, 1, D)
    s5re_p = cat("s5re_p").reshape(1, 8, 64, 64)
    s5im_p = cat("s5im_p").reshape(1, 8, 64, 64)
    ssm_p = cat("ssm_p").reshape(1, 8, 32, 64, 128)
    conv_p = cat("conv_p").reshape(1, 8, 3, 3072)
    s5re_s = cat("s5re_s").reshape(1, 128, 64, 64)
    s5im_s = cat("s5im_s").reshape(1, 128, 64, 64)
    ssm_s = cat("ssm_s").reshape(1, 128, 32, 64, 128)
    conv_s = cat("conv_s").reshape(1, 128, 3, 3072)
    return tuple(np.ascontiguousarray(a, dtype=np.float32) for a in
                 (y_p, y_s, s5re_p, s5im_p, ssm_p, conv_p, s5re_s, s5im_s, ssm_s, conv_s))
```
